# Optimizing a Trainium2 kernel written in Bass

```python
import jax, jax.numpy as jnp
from jax import lax
import numpy as np

D_MODEL = 1024
BATCH = 8
SEQ = 4096
DEPTH = 2

N_A = DEPTH // 2
N_B = DEPTH - N_A
N_META = 16
HEAD_DIM = 64
N_HEADS_A = D_MODEL // HEAD_DIM
LORA_DECAY = 64
LORA_AAA = 64
LORA_GATE = 128
GN_EPS = 64e-5
N_HEADS_Q = D_MODEL // HEAD_DIM
N_HEADS_KV = 4
GROUP = N_HEADS_Q // N_HEADS_KV
WINDOW = 128
BLOCK = 128
PAD_FRONT = BLOCK - N_META
ROPE_THETA = 10000.0
D_FF = 4 * D_MODEL
ALPHA = (2.0 * DEPTH) ** 0.25
BETA = (8.0 * DEPTH) ** -0.25
LN_EPS = 1e-5

kernel_name = "yoco_rwkv7_swa_sink_hybrid"


def layer_norm(x, g, b):
    xf = x.astype(jnp.float32)
    mu = jnp.mean(xf, axis=-1, keepdims=True)
    xc = xf - mu
    var = jnp.mean(xc * xc, axis=-1, keepdims=True)
    y = xc * lax.rsqrt(var + LN_EPS) * g.astype(jnp.float32) + b.astype(jnp.float32)
    return y.astype(x.dtype)


def rope_tables(length):
    inv_freq = 1.0 / (ROPE_THETA ** (jnp.arange(0, HEAD_DIM, 2, dtype=jnp.float32) / HEAD_DIM))
    ang = jnp.arange(length, dtype=jnp.float32)[:, None] * inv_freq[None, :]
    return jnp.cos(ang), jnp.sin(ang)


def apply_rope(t, cos, sin):
    tf = t.astype(jnp.float32)
    t1, t2 = jnp.split(tf, 2, axis=-1)
    c = cos[None, :, None, :]
    s = sin[None, :, None, :]
    out = jnp.concatenate([t1 * c - t2 * s, t2 * c + t1 * s], axis=-1)
    return out.astype(t.dtype)


def sq_relu_mlp(x, w_up, w_down):
    h = jax.nn.relu(x @ w_up)
    return (h * h) @ w_down


def rwkv7_time_mix(x, mu, w_r, w_k, w_v, w_o, w0, w1, w2, a0, a1, a2, g1, g2,
                   k_k, k_a, r_k, gn_w, gn_b):
    bsz, length, d = x.shape
    H, N = N_HEADS_A, HEAD_DIM
    x_prev = jnp.pad(x, ((0, 0), (1, 0), (0, 0)))[:, :-1]
    xx = x_prev - x
    xr = x + xx * mu[0]
    xw = x + xx * mu[1]
    xk = x + xx * mu[2]
    xv = x + xx * mu[3]
    xa = x + xx * mu[4]
    xg = x + xx * mu[5]
    r = xr @ w_r
    k = xk @ w_k
    v = xv @ w_v
    w = -jax.nn.softplus(-(w0 + jnp.tanh(xw @ w1) @ w2)) - 0.5
    a = jax.nn.sigmoid(a0 + (xa @ a1) @ a2)
    g = jax.nn.sigmoid(xg @ g1) @ g2
    f32 = jnp.float32
    kk = (k * k_k).reshape(bsz, length, H, N).astype(f32)
    kk = kk / jnp.maximum(jnp.linalg.norm(kk, axis=-1, keepdims=True), 1e-12)
    k = k * (1.0 + (a - 1.0) * k_a)
    rh = r.reshape(bsz, length, H, N).astype(f32)
    kh = k.reshape(bsz, length, H, N).astype(f32)
    vh = v.reshape(bsz, length, H, N).astype(f32)
    ah = a.reshape(bsz, length, H, N).astype(f32)
    decay = jnp.exp(-jnp.exp(w.reshape(bsz, length, H, N).astype(f32)))
    seq_first = lambda t: jnp.transpose(t, (1, 0, 2, 3))
    inputs = (seq_first(rh), seq_first(decay), seq_first(kh), seq_first(vh),
              seq_first(-kk), seq_first(kk * ah))

    def step(state, inp):
        r_t, w_t, k_t, v_t, a_t, b_t = inp
        sa = jnp.einsum('bhij,bhj->bhi', state, a_t)
        state = (state * w_t[:, :, None, :] + sa[..., None] * b_t[:, :, None, :]
                 + v_t[..., None] * k_t[:, :, None, :])
        y = jnp.einsum('bhij,bhj->bhi', state, r_t)
        return state, y

    state0 = jnp.zeros((bsz, H, N, N), f32)
    _, ys = lax.scan(step, state0, inputs)
    y = jnp.transpose(ys, (1, 0, 2, 3))
    ym = jnp.mean(y, axis=-1, keepdims=True)
    yc = y - ym
    yv = jnp.mean(yc * yc, axis=-1, keepdims=True)
    y = (yc * lax.rsqrt(yv + GN_EPS) * gn_w.reshape(H, N).astype(f32)
         + gn_b.reshape(H, N).astype(f32))
    bonus = jnp.sum(rh * kh * r_k.astype(f32), axis=-1, keepdims=True) * vh
    o = (y + bonus).reshape(bsz, length, d).astype(x.dtype) * g
    return o @ w_o


def to_blocks(t):
    pad = [(0, 0), (PAD_FRONT, 0)] + [(0, 0)] * (t.ndim - 2)
    t = jnp.pad(t, pad)
    return t.reshape(t.shape[0], -1, BLOCK, *t.shape[2:])


def with_prev_block(tb):
    pad = [(0, 0), (1, 0)] + [(0, 0)] * (tb.ndim - 2)
    prev = jnp.pad(tb, pad)[:, :-1]
    return jnp.concatenate([prev, tb], axis=2)


def band_mask(nb):
    qi = jnp.arange(BLOCK)[:, None]
    kj = jnp.arange(2 * BLOCK)[None, :]
    rel = BLOCK + qi - kj
    in_window = (rel >= 0) & (rel < WINDOW)
    key_pos = (jnp.arange(nb)[:, None] - 1) * BLOCK + jnp.arange(2 * BLOCK)[None, :]
    valid = key_pos >= PAD_FRONT
    return in_window[None] & valid[:, None, :]


def shared_kv(x, w_k, w_v, cos, sin):
    bsz, length, _ = x.shape
    k = (x @ w_k).reshape(bsz, length, N_HEADS_KV, HEAD_DIM)
    v = (x @ w_v).reshape(bsz, length, N_HEADS_KV, HEAD_DIM)
    k = apply_rope(k, cos, sin)
    return with_prev_block(to_blocks(k)), with_prev_block(to_blocks(v))


def swa_sink_attention(x, w_q, sinks, w_o, kb, vb, cos, sin, mask):
    bsz, length, d = x.shape
    q = (x @ w_q).reshape(bsz, length, N_HEADS_Q, HEAD_DIM)
    q = apply_rope(q, cos, sin)
    qb = to_blocks(q)
    nb = qb.shape[1]
    qb = qb.reshape(bsz, nb, BLOCK, N_HEADS_KV, GROUP, HEAD_DIM)
    f32 = jnp.float32
    s = jnp.einsum('bnqhgd,bnshd->bnhgqs', qb.astype(f32), kb.astype(f32)) * (HEAD_DIM ** -0.5)
    s = jnp.where(mask[None, :, None, None], s, -jnp.inf)
    sink = sinks.reshape(N_HEADS_KV, GROUP).astype(f32)[None, None, :, :, None]
    m = jnp.maximum(jnp.max(s, axis=-1), sink)
    p = jnp.exp(s - m[..., None])
    denom = jnp.sum(p, axis=-1) + jnp.exp(sink - m)
    p = p / denom[..., None]
    o = jnp.einsum('bnhgqs,bnshd->bnqhgd', p, vb.astype(f32))
    o = o.reshape(bsz, nb * BLOCK, d)[:, PAD_FRONT:].astype(x.dtype)
    return o @ w_o


def setup_inputs(seed: int = 0) -> dict:
    key = jax.random.key(seed)
    ks = iter(jax.random.split(key, 40))
    D = D_MODEL
    inv = D ** -0.5

    def nrm(shape, scale):
        return jax.random.normal(next(ks), shape, jnp.float32) * scale

    return {
        "x": nrm((BATCH, SEQ, D), 1.0),
        "meta_tokens": nrm((N_META, D), 1.0),
        "a_mu": jax.random.uniform(next(ks), (N_A, 6, D), jnp.float32),
        "a_w_r": nrm((N_A, D, D), inv),
        "a_w_k": nrm((N_A, D, D), inv),
        "a_w_v": nrm((N_A, D, D), inv * BETA),
        "a_w_o": nrm((N_A, D, D), inv * BETA),
        "a_w0": jax.random.uniform(next(ks), (N_A, D), jnp.float32, -6.0, 1.0),
        "a_w1": nrm((N_A, D, LORA_DECAY), inv),
        "a_w2": nrm((N_A, LORA_DECAY, D), 0.1 * LORA_DECAY ** -0.5),
        "a_a0": nrm((N_A, D), 0.5),
        "a_a1": nrm((N_A, D, LORA_AAA), inv),
        "a_a2": nrm((N_A, LORA_AAA, D), LORA_AAA ** -0.5),
        "a_g1": nrm((N_A, D, LORA_GATE), inv),
        "a_g2": nrm((N_A, LORA_GATE, D), LORA_GATE ** -0.5),
        "a_k_k": 0.85 + nrm((N_A, D), 0.05),
        "a_k_a": 1.0 + nrm((N_A, D), 0.05),
        "a_r_k": nrm((N_A, N_HEADS_A, HEAD_DIM), 0.1),
        "a_gn_w": 1.0 + nrm((N_A, D), 0.05),
        "a_gn_b": nrm((N_A, D), 0.02),
        "kv_w_k": nrm((D, N_HEADS_KV * HEAD_DIM), inv),
        "kv_w_v": nrm((D, N_HEADS_KV * HEAD_DIM), inv * BETA),
        "b_w_q": nrm((N_B, D, N_HEADS_Q * HEAD_DIM), inv),
        "b_sinks": nrm((N_B, N_HEADS_Q), 1.0),
        "b_w_o": nrm((N_B, N_HEADS_Q * HEAD_DIM, D), inv * BETA),
        "mlp_w_up": nrm((DEPTH, D, D_FF), inv),
        "mlp_w_down": nrm((DEPTH, D_FF, D), D_FF ** -0.5 * BETA),
        "ln_g": 1.0 + nrm((DEPTH, 2, D), 0.05),
        "ln_b": nrm((DEPTH, 2, D), 0.02),
    }


def reference(x, meta_tokens, a_mu, a_w_r, a_w_k, a_w_v, a_w_o, a_w0, a_w1, a_w2,
              a_a0, a_a1, a_a2, a_g1, a_g2, a_k_k, a_k_a, a_r_k, a_gn_w, a_gn_b,
              kv_w_k, kv_w_v, b_w_q, b_sinks, b_w_o, mlp_w_up, mlp_w_down, ln_g, ln_b):
    bsz = x.shape[0]
    meta = jnp.broadcast_to(meta_tokens[None].astype(x.dtype), (bsz, N_META, x.shape[2]))
    h = jnp.concatenate([meta, x], axis=1)
    length = h.shape[1]
    cos, sin = rope_tables(length)
    nb = (length + PAD_FRONT) // BLOCK
    mask = band_mask(nb)
    kb = vb = None
    for i in range(DEPTH):
        if i < N_A:
            j = i
            mix = rwkv7_time_mix(h, a_mu[j], a_w_r[j], a_w_k[j], a_w_v[j], a_w_o[j],
                                 a_w0[j], a_w1[j], a_w2[j], a_a0[j], a_a1[j], a_a2[j],
                                 a_g1[j], a_g2[j], a_k_k[j], a_k_a[j], a_r_k[j],
                                 a_gn_w[j], a_gn_b[j])
        else:
            if i == N_A:
                kb, vb = shared_kv(h, kv_w_k, kv_w_v, cos, sin)
            j = i - N_A
            mix = swa_sink_attention(h, b_w_q[j], b_sinks[j], b_w_o[j], kb, vb, cos, sin, mask)
        h = layer_norm(ALPHA * h + mix, ln_g[i, 0], ln_b[i, 0])
        h = layer_norm(ALPHA * h + sq_relu_mlp(h, mlp_w_up[i], mlp_w_down[i]), ln_g[i, 1], ln_b[i, 1])
    return h[:, N_META:]
```

```python
import contextlib
import numpy as np
import concourse.bass as bass
import concourse.mybir as mybir
from concourse.bass_utils import run_bass_kernel_spmd

F32 = mybir.dt.float32
BF16 = mybir.dt.bfloat16
ALU = mybir.AluOpType
AF = mybir.ActivationFunctionType
AX = mybir.AxisListType

D = 1024
DC = 8
SEQ = 4096
NMETA = 16
NBLK = 33
LP = NBLK * 128
DFF = 4096
ALPHA = (2.0 * 2) ** 0.25
LN_EPS = 1e-5
GN_EPS = 64e-5
C0 = float(np.exp(-0.5))
NDMA_SLOTS = 8


class Op:
    __slots__ = ("eng", "fn", "waits", "needs_inc", "idx", "semval", "dma", "selfwait")

    def __init__(self, eng, fn):
        self.eng = eng
        self.fn = fn
        self.waits = []
        self.needs_inc = False
        self.idx = 0
        self.semval = 0
        self.dma = None
        self.selfwait = None


class T:
    def __init__(self, ap, name=""):
        self.ap = ap
        self.name = name
        self.w = None
        self.r = {}

    def __getitem__(self, k):
        return V(self, self.ap[k])

    def v(self, ap):
        return V(self, ap)


class V:
    def __init__(self, t, ap):
        self.t = t
        self.ap = ap

    def __getitem__(self, k):
        return V(self.t, self.ap[k])


def _ts(vs):
    out = []
    for v in vs:
        if v is None:
            continue
        out.append(v.t if isinstance(v, V) else v)
    return out


class Prog:
    ENGS = ("pe", "act", "dve", "pool", "sp")

    def __init__(self, nc, stack):
        self.nc = nc
        self.stack = stack
        self.streams = {e: [] for e in self.ENGS}
        self.cnt = {}
        self.seen = {e: {} for e in self.ENGS}
        self.sems = {}
        for e in self.ENGS:
            self.sems[("eng", e)] = stack.enter_context(nc.semaphore("s_" + e))
            self.cnt[("eng", e)] = 0
        self.dma_n = {}
        self.dma_last = {}
        for q in ("sp", "pool", "act"):
            self.dma_n[q] = 0
            for s in range(NDMA_SLOTS):
                key = ("dma", q, s)
                self.sems[key] = stack.enter_context(nc.semaphore("d_%s%d" % (q, s)))
                self.dma_last[key] = None
        self.n_ops = 0
        self.all_ts = []
        self.semc = {e: 0 for e in self.ENGS}
        self.last_op = {e: None for e in self.ENGS}

    def _reg(self, t):
        self.all_ts.append(t)
        return t

    def sb(self, name, shape, dt, stack=None):
        st = stack if stack is not None else self.stack
        self.n_names = getattr(self, "n_names", 0) + 1
        name = "%s_%d" % (name, self.n_names)
        return self._reg(T(st.enter_context(self.nc.sbuf_tensor(name, list(shape), dt)), name))

    def ps(self, name, shape, dt):
        return self._reg(T(self.stack.enter_context(self.nc.psum_tensor(name, list(shape), dt)), name))

    def dram(self, name, shape, dt, kind="Internal"):
        return self._reg(T(self.nc.dram_tensor(name, list(shape), dt, kind=kind).ap(), name))

    def barrier(self):
        toks = []
        for e in ("pe", "act", "dve", "pool"):
            lo = self.last_op[e]
            if lo is not None:
                lo.needs_inc = True
                toks.append(lo)
        for key, last in self.dma_last.items():
            if last is not None:
                toks.append(last)
        for e in self.ENGS:
            o = Op(e, None)
            o.waits = [p for p in toks if not (p.dma is None and p.eng == e)]
            self.streams[e].append(o)
            for p in toks:
                key = p.dma if p.dma is not None else ("eng", p.eng)
                if self.seen[e].get(key, 0) < p.idx:
                    self.seen[e][key] = p.idx
        for t in self.all_ts:
            t.w = None
            t.r = {}

    def _collect(self, eng, op, reads, writes):
        deps = {}

        def add(key, idx, pop):
            if key not in deps or deps[key][0] < idx:
                deps[key] = (idx, pop)

        for t in reads:
            if t.w is not None:
                add(*t.w)
        for t in writes:
            if t.w is not None:
                add(*t.w)
            for key, (idx, pop) in t.r.items():
                add(key, idx, pop)
        for key, (idx, pop) in deps.items():
            if key == ("eng", "pe") and eng == "pe":
                continue
            if self.seen[eng].get(key, 0) >= idx:
                continue
            self.seen[eng][key] = idx
            pop.needs_inc = True
            op.waits.append(pop)

    def op(self, eng, fn, reads=(), writes=()):
        reads = _ts(reads)
        writes = _ts(writes)
        o = Op(eng, fn)
        self._collect(eng, o, reads, writes)
        key = ("eng", eng)
        self.cnt[key] += 1
        o.idx = self.cnt[key]
        self.streams[eng].append(o)
        self.last_op[eng] = o
        for t in reads:
            if key not in t.r or t.r[key][0] < o.idx:
                t.r[key] = (o.idx, o)
        for t in writes:
            t.w = (key, o.idx, o)
            t.r = {}
        self.n_ops += 1
        return o

    def dma(self, q, out, in_, **kw):
        reads = _ts([in_])
        writes = _ts([out])
        oap, iap = out.ap, in_.ap
        o = Op(q, lambda e: e.dma_start(out=oap, in_=iap, **kw))
        self._collect(q, o, reads, writes)
        n = self.dma_n[q]
        self.dma_n[q] += 1
        slot = n % NDMA_SLOTS
        rnd = n // NDMA_SLOTS
        key = ("dma", q, slot)
        o.dma = key
        o.idx = rnd + 1
        o.semval = 16 * (rnd + 1)
        o.needs_inc = True
        prev = self.dma_last[key]
        if prev is not None and self.seen[q].get(key, 0) < prev.idx:
            self.seen[q][key] = prev.idx
            o.waits.append(prev)
        self.dma_last[key] = o
        self.streams[q].append(o)
        for t in reads:
            t.r[key] = (o.idx, o)
        for t in writes:
            t.w = (key, o.idx, o)
            t.r = {}
        self.n_ops += 1
        return o

    def finish(self, final_ts):
        o = Op("sp", None)
        for key, last in self.dma_last.items():
            if last is not None:
                o.waits.append(last)
        self.streams["sp"].append(o)

    def emit(self):
        nc = self.nc
        for e in self.ENGS:
            c = self.semc[e]
            for o in self.streams[e]:
                if o.dma is None and o.fn is not None and o.needs_inc:
                    c += 1
                    o.semval = c
            self.semc[e] = c
        engmap = {"pe": "tensor", "act": "scalar", "dve": "vector", "pool": "gpsimd", "sp": "sync"}
        sems = self.sems

        def run_stream(ename, eng):
            waited = {}
            for o in self.streams[ename]:
                for p in o.waits:
                    key = p.dma if p.dma is not None else ("eng", p.eng)
                    val = p.semval
                    if waited.get(key, 0) >= val:
                        continue
                    waited[key] = val
                    eng.wait_ge(sems[key], val)
                if o.fn is None:
                    continue
                ins = o.fn(eng)
                if o.dma is not None:
                    ins.then_inc(sems[o.dma], 16)
                elif o.needs_inc:
                    ins.then_inc(sems[("eng", ename)], 1)

        with nc.Block() as block:
            @block.tensor
            def _(eng):
                run_stream("pe", eng)

            @block.scalar
            def _(eng):
                run_stream("act", eng)

            @block.vector
            def _(eng):
                run_stream("dve", eng)

            @block.gpsimd
            def _(eng):
                run_stream("pool", eng)

            @block.sync
            def _(eng):
                run_stream("sp", eng)
        for e in self.ENGS:
            self.streams[e] = []

    def mm(self, out, lhsT, rhs, start=True, stop=True):
        a, b, c = out.ap, lhsT.ap, rhs.ap
        return self.op("pe", lambda e: e.matmul(a, b, c, start=start, stop=stop),
                       reads=[lhsT, rhs], writes=[out])

    def tr(self, out, in_, ident):
        a, b, c = out.ap, in_.ap, ident.ap
        return self.op("pe", lambda e: e.transpose(a, b, c), reads=[in_, ident], writes=[out])

    def act(self, out, in_, func, bias=None, scale=None, accum_out=None, eng="act"):
        kw = {}
        rd = [in_]
        wr = [out]
        if bias is not None:
            if isinstance(bias, V):
                kw["bias"] = bias.ap
                rd.append(bias)
            else:
                kw["bias"] = bias
        if scale is not None:
            if isinstance(scale, V):
                kw["scale"] = scale.ap
                rd.append(scale)
            else:
                kw["scale"] = scale
        if accum_out is not None:
            kw["accum_out"] = accum_out.ap
            wr.append(accum_out)
        a, b = out.ap, in_.ap
        return self.op("act", lambda e: e.activation(a, b, func, **kw), reads=rd, writes=wr)

    def tt(self, eng, out, in0, in1, op):
        a, b, c = out.ap, in0.ap, in1.ap
        return self.op(eng, lambda e: e.tensor_tensor(a, b, c, op), reads=[in0, in1], writes=[out])

    def ts(self, eng, out, in0, s1, op0, s2=None, op1=None, accum_out=None):
        rd = [in0]
        wr = [out]
        a, b = out.ap, in0.ap
        if isinstance(s1, V):
            rd.append(s1)
            s1 = s1.ap
        if isinstance(s2, V):
            rd.append(s2)
            s2 = s2.ap
        kw = {}
        if op1 is not None:
            kw["op1"] = op1
        if accum_out is not None:
            kw["accum_out"] = accum_out.ap
            wr.append(accum_out)
        return self.op(eng, lambda e: e.tensor_scalar(a, b, s1, s2, op0, **kw), reads=rd, writes=wr)

    def stt(self, out, in0, scalar, in1, op0, op1, eng="dve"):
        rd = [in0, in1]
        a, b, c = out.ap, in0.ap, in1.ap
        if isinstance(scalar, V):
            rd.append(scalar)
            scalar = scalar.ap
        return self.op(eng, lambda e: e.scalar_tensor_tensor(a, b, scalar, c, op0, op1),
                       reads=rd, writes=[out])

    def copy(self, eng, out, in_):
        a, b = out.ap, in_.ap
        if eng == "act":
            return self.op("act", lambda e: e.copy(a, b), reads=[in_], writes=[out])
        return self.op(eng, lambda e: e.tensor_copy(a, b), reads=[in_], writes=[out])

    def reduce(self, out, in_, op, axis=None, eng="dve"):
        a, b = out.ap, in_.ap
        ax = axis if axis is not None else AX.X
        return self.op(eng, lambda e: e.tensor_reduce(a, b, ax, op), reads=[in_], writes=[out])

    def recip(self, out, in_):
        a, b = out.ap, in_.ap
        return self.op("dve", lambda e: e.reciprocal(a, b), reads=[in_], writes=[out])

    def memset(self, eng, out, val):
        a = out.ap
        return self.op(eng, lambda e: e.memset(a, val), reads=[], writes=[out])


def bcast_last(v, n):
    ap = v.ap
    pairs = [list(x) for x in ap.ap]
    new = bass.AP(ap.tensor, ap.offset, pairs + [[0, n]])
    return V(v.t, new)


def bcast_mid(v, n):
    ap = v.ap
    pairs = [list(x) for x in ap.ap]
    new = bass.AP(ap.tensor, ap.offset, [pairs[0], [0, n]] + pairs[1:])
    return V(v.t, new)


class RR:
    def __init__(self, engs):
        self.engs = engs
        self.i = 0

    def __call__(self):
        e = self.engs[self.i % len(self.engs)]
        self.i += 1
        return e


class Ctx:
    pass


def convert_weight_gen(P, cx, dst, src, rows, cols):
    kc_n = rows // 128
    sv = src.ap.rearrange("(kc p) o -> p kc o", p=128)
    dv = dst.ap.rearrange("(kc p) o -> p kc o", p=128)
    UN = 1024
    if cols >= UN:
        steps = [(k, 1, c0, UN) for k in range(kc_n) for c0 in range(0, cols, UN)]
    else:
        g = max(1, UN // cols)
        steps = [(k, min(g, kc_n - k), 0, cols) for k in range(0, kc_n, g)]
    for (k, nk, c0, cw) in steps:
        i = cx.cvg_i
        cx.cvg_i += 1
        st = cx.cvg_f[i % 2]
        sb = cx.cvg_b[i % 2]
        sview = st.v(st.ap[:, 0:nk * cw].rearrange("p (k c) -> p k c", k=nk))
        bview = sb.v(sb.ap[:, 0:nk * cw].rearrange("p (k c) -> p k c", k=nk))
        P.dma("sp", sview, src.v(sv[:, k:k + nk, c0:c0 + cw]))
        yield
        P.copy("act", bview, sview)
        yield
        P.dma("pool", dst.v(dv[:, k:k + nk, c0:c0 + cw]), bview)
        yield


def convert_weight(P, cx, dst, src, rows, cols):
    if rows % 128 == 0:
        kc_n = rows // 128
        sv = src.ap.rearrange("(kc p) o -> p kc o", p=128)
        dv = dst.ap.rearrange("(kc p) o -> p kc o", p=128)
        npart = 128
    else:
        kc_n = 1
        sv = src.ap.rearrange("(kc p) o -> p kc o", p=rows)
        dv = dst.ap.rearrange("(kc p) o -> p kc o", p=rows)
        npart = rows
    UN = 2048
    if cols >= UN:
        steps = [(k, 1, c0, UN) for k in range(kc_n) for c0 in range(0, cols, UN)]
    else:
        g = max(1, UN // cols)
        steps = [(k, min(g, kc_n - k), 0, cols) for k in range(0, kc_n, g)]
    for (k, nk, c0, cw) in steps:
        i = cx.cv_i
        cx.cv_i += 1
        st = cx.cv_f[i % len(cx.cv_f)]
        sb = cx.cv_b[i % len(cx.cv_b)]
        sview = st.v(st.ap[0:npart, 0:nk * cw].rearrange("p (k c) -> p k c", k=nk))
        bview = sb.v(sb.ap[0:npart, 0:nk * cw].rearrange("p (k c) -> p k c", k=nk))
        P.dma("sp", sview, src.v(sv[:, k:k + nk, c0:c0 + cw]))
        eng = cx.cv_rr()
        P.copy(eng, bview, sview)
        P.dma("pool", dst.v(dv[:, k:k + nk, c0:c0 + cw]), bview)


def layer_norm_fm(P, cx, z, hout, Tn, gcol, bcol, ln=None):
    if ln is None:
        ln = cx
    zsq = ln.ln_zsq
    P.act(zsq[:, :, 0:Tn], z, AF.Square)
    pm, pe2 = (cx.ps[6], cx.ps[7]) if not hasattr(ln, "ln_banks") else ln.ln_banks
    for c in range(DC):
        P.mm(pm[:, 0:Tn], cx.onesD[:, :], z[:, c, :], start=(c == 0), stop=(c == DC - 1))
    yield
    for c in range(DC):
        P.mm(pe2[:, 0:Tn], cx.onesD[:, :], zsq[:, c, 0:Tn], start=(c == 0), stop=(c == DC - 1))
    mean = ln.ln_mean
    rstd = ln.ln_rstd
    tmp = ln.ln_tmp
    P.copy("act", mean[:, 0:Tn], pm[:, 0:Tn])
    yield
    P.tt("dve", tmp[:, 0:Tn], mean[:, 0:Tn], mean[:, 0:Tn], ALU.mult)
    P.tt("dve", tmp[:, 0:Tn], pe2[:, 0:Tn], tmp[:, 0:Tn], ALU.subtract)
    yield
    P.act(tmp[:, 0:Tn], tmp[:, 0:Tn], AF.Sqrt, bias=cx.eps_ln[:, 0:1], scale=1.0)
    yield
    P.recip(rstd[:, 0:Tn], tmp[:, 0:Tn])
    P.tt("dve", z, z, bcast_mid(mean[:, 0:Tn], DC), ALU.subtract)
    yield
    P.tt("dve", z, z, bcast_mid(rstd[:, 0:Tn], DC), ALU.mult)
    yield
    for c in range(DC):
        P.act(hout[:, c, :], z[:, c, :], AF.Identity, bias=bcol[:, c:c + 1], scale=gcol[:, c:c + 1])


def mlp_tile(P, cx, ti, t0, Tn, hin, hout, wu, wd, gcol, bcol, final_out, nxt=None):
    hin_v = hin.ap.rearrange("(c p) t -> p c t", p=128)
    hout_v = None if hout is None else hout.ap.rearrange("(c p) t -> p c t", p=128)
    wu_v = wu.ap.rearrange("(kc p) o -> p kc o", p=128)
    wd_v = wd.ap.rearrange("(kc p) o -> p kc o", p=128)
    rr_sq = RR(["dve", "pool"])
    xin = cx.m_xin[ti % 2]
    xb = cx.m_xb[ti % 2]

    def load_in(tj, tt0, tTn):
        xi, xbb = cx.m_xin[tj % 2], cx.m_xb[tj % 2]
        P.dma("sp", xi[:, :, 0:tTn], hin.v(hin_v[:, :, tt0:tt0 + tTn]))
        P.copy("pool", xbb[:, :, 0:tTn], xi[:, :, 0:tTn])

    load_in(ti, t0, Tn)
    hmid = cx.m_hmid
    yield
    for u in range(4):
        wi = cx.w_i
        cx.w_i += 1
        wbuf = cx.wring[wi % len(cx.wring)]
        P.dma("sp", wbuf[:, :, :], wu.v(wu_v[:, :, u * 1024:(u + 1) * 1024]))
        for f8 in range(8):
            f = u * 8 + f8
            pb = cx.ps[f % 4]
            for kc in range(DC):
                P.mm(pb[:, 0:Tn], wbuf[:, kc, f8 * 128:(f8 + 1) * 128], xb[:, kc, 0:Tn],
                     start=(kc == 0), stop=(kc == DC - 1))
            rl = cx.m_relu[f % 2]
            P.act(rl[:, 0:Tn], pb[:, 0:Tn], AF.Relu)
            P.tt(rr_sq(), hmid[:, f, 0:Tn], rl[:, 0:Tn], rl[:, 0:Tn], ALU.mult)
        yield
    yield
    wbufs = []
    for u in range(4):
        wi = cx.w_i
        cx.w_i += 1
        wbuf = cx.wring[wi % len(cx.wring)]
        P.dma("sp", wbuf[:, :, :], wd.v(wd_v[:, u * 8:(u + 1) * 8, :]))
        wbufs.append(wbuf)
    z = cx.m_z
    for oc in range(DC):
        pb = cx.ps[4 + oc % 2]
        for kc in range(32):
            P.mm(pb[:, 0:Tn], wbufs[kc // 8][:, kc % 8, oc * 128:(oc + 1) * 128], hmid[:, kc, 0:Tn],
                 start=(kc == 0), stop=(kc == 31))
        P.stt(z[:, oc, 0:Tn], xin[:, oc, 0:Tn], ALPHA, pb[:, 0:Tn], ALU.mult, ALU.add)
        if oc % 2 == 1:
            yield
    ho = cx.m_ho
    yield from layer_norm_fm(P, cx, z[:, :, 0:Tn], ho[:, :, 0:Tn], Tn, gcol, bcol, None)
    yield
    if final_out is None:
        P.dma("pool", hout.v(hout_v[:, :, t0:t0 + Tn]), ho[:, :, 0:Tn])
    else:
        for sbk in range(Tn // 128):
            tok0 = t0 + sbk * 128
            if tok0 < 128:
                continue
            ot = cx.m_otm[cx.o_i % 2]
            cx.o_i += 1
            for half in range(2):
                pb = cx.ps[4 + half]
                for c4 in range(4):
                    c = half * 4 + c4
                    P.tr(pb[:, c4 * 128:(c4 + 1) * 128], ho[:, c, sbk * 128:(sbk + 1) * 128], cx.ident[:, :])
                P.copy("act" if half == 0 else "dve", ot[:, half * 512:(half + 1) * 512], pb[:, :])
            P.dma("pool", final_out[tok0 - 128:tok0, :], ot[:, :])
            yield


def mlp_phase(P, cx, hin, hout, wu, wd, gcol, bcol, tiles, final_out=None):
    gens = [mlp_tile(P, cx, ti, t0, Tn, hin, hout, wu, wd, gcol, bcol, final_out,
                     nxt=(tiles[ti + 1] if ti + 1 < len(tiles) else None))
            for ti, (t0, Tn) in enumerate(tiles)]
    run_pipelined(gens, lag=cx.mlp_lag, depth=2)


WSPECS = [
    ("a_w_r", 1024, 1024), ("a_w_k", 1024, 1024), ("a_w_v", 1024, 1024), ("a_w_o", 1024, 1024),
    ("a_w1", 1024, 64), ("a_w2", 64, 1024), ("a_a1", 1024, 64), ("a_a2", 64, 1024),
    ("a_g1", 1024, 128), ("a_g2", 128, 1024),
    ("kv_w_k", 1024, 256), ("kv_w_v", 1024, 256), ("b_w_q", 1024, 1024), ("b_w_o", 1024, 1024),
    ("up0", 1024, 4096), ("dn0", 4096, 1024), ("up1", 1024, 4096), ("dn1", 4096, 1024),
]
COL_MU, COL_LNG, COL_LNB, NCOL = 0, 48, 80, 112
ROW_W0, ROW_A0, ROW_KK, ROW_KA, ROW_RK, ROW_GNW, ROW_GNB, NROW = 0, 1, 2, 3, 4, 5, 6, 7
CM_IDENT, CM_ONESD, CM_MU, CM_MUE, CM_ML, CM_M1, CM_M2, NCM = 0, 1, 2, 3, 4, 5, 6, 7


def build_program(cfg):
    nc = bass.Bass("TRN2", target_bir_lowering=False)
    stack = contextlib.ExitStack()
    with stack:
        P = Prog(nc, stack)
        cx = Ctx()
        x_in = P.dram("x", [SEQ, D], F32, kind="ExternalInput")
        meta_in = P.dram("meta", [NMETA, D], F32, kind="ExternalInput")
        win = {}
        wbf = {}
        for (n, r, c) in WSPECS:
            win[n] = P.dram(n, [r, c], F32, kind="ExternalInput")
            wbf[n] = P.dram(n + "_bf", [r, c], BF16)
        colp_in = P.dram("colp", [128, NCOL], F32, kind="ExternalInput")
        rowp_in = P.dram("rowp", [NROW, D], F32, kind="ExternalInput")
        sinks_in = P.dram("sinks", [1, 16], F32, kind="ExternalInput")
        cmat_in = P.dram("cmat", [128, NCM * 128], F32, kind="ExternalInput")
        amask_in = P.dram("amask", [128, 3 * 256], F32, kind="ExternalInput")
        rope_in = P.dram("rope", [LP, 64], F32, kind="ExternalInput")
        out_t = P.dram("out", [SEQ, D], F32, kind="ExternalOutput")
        h_scr = [P.dram("hscr%d" % i, [D, LP], F32,
                        kind=("ExternalOutput" if cfg.get("dbg_h") == i else "Internal")) for i in range(3)]
        cx.o_scr = P.dram("o_scr", [D, LP], BF16)
        if cfg.get("test_hin"):
            h_test = P.dram("h_test", [D, LP], F32, kind="ExternalInput")
        if cfg.get("dbg_out"):
            dbg = P.dram("dbg", [D, LP], F32, kind="ExternalOutput")

        cx.ps = [P.ps("psb%d" % i, [128, 512], F32) for i in range(8)]
        cmat = P.sb("cmat_sb", [128, NCM * 128], F32)
        colp = P.sb("colp_sb", [128, NCOL], F32)
        cx.eps_ln = P.sb("eps_ln", [128, 1], F32)
        cx.eps_gn = P.sb("eps_gn", [128, 1], F32)
        P.dma("sp", cmat[:, :], cmat_in[:, :])
        P.dma("sp", colp[:, :], colp_in[:, :])
        P.memset("dve", cx.eps_ln[:, :], LN_EPS)
        P.memset("dve", cx.eps_gn[:, :], GN_EPS)
        cx.cmat = cmat
        cx.colp = colp
        cx.ident = cmat[:, CM_IDENT * 128:(CM_IDENT + 1) * 128]
        cx.onesD = cmat[:, CM_ONESD * 128:(CM_ONESD + 1) * 128]
        cx.w_i = 0
        cx.o_i = 0
        cx.cv_i = 0
        cx.TM = 384
        cx.c_depth = cfg.get("c_depth", 2)
        cx.mlp_lag = cfg.get("mlp_lag", 11)

        nblk = cfg.get("nblk", NBLK)
        with contextlib.ExitStack() as ph:
            gens0 = []
            if cfg.get("p0", True):
                cx.cv_f = [P.sb("cvf%d" % i, [128, 2048], F32, ph) for i in range(3)]
                cx.cv_b = [P.sb("cvb%d" % i, [128, 2048], BF16, ph) for i in range(3)]
                cx.cv_rr = RR(["dve", "act", "pool"])
                bg_on = cfg.get("bg_conv", True) and cfg.get("p1", True) and cfg.get("p1_split", True) \
                    and "b" in cfg.get("p1_parts", "abc")
                cx.bg_list = []
                for (n, r, c) in WSPECS:
                    if cfg.get("only_w") is not None and n not in cfg["only_w"]:
                        continue
                    if bg_on and not n.startswith("a_"):
                        cx.bg_list.append((n, r, c))
                        continue
                    gens0.append(("w", n, r, c))
            if cfg.get("p0b", True):
                cx.x_tm = [P.sb("x_tm%d" % i, [128, D], F32, ph) for i in range(2)]
                cx.x_fm = [P.sb("x_fm%d" % i, [128, DC, 128], F32, ph) for i in range(2)]

            def g_conv():
                for (_, n_, r_, c_) in gens0:
                    convert_weight(P, cx, wbf[n_], win[n_], r_, c_)
                    yield

            def g_xpose():
                if cfg.get("p0b", True):
                    for nn in range(nblk):
                        xpose_in_phase(P, cx, x_in, meta_in, h_scr[0], nn)
                        yield

            ga, gb = g_conv(), g_xpose()
            alive = [ga, gb]
            while alive:
                for g in list(alive):
                    try:
                        next(g)
                    except StopIteration:
                        alive.remove(g)
            P.barrier()
            P.emit()

        def mlp(layer, hin, hout, final_out, ph):
            TMm = cx.TM
            cx.m_xin = [P.sb("m_xin%d" % i, [128, DC, TMm], F32, ph) for i in range(2)]
            cx.m_xb = [P.sb("m_xb%d" % i, [128, DC, TMm], BF16, ph) for i in range(2)]
            cx.m_hmid = P.sb("m_hmid", [128, 32, TMm], BF16, ph)
            cx.m_relu = [P.sb("m_relu%d" % i, [128, TMm], F32, ph) for i in range(2)]
            cx.m_z = P.sb("m_z", [128, DC, TMm], F32, ph)
            cx.m_ho = P.sb("m_ho", [128, DC, TMm], F32, ph)
            cx.ln_zsq = P.sb("ln_zsq", [128, DC, TMm], F32, ph)
            cx.ln_mean = P.sb("ln_mean", [128, TMm], F32, ph)
            cx.ln_rstd = P.sb("ln_rstd", [128, TMm], F32, ph)
            cx.ln_tmp = P.sb("ln_tmp", [128, TMm], F32, ph)
            cx.m_otm = [P.sb("m_otm%d" % i, [128, D], F32, ph) for i in range(2)]
            cx.wring = [P.sb("wring%d" % i, [128, DC, 1024], BF16, ph) for i in range(5)]
            ntl = cfg.get("mlp_tiles", LP // TMm)
            if "nblk" in cfg and "mlp_tiles" not in cfg:
                tiles = [(i * 128, 128) for i in range(cfg["nblk"])]
            else:
                tiles = [(i * TMm, TMm) for i in range(ntl)]
            gi = COL_LNG + (layer * 2 + 1) * 8
            bi = COL_LNB + (layer * 2 + 1) * 8
            mlp_phase(P, cx, hin, hout, wbf["up%d" % layer], wbf["dn%d" % layer],
                      colp[:, gi:gi + 8], colp[:, bi:bi + 8], tiles,
                      final_out=final_out)

        if cfg.get("p1", True) and cfg.get("p1_split", True):
            cx.cm_scr = P.dram("cm_scr", [NBLK * 128, 4096], BF16)
            cx.tmb_scr = P.dram("tmb_scr", [NBLK * 128, 4096], BF16)
            cx.tmf_scr = P.dram("tmf_scr", [NBLK * 128, 2048], F32)
            cx.sm_scr = P.dram("sm_scr", [NBLK * 128, 32], F32)
            cx.front_pool = cfg.get("front_pool", 17)
            cx.back_depth = cfg.get("back_depth", 4)
            with contextlib.ExitStack() as ph:
                rwkv_front_setup(P, cx, ph, wbf, rowp_in)
                run_pipelined([rwkv_front_gen(P, cx, n, h_scr[0]) for n in range(nblk)],
                              lag=cfg.get("front_lag", 6), depth=2)
                P.barrier()
                P.emit()
            with contextlib.ExitStack() as ph:
                if "b" not in cfg.get("p1_parts", "abc"):
                    nblk_b = 0
                else:
                    nblk_b = nblk
                rwkv_back_setup(P, cx, ph, rowp_in)
                gens = []
                for n in range(nblk_b):
                    for b in range(4):
                        gens.append(rwkv_back_gen(P, cx, n, b, len(gens), nblk))
                bg = None
                if getattr(cx, "bg_list", None):
                    cx.cvg_f = [P.sb("cvgf", [128, 1024], F32, ph) for i in range(2)]
                    cx.cvg_b = [P.sb("cvgb", [128, 1024], BF16, ph) for i in range(2)]
                    cx.cvg_i = 0

                    def bg_all():
                        for (nm_, r_, c_) in cx.bg_list:
                            yield from convert_weight_gen(P, cx, wbf[nm_], win[nm_], r_, c_)
                    bg = bg_all()
                run_pipelined(gens, lag=cfg.get("back_lag", 2), depth=cx.back_depth, bg=bg,
                              bg_every=cfg.get("bg_every", 1))
                P.barrier()
                P.emit()
        if cfg.get("p1", True) and not cfg.get("p1_split", True):
            with contextlib.ExitStack() as ph:
                rwkv_setup(P, cx, ph, wbf, rowp_in)
                for n in range(0 if cfg.get("p1_skip_blocks") else nblk):
                    rwkv_block(P, cx, n, h_scr[0], h_scr[1])
                P.barrier()
                P.emit()
        if cfg.get("p1", True):
            with contextlib.ExitStack() as ph:
                if not cfg.get("p1c", True) or "c" not in cfg.get("p1_parts", "abc"):
                    raise_skip = True
                else:
                    raise_skip = False
                if "nblk" in cfg:
                    ctiles = [(i * 128, 128) for i in range(cfg["nblk"])]
                else:
                    ctiles = [(i * cx.TM, cx.TM) for i in range(LP // cx.TM)]
                if not raise_skip:
                    oproj_phase(P, cx, ph, wbf, h_scr[0], h_scr[1], ctiles)
                P.barrier()
                P.emit()

        cx.rope_in = rope_in
        if cfg.get("p2", True):
            with contextlib.ExitStack() as ph:
                mlp(0, h_scr[1], h_scr[2], None, ph)
                P.barrier()
                P.emit()

        if cfg.get("p3", True):
            with contextlib.ExitStack() as ph:
                attn_setup(P, cx, ph, wbf, sinks_in, amask_in)
                run_pipelined([attn_block(P, cx, n, h_scr[2], h_scr[0]) for n in range(nblk)], lag=cfg.get("attn_lag", 6), depth=3)
                P.barrier()
                P.emit()

        if cfg.get("p4", True):
            with contextlib.ExitStack() as ph:
                hin = h_test if cfg.get("test_hin") else h_scr[0]
                mlp(1, hin, None, out_t, ph)
                P.barrier()
                P.emit()
        P.finish([out_t])
        P.emit()
        cx.n_ops = P.n_ops
    return nc, cx


def host_consts():
    cm = np.zeros((128, NCM, 128), np.float32)
    i = np.arange(128)
    cm[:, CM_IDENT] = np.eye(128, dtype=np.float32)
    cm[:, CM_ONESD] = 1.0 / D
    s, t = i[:, None], i[None, :]
    cm[:, CM_MU] = (s < t)
    cm[:, CM_MUE] = (s <= t)
    cm[:, CM_ML] = (t < s)
    cm[:, CM_M1] = (s <= t).astype(np.float32) - (s <= 63).astype(np.float32)
    cm[:, CM_M2] = (s > t)
    return cm.reshape(128, NCM * 128)


def host_rope():
    inv_freq = (1.0 / (10000.0 ** (np.arange(0, 64, 2, dtype=np.float32) / np.float32(64)))).astype(np.float32)
    pos = np.maximum(np.arange(LP) - 112, 0).astype(np.float32)
    ang = (pos[:, None] * inv_freq[None, :]).astype(np.float32)
    return np.concatenate([np.cos(ang), np.sin(ang)], axis=1).astype(np.float32)


def host_amask():
    NEG = -30000.0
    qi = np.arange(128)[:, None]
    kj = np.arange(256)[None, :]
    rel = 128 + qi - kj
    inwin = (rel >= 0) & (rel < 128)
    m = np.zeros((3, 128, 256), np.float32)
    for v, nb in enumerate([2, 0, 1]):
        key_pos = (nb - 1) * 128 + np.arange(256)[None, :]
        ok = inwin & (key_pos >= 112)
        m[v] = np.where(ok, 0.0, NEG)
    return np.ascontiguousarray(m.transpose(1, 0, 2).reshape(128, 768))


_CACHE = {}


def kernel(x, meta_tokens, a_mu, a_w_r, a_w_k, a_w_v, a_w_o, a_w0, a_w1, a_w2,
           a_a0, a_a1, a_a2, a_g1, a_g2, a_k_k, a_k_a, a_r_k, a_gn_w, a_gn_b,
           kv_w_k, kv_w_v, b_w_q, b_sinks, b_w_o, mlp_w_up, mlp_w_down, ln_g, ln_b):
    f = lambda a: np.ascontiguousarray(np.asarray(a, dtype=np.float32))
    x = f(x)
    if "nc" not in _CACHE:
        _CACHE["nc"] = build_program({})[0]
    nc = _CACHE["nc"]
    shared = {
        "meta": f(meta_tokens),
        "a_w_r": f(a_w_r)[0], "a_w_k": f(a_w_k)[0], "a_w_v": f(a_w_v)[0], "a_w_o": f(a_w_o)[0],
        "a_w1": f(a_w1)[0], "a_w2": f(a_w2)[0], "a_a1": f(a_a1)[0], "a_a2": f(a_a2)[0],
        "a_g1": f(a_g1)[0], "a_g2": f(a_g2)[0],
        "kv_w_k": f(kv_w_k), "kv_w_v": f(kv_w_v), "b_w_q": f(b_w_q)[0], "b_w_o": f(b_w_o)[0],
        "up0": f(mlp_w_up)[0], "dn0": f(mlp_w_down)[0], "up1": f(mlp_w_up)[1], "dn1": f(mlp_w_down)[1],
    }
    colp = np.zeros((128, NCOL), np.float32)
    mu = f(a_mu)[0]
    for i in range(6):
        colp[:, COL_MU + i * 8:COL_MU + (i + 1) * 8] = mu[i].reshape(8, 128).T
    lg, lb = f(ln_g), f(ln_b)
    for l in range(2):
        for j in range(2):
            k = (l * 2 + j) * 8
            colp[:, COL_LNG + k:COL_LNG + k + 8] = lg[l, j].reshape(8, 128).T
            colp[:, COL_LNB + k:COL_LNB + k + 8] = lb[l, j].reshape(8, 128).T
    rowp = np.stack([f(a_w0)[0], f(a_a0)[0], f(a_k_k)[0], f(a_k_a)[0], f(a_r_k)[0].reshape(-1),
                     f(a_gn_w)[0], f(a_gn_b)[0]], axis=0)
    shared.update({"colp": colp, "rowp": np.ascontiguousarray(rowp),
                   "sinks": f(b_sinks)[0].reshape(1, 16),
                   "cmat": host_consts(), "amask": host_amask(), "rope": host_rope()})
    in_maps = []
    for b in range(8):
        m = dict(shared)
        m["x"] = x[b]
        in_maps.append(m)
    res = run_bass_kernel_spmd(nc, in_maps, core_ids=list(range(8)))
    return np.stack([res.results[b]["out"] for b in range(8)], axis=0).astype(np.float32)


def xpose_in_phase(P, cx, x_in, meta_in, h0, n_only):
    h0v = h0.ap.rearrange("(c p) t -> p c t", p=128)
    for n in [n_only]:
        xt = cx.x_tm[n % 2]
        if n == 0:
            P.memset("pool", xt[:, :], 0.0)
            P.dma("sp", xt[112:128, :], meta_in[:, :])
        else:
            P.dma("sp", xt[:, :], x_in[(n - 1) * 128:n * 128, :])
        xo = cx.x_fm[n % 2]
        for half in range(2):
            pb = cx.ps[(2 * n + half) % 8]
            for c4 in range(4):
                c = half * 4 + c4
                P.tr(pb[:, c4 * 128:(c4 + 1) * 128], xt[:, c * 128:(c + 1) * 128], cx.ident[:, :])
            dst = xo[:, half * 4:(half + 1) * 4, :]
            P.copy("act" if half == 0 else "dve", dst, v3(pb[:, :], 4))
        P.dma("pool", h0.v(h0v[:, :, n * 128:(n + 1) * 128]), xo[:, :, :])


class Pool_:
    def __init__(self, bufs):
        self.free = list(bufs)

    def get(self):
        return self.free.pop(0)

    def put(self, *bs):
        for b in bs:
            self.free.append(b)


def v3(v, c):
    return V(v.t, v.ap.rearrange("p (c t) -> p c t", c=c))


def rwkv_block(P, cx, n, h0, h1):
    R = cx.rw
    ident = cx.ident
    tp = cx.tmpool
    colp = cx.colp
    cm_ = cx.cmat
    nb = cx.nextbank
    t0 = n * 128
    h0v = h0.ap.rearrange("(c p) t -> p c t", p=128)
    h1v = h1.ap.rearrange("(c p) t -> p c t", p=128)
    xf = cx.xfm[0]
    if n == 0:
        P.memset("pool", xf[:, :, 0:1], 0.0)
        P.dma("sp", xf[:, :, 1:129], h0.v(h0v[:, :, 0:128]))
    else:
        P.dma("sp", xf[:, :, 0:129], h0.v(h0v[:, :, t0 - 1:t0 + 128]))
    xxb = tp.get()
    xx = v3(xxb[:, :], 8)
    P.tt("pool", xx[:, :, :], xf[:, :, 0:128], xf[:, :, 1:129], ALU.subtract)

    def mix(i):
        m = cx.mixb[cx.mix_i % 2]
        cx.mix_i += 1
        tmp = tp.get()
        tv = v3(tmp[:, :], 8)
        P.tt("dve", tv, xx[:, :, :], bcast_last(colp[:, COL_MU + i * 8:COL_MU + i * 8 + 8], 128), ALU.mult)
        P.tt("dve", m[:, :, :], tv, xf[:, :, 1:129], ALU.add)
        tp.put(tmp)
        return m

    def proj_tm(m, W):
        banks = [nb(), nb()]
        for half in range(2):
            for c in range(DC):
                P.mm(banks[half][:, :], m[:, c, :], W[:, c, half * 512:(half + 1) * 512],
                     start=(c == 0), stop=(c == DC - 1))
        return banks

    def evac2(banks, dst, engs):
        for half in range(2):
            P.copy(engs[half], dst[:, half * 512:(half + 1) * 512], banks[half][:, :])

    r32, k32, v32 = tp.get(), tp.get(), tp.get()
    m_r = mix(0)
    m_k = mix(2)
    b_r = proj_tm(m_r, R["wr"])
    m_v = mix(3)
    b_k = proj_tm(m_k, R["wk"])
    evac2(b_r, r32, ["act", "act"])
    b_v = proj_tm(m_v, R["wv"])
    evac2(b_k, k32, ["act", "act"])
    evac2(b_v, v32, ["act", "act"])
    vbf = cx.vbf
    P.copy("act", vbf[:, :], v32[:, :])

    def lora1(i, W1, width, func, dst):
        m = mix(i)
        b = nb()
        for c in range(DC):
            P.mm(b[:, 0:128], W1[:, c, :], m[:, c, :], start=(c == 0), stop=(c == DC - 1))
        P.act(dst[:, :], b[:, 0:128], func)

    lora1(1, R["w1"], 64, AF.Tanh, cx.hw)
    lora1(4, R["a1"], 64, AF.Identity, cx.ha)
    lora1(5, R["g1"], 128, AF.Sigmoid, cx.hg)
    tp.put(xxb)

    def lora2(hsrc, width, W2, biasrow):
        banks = [nb(), nb()]
        for half in range(2):
            sl = slice(half * 512, (half + 1) * 512)
            if biasrow is not None:
                P.mm(banks[half][:, :], cx.ones_row[:, :], cx.brow[:, biasrow, sl], start=True, stop=False)
            P.mm(banks[half][:, :], hsrc[:, :], W2[:, sl], start=(biasrow is None), stop=True)
        return banks

    sg, alr, g32 = tp.get(), tp.get(), tp.get()
    bw = lora2(cx.hw, 64, R["w2"], 0)
    for half in range(2):
        P.act(sg[:, half * 512:(half + 1) * 512], bw[half][:, :], AF.Sigmoid)
    ba = lora2(cx.ha, 64, R["a2"], 1)
    for half in range(2):
        P.act(alr[:, half * 512:(half + 1) * 512], ba[half][:, :], AF.Sigmoid)
    evac2(lora2(cx.hg, 128, R["g2"], None), g32, ["act", "dve"])

    M1 = cm_[:, CM_M1 * 128:(CM_M1 + 1) * 128]
    M2 = cm_[:, CM_M2 * 128:(CM_M2 + 1) * 128]
    eP, eM, eR, eS = tp.get(), tp.get(), tp.get(), tp.get()
    for half in range(2):
        sl = slice(half * 512, (half + 1) * 512)
        bD = nb()
        P.mm(bD[:, :], M1, sg[:, sl])
        P.act(eP[:, sl], bD[:, :], AF.Exp, scale=-C0)
        P.act(eM[:, sl], bD[:, :], AF.Exp, scale=C0)
        bR = nb()
        P.mm(bR[:, :], M2, sg[:, sl])
        P.act(eR[:, sl], bR[:, :], AF.Exp, scale=-C0)
    P.act(eS[:, :], sg[:, :], AF.Exp, scale=C0)
    bT = nb()
    for c in range(DC):
        P.mm(bT[:, 2 * c:2 * c + 2], sg[:, c * 128:(c + 1) * 128], cx.ones2[:, 0:2])
    PC, PM = cx.PC, cx.PM
    bTv = v3(bT[:, 0:16], 8)
    P.act(PC[:, :], bTv[:, :, 0], AF.Exp, scale=-C0)
    P.act(PM[:, :], bTv[:, :, 1], AF.Exp, scale=-C0)
    tp.put(sg)
    P.tt("pool", eS[:, :], eS[:, :], eP[:, :], ALU.mult)
    eA = eS

    rowb = cx.rowb
    kk = tp.get()
    P.tt("pool", kk[:, :], k32[:, :], rowb[:, 0, :], ALU.mult)
    kmod = tp.get()
    P.stt(kmod[:, :], alr[:, :], -1.0, rowb[:, 1, :], ALU.add, ALU.mult)
    P.stt(kmod[:, :], kmod[:, :], 1.0, k32[:, :], ALU.add, ALU.mult)
    tp.put(k32)
    sq = tp.get()
    P.act(sq[:, :], kk[:, :], AF.Square)
    ss, rn = cx.ss, cx.rn
    P.reduce(ss[:, :], v3(sq[:, :], 16), ALU.add)
    P.ts("dve", ss[:, :], ss[:, :], 1e-24, ALU.max)
    P.act(ss[:, :], ss[:, :], AF.Sqrt)
    P.recip(rn[:, :], ss[:, :])
    P.tt("dve", v3(kk[:, :], 16), v3(kk[:, :], 16), bcast_last(rn[:, :], 64), ALU.mult)
    tp.put(sq)
    bb = tp.get()
    P.tt("pool", bb[:, :], kk[:, :], alr[:, :], ALU.mult)
    tp.put(alr)
    sq = tp.get()
    P.tt("pool", sq[:, :], r32[:, :], kmod[:, :], ALU.mult)
    P.tt("pool", sq[:, :], sq[:, :], rowb[:, 2, :], ALU.mult)
    bon = cx.bon
    P.reduce(bon[:, :], v3(sq[:, :], 16), ALU.add)
    tp.put(sq)
    P.tt("dve", eP[:, :], eP[:, :], r32[:, :], ALU.mult)
    rt = eP
    tp.put(r32)
    bt = tp.get()
    P.tt("pool", bt[:, :], bb[:, :], eM[:, :], ALU.mult)
    P.tt("dve", eM[:, :], eM[:, :], kmod[:, :], ALU.mult)
    kt = eM
    P.stt(eA[:, :], kk[:, :], -1.0, eA[:, :], ALU.mult, ALU.mult)
    at = eA
    tp.put(kk)
    khat, bhat, atbf = cx.khat, cx.bhat, cx.atbf
    P.tt("pool", khat[:, :], kmod[:, :], eR[:, :], ALU.mult)
    P.tt("pool", bhat[:, :], bb[:, :], eR[:, :], ALU.mult)
    P.copy("act", atbf[:, :], at[:, :])
    tp.put(eR, kmod, bb)

    cmz = cx.cmz
    for kind, src in enumerate([bt, kt, at, rt]):
        for half in range(2):
            pb = nb()
            for c4 in range(4):
                p = half * 4 + c4
                P.tr(pb[:, c4 * 128:(c4 + 1) * 128], src[:, p * 128:(p + 1) * 128], ident[:, :])
            P.copy("act", cmz[0][0:64, half * 4:(half + 1) * 4, kind, :], v3(pb[0:64, :], 4))
            P.copy("dve", cmz[1][64:128, half * 4:(half + 1) * 4, kind, :], v3(pb[64:128, :], 4))
    tp.put(bt, kt, at, rt)

    ST, STz = cx.ST, cx.STz
    P.tt("pool", STz[0][0:64, :, :], ST[0:64, :, :], bcast_last(PM[0:64, :], 64), ALU.mult)
    P.tt("pool", STz[1][64:128, :, :], ST[64:128, :, :], bcast_last(PM[64:128, :], 64), ALU.mult)

    y32 = tp.get()
    MU = cm_[:, CM_MU * 128:(CM_MU + 1) * 128]
    MUE = cm_[:, CM_MUE * 128:(CM_MUE + 1) * 128]
    ML = cm_[:, CM_ML * 128:(CM_ML + 1) * 128]
    identb = cx.identb
    def batch_gen(bt_i):
        heads = [4 * bt_i + i for i in range(4)]
        BS = cx.bsets[bt_i % 3]

        def hsl(h):
            return h % 2, h // 2

        def gram(kl, kr, mask, dst, eng):
            pb = nb()
            for i, h in enumerate(heads):
                e, p = hsl(h)
                P.mm(pb[:, i * 128:(i + 1) * 128], cmz[e][:, p, kl, :], cmz[e][:, p, kr, :])
            P.tt(eng, dst[:, :, :], v3(pb[:, :], 4), bcast_mid(mask, 4), ALU.mult)

        X, XT = BS.Xb[0], BS.XTb[0]
        gram(0, 2, MU, X, "dve")
        gram(2, 0, ML, XT, "dve")
        yield
        AakT, Arb, Ark = BS.AakT, BS.Arb, BS.Ark
        gram(2, 1, ML, AakT, "dve")
        gram(0, 3, MUE, Arb, "dve")
        gram(1, 3, MUE, Ark, "dve")
        yield
        Pm = BS.Pb[0]
        P.tt("pool", Pm[:, :, :], X[:, :, :], bcast_mid(identb[:, :], 4), ALU.add)
        cur = 0
        for k in range(1, 7):
            pbx = None
            if k <= 5:
                pbx = nb()
                for i in range(4):
                    P.mm(pbx[:, i * 128:(i + 1) * 128], XT[:, i, :], X[:, i, :])
            pbt = nb()
            for i in range(4):
                P.mm(pbt[:, i * 128:(i + 1) * 128], X[:, i, :], XT[:, i, :])
            if pbx is not None:
                P.copy("act", X[:, :, :], v3(pbx[:, :], 4))
            P.copy("act", XT[:, :, :], v3(pbt[:, :], 4))
            pb = nb()
            for i in range(4):
                P.mm(pb[:, i * 128:(i + 1) * 128], XT[:, i, :], Pm[:, i, :])
            P.tt("dve", Pm[:, :, :], v3(pb[:, :], 4), Pm[:, :, :], ALU.add)
            yield
        Tm = Pm
        Wt, M2m = BS.Wt, BS.M2m
        Ubf = BS.Ubf
        pb = nb()
        for i, h in enumerate(heads):
            e, p = hsl(h)
            P.mm(pb[:, i * 128:(i + 1) * 128], atbf[:, p * 128:(p + 1) * 128], Tm[:, i, :])
        pb4 = v3(pb[:, :], 4)
        P.copy("act", Wt[0:64, 0, :], pb4[0:64, 0, :])
        P.copy("act", Wt[0:64, 1, :], pb4[0:64, 2, :])
        P.copy("dve", Wt[64:128, 0, :], pb4[64:128, 1, :])
        P.copy("dve", Wt[64:128, 1, :], pb4[64:128, 3, :])
        pb = nb()
        for i in range(4):
            P.mm(pb[:, i * 128:(i + 1) * 128], AakT[:, i, :], Tm[:, i, :])
        P.copy("act", M2m[:, :, :], v3(pb[:, :], 4))
        yield
        pb = nb()
        for i, h in enumerate(heads):
            e, p = hsl(h)
            hs = slice(h * 64, (h + 1) * 64)
            P.mm(pb[:, i * 64:(i + 1) * 64], Wt[:, i // 2, :], STz[e][:, p, :], start=True, stop=False)
            P.mm(pb[:, i * 64:(i + 1) * 64], M2m[:, i, :], vbf[:, hs], start=False, stop=True)
        P.copy("act", Ubf[:, :], pb[:, 0:256])
        yield
        pb = nb()
        for i, h in enumerate(heads):
            e, p = hsl(h)
            hs = slice(h * 64, (h + 1) * 64)
            P.mm(pb[:, i * 64:(i + 1) * 64], cmz[e][:, p, 3, :], STz[e][:, p, :], start=True, stop=False)
            P.mm(pb[:, i * 64:(i + 1) * 64], Arb[:, i, :], Ubf[:, i * 64:(i + 1) * 64], start=False, stop=False)
            P.mm(pb[:, i * 64:(i + 1) * 64], Ark[:, i, :], vbf[:, hs], start=False, stop=True)
        P.copy("act", y32[:, bt_i * 256:(bt_i + 1) * 256], pb[:, 0:256])
        yield
        pb = nb()
        for i, h in enumerate(heads):
            e, p = hsl(h)
            hs = slice(h * 64, (h + 1) * 64)
            ps_ = slice(p * 128, (p + 1) * 128)
            P.mm(pb[:, i * 64:(i + 1) * 64], bhat[:, ps_], Ubf[:, i * 64:(i + 1) * 64], start=True, stop=False)
            P.mm(pb[:, i * 64:(i + 1) * 64], khat[:, ps_], vbf[:, hs], start=False, stop=True)
        for i, h in enumerate(heads):
            e, p = hsl(h)
            o = slice(64 * e, 64 * e + 64)
            P.stt(ST[o, p, :], ST[o, p, :], PC[o, p:p + 1], pb[o, i * 64:(i + 1) * 64], ALU.mult, ALU.add)

        yield

    run_pipelined([batch_gen(b) for b in range(4)], lag=4, depth=3)

    ysum, yv = cx.ysum, cx.yv
    P.reduce(ysum[:, :], v3(y32[:, :], 16), ALU.add)
    P.ts("dve", ysum[:, :], ysum[:, :], -1.0 / 64, ALU.mult)
    P.tt("pool", v3(y32[:, :], 16), v3(y32[:, :], 16), bcast_last(ysum[:, :], 64), ALU.add)
    sq = tp.get()
    P.act(sq[:, :], y32[:, :], AF.Square)
    P.reduce(yv[:, :], v3(sq[:, :], 16), ALU.add)
    P.act(yv[:, :], yv[:, :], AF.Sqrt, bias=cx.eps_gn[:, 0:1], scale=1.0 / 64)
    P.recip(yv[:, :], yv[:, :])
    P.tt("dve", v3(y32[:, :], 16), v3(y32[:, :], 16), bcast_last(yv[:, :], 64), ALU.mult)
    P.tt("pool", y32[:, :], y32[:, :], rowb[:, 3, :], ALU.mult)
    P.tt("pool", y32[:, :], y32[:, :], rowb[:, 4, :], ALU.add)
    P.tt("dve", v3(sq[:, :], 16), v3(v32[:, :], 16), bcast_last(bon[:, :], 64), ALU.mult)
    P.tt("pool", y32[:, :], y32[:, :], sq[:, :], ALU.add)
    P.tt("dve", y32[:, :], y32[:, :], g32[:, :], ALU.mult)
    tp.put(sq, v32, g32)
    ofm = cx.ofm[0]
    for half in range(2):
        pb = nb()
        for c4 in range(4):
            c = half * 4 + c4
            P.tr(pb[:, c4 * 128:(c4 + 1) * 128], y32[:, c * 128:(c + 1) * 128], ident[:, :])
        P.copy("act" if half == 0 else "dve", ofm[:, half * 4:(half + 1) * 4, :], v3(pb[:, :], 4))
    tp.put(y32)
    osv = cx.o_scr.ap.rearrange("(c p) t -> p c t", p=128)
    P.dma("pool", cx.o_scr.v(osv[:, :, t0:t0 + 128]), ofm[:, :, :])


def oproj_tile(P, cx, ti, t0, Tn, h0, h1):
    S = cx.c_sets[ti % 3]
    h0v = h0.ap.rearrange("(c p) t -> p c t", p=128)
    h1v = h1.ap.rearrange("(c p) t -> p c t", p=128)
    osv = cx.o_scr.ap.rearrange("(c p) t -> p c t", p=128)
    P.dma("sp", S.o[:, :, 0:Tn], cx.o_scr.v(osv[:, :, t0:t0 + Tn]))
    P.dma("sp", S.x[:, :, 0:Tn], h0.v(h0v[:, :, t0:t0 + Tn]))
    yield
    for oc in range(DC):
        pb = cx.ps[oc % 2]
        for kc in range(DC):
            P.mm(pb[:, 0:Tn], cx.c_wo[:, kc, oc * 128:(oc + 1) * 128], S.o[:, kc, 0:Tn],
                 start=(kc == 0), stop=(kc == DC - 1))
        P.stt(S.z[:, oc, 0:Tn], S.x[:, oc, 0:Tn], ALPHA, pb[:, 0:Tn], ALU.mult, ALU.add)
        if oc % 2 == 1:
            yield
    yield from layer_norm_fm(P, cx, S.z[:, :, 0:Tn], S.ho[:, :, 0:Tn], Tn, cx.colp[:, COL_LNG:COL_LNG + 8],
                             cx.colp[:, COL_LNB:COL_LNB + 8], S)
    yield
    P.dma("pool", h1.v(h1v[:, :, t0:t0 + Tn]), S.ho[:, :, 0:Tn])
    yield


def oproj_phase(P, cx, ph, wbf, h0, h1, tiles):
    TMm = cx.TM
    cx.c_wo = P.sb("c_wo", [128, 8, 1024], BF16, ph)
    P.dma("sp", cx.c_wo[:, :, :], wbf["a_w_o"].v(wbf["a_w_o"].ap.rearrange("(kc p) o -> p kc o", p=128)))
    cx.c_sets = []
    for par in range(3):
        S = Ctx()
        S.o = P.sb("c_o", [128, DC, TMm], BF16, ph)
        S.x = P.sb("c_x", [128, DC, TMm], F32, ph)
        S.z = P.sb("c_z", [128, DC, TMm], F32, ph)
        S.ho = P.sb("c_ho", [128, DC, TMm], F32, ph)
        S.ln_zsq = P.sb("c_zsq", [128, DC, TMm], F32, ph)
        S.ln_mean = P.sb("c_mean", [128, TMm], F32, ph)
        S.ln_rstd = P.sb("c_rstd", [128, TMm], F32, ph)
        S.ln_tmp = P.sb("c_tmp", [128, TMm], F32, ph)
        S.ln_banks = (cx.ps[2 + 2 * par], cx.ps[3 + 2 * par])
        cx.c_sets.append(S)
    gens = [oproj_tile(P, cx, ti, t0, Tn, h0, h1) for ti, (t0, Tn) in enumerate(tiles)]
    run_pipelined(gens, lag=4, depth=3)


def rwkv_setup(P, cx, ph, wbf, rowp_in):
    R = {}

    def wload(key, name, rows, cols):
        if rows % 128 == 0 and rows > 128:
            t = P.sb("rw_" + key, [128, rows // 128, cols], BF16, ph)
            P.dma("sp", t[:, :, :], wbf[name].v(wbf[name].ap.rearrange("(kc p) o -> p kc o", p=128)))
        else:
            t = P.sb("rw_" + key, [rows, cols], BF16, ph)
            P.dma("sp", t[:, :], wbf[name][:, :])
        R[key] = t

    wload("wr", "a_w_r", 1024, 1024)
    wload("wk", "a_w_k", 1024, 1024)
    wload("wv", "a_w_v", 1024, 1024)
    for key, name in (("w1", "a_w1"), ("a1", "a_a1")):
        t = P.sb("rw_" + key, [128, 8, 128], BF16, ph)
        P.memset("pool", t[:, :, :], 0.0)
        P.dma("sp", t[:, :, 0:64], wbf[name].v(wbf[name].ap.rearrange("(kc p) o -> p kc o", p=128)))
        R[key] = t
    wload("g1", "a_g1", 1024, 128)
    for key, name in (("w2", "a_w2"), ("a2", "a_a2")):
        t = P.sb("rw_" + key, [128, 1024], BF16, ph)
        P.memset("pool", t[:, :], 0.0)
        P.dma("sp", t[0:64, :], wbf[name][:, :])
        R[key] = t
    wload("g2", "a_g2", 128, 1024)
    cx.rw = R
    cx.rowb = P.sb("rowb", [128, 5, D], F32, ph)
    for j, ri in enumerate([ROW_KK, ROW_KA, ROW_RK, ROW_GNW, ROW_GNB]):
        src = rowp_in.ap[ri:ri + 1, :]
        bsrc = bass.AP(src.tensor, src.offset, [[0, 128], [1, D]])
        P.dma("sp", cx.rowb[:, j, :], rowp_in.v(bsrc))
    cx.brow = P.sb("brow", [128, 2, D], F32, ph)
    P.memset("pool", cx.brow[:, :, :], 0.0)
    P.dma("sp", cx.brow[0:1, 0, :], rowp_in[ROW_W0:ROW_W0 + 1, :])
    P.dma("sp", cx.brow[0:1, 1, :], rowp_in[ROW_A0:ROW_A0 + 1, :])
    cx.ones_row = P.sb("ones_row", [128, 128], F32, ph)
    P.memset("dve", cx.ones_row[:, :], 0.0)
    P.memset("dve", cx.ones_row[0:1, :], 1.0)
    cx.ones2 = P.sb("ones2", [128, 2], F32, ph)
    P.memset("dve", cx.ones2[:, :], 1.0)
    P.memset("dve", cx.ones2[64:128, 1:2], 0.0)
    cx.identb = P.sb("identb", [128, 128], BF16, ph)
    P.copy("dve", cx.identb[:, :], cx.ident[:, :])
    cx.xfm = [P.sb("xfm%d" % i, [128, DC, 129], F32, ph) for i in range(1)]
    cx.mixb = [P.sb("mixb%d" % i, [128, DC, 128], BF16, ph) for i in range(2)]
    cx.mix_i = 0
    cx.tmpool = Pool_([P.sb("tm%d" % i, [128, D], F32, ph) for i in range(11)])
    cx.vbf = P.sb("vbf", [128, D], BF16, ph)
    cx.khat = P.sb("khat", [128, D], BF16, ph)
    cx.bhat = P.sb("bhat", [128, D], BF16, ph)
    cx.atbf = P.sb("atbf", [128, D], BF16, ph)
    cx.hw = P.sb("hw", [128, 128], BF16, ph)
    cx.ha = P.sb("ha", [128, 128], BF16, ph)
    cx.hg = P.sb("hg", [128, 128], BF16, ph)
    cx.cmz = [P.sb("cmz%d" % i, [128, 8, 4, 128], BF16, ph) for i in range(2)]
    for i in range(2):
        P.memset("pool", cx.cmz[i][:, :, :, :], 0.0)
    cx.bsets = []
    for i in range(3):
        BS = Ctx()
        BS.Xb = [P.sb("Xb", [128, 4, 128], BF16, ph) for j in range(1)]
        BS.XTb = [P.sb("XTb", [128, 4, 128], BF16, ph) for j in range(1)]
        BS.Pb = [P.sb("Pb", [128, 4, 128], BF16, ph) for j in range(1)]
        BS.AakT = P.sb("AakT", [128, 4, 128], BF16, ph)
        BS.Arb = P.sb("Arb", [128, 4, 128], BF16, ph)
        BS.Ark = P.sb("Ark", [128, 4, 128], BF16, ph)
        BS.Wt = P.sb("Wt", [128, 2, 128], BF16, ph)
        BS.M2m = P.sb("M2m", [128, 4, 128], BF16, ph)
        BS.Ubf = P.sb("Ubf", [128, 256], BF16, ph)
        cx.bsets.append(BS)
    cx.ST = P.sb("ST", [128, 8, 64], F32, ph)
    cx.STz = [P.sb("STz%d" % i, [128, 8, 64], BF16, ph) for i in range(2)]
    P.memset("dve", cx.ST[:, :, :], 0.0)
    for i in range(2):
        P.memset("pool", cx.STz[i][:, :, :], 0.0)
    cx.PC = P.sb("PC", [128, 8], F32, ph)
    cx.PM = P.sb("PM", [128, 8], F32, ph)
    cx.ss = P.sb("ss", [128, 16], F32, ph)
    cx.rn = P.sb("rn", [128, 16], F32, ph)
    cx.bon = P.sb("bon", [128, 16], F32, ph)
    cx.ysum = P.sb("ysum", [128, 16], F32, ph)
    cx.yv = P.sb("yv", [128, 16], F32, ph)
    cx.ofm = [P.sb("ofm", [128, DC, 128], BF16, ph) for i in range(1)]
    cx.bank_i = 0

    def nextbank():
        b = cx.ps[cx.bank_i % 8]
        cx.bank_i += 1
        return b

    cx.nextbank = nextbank


def attn_setup(P, cx, ph, wbf, sinks_in, amask_in):
    A = {}
    for key, name, cols in (("wq", "b_w_q", 1024), ("wo", "b_w_o", 1024), ("wk", "kv_w_k", 256), ("wv", "kv_w_v", 256)):
        t = P.sb("aw_" + key, [128, 8, cols], BF16, ph)
        P.dma("sp", t[:, :, :], wbf[name].v(wbf[name].ap.rearrange("(kc p) o -> p kc o", p=128)))
        A[key] = t
    cx.aw = A
    cx.amask = P.sb("amask_sb", [128, 3, 256], F32, ph)
    P.dma("sp", cx.amask[:, :, :], amask_in.v(amask_in.ap.rearrange("p (v k) -> p v k", v=3)))
    cx.sinkb = P.sb("sinkb", [128, 16], F32, ph)
    src = sinks_in.ap[0:1, :]
    P.dma("sp", cx.sinkb[:, :], sinks_in.v(bass.AP(src.tensor, src.offset, [[0, 128], [1, 16]])))
    cx.a_kT = P.sb("a_kT", [128, 4, 2, 4, 128], BF16, ph)
    P.memset("pool", cx.a_kT[:, :, :, :, :], 0.0)
    cx.a_vb = P.sb("a_vb", [128, 4, 256], BF16, ph)
    P.memset("pool", cx.a_vb[:, :, :], 0.0)
    cx.a_sets = []
    for par in range(3):
        S = Ctx()
        S.x = P.sb("a_x", [128, DC, 128], F32, ph)
        S.xb = P.sb("a_xb", [128, DC, 128], BF16, ph)
        S.rope = P.sb("a_rope", [128, 64], F32, ph)
        S.q32 = P.sb("a_q32", [128, D], F32, ph)
        S.qr = P.sb("a_qr", [128, D], F32, ph)
        S.k32 = P.sb("a_k32", [128, 256], F32, ph)
        S.kr = P.sb("a_kr", [128, 2, 256], F32, ph)
        S.tA = P.sb("a_tA", [128, 512], F32, ph)
        S.tB = P.sb("a_tB", [128, 512], F32, ph)
        S.qT = P.sb("a_qT", [128, 8, 128], BF16, ph)
        S.sm = [P.sb("a_sm", [128, 2, 256], F32, ph) for i in range(8)]
        S.pT = [P.sb("a_pT", [128, 4, 128], BF16, ph) for i in range(2)]
        S.mx = P.sb("a_mx", [128, 16], F32, ph)
        S.negm = P.sb("a_negm", [128, 16], F32, ph)
        S.rs = P.sb("a_rs", [128, 16], F32, ph)
        S.es = P.sb("a_es", [128, 16], F32, ph)
        S.o32 = S.q32
        S.ofm = P.sb("a_ofm", [128, DC, 128], BF16, ph)
        S.z = v3(S.qr[:, :], 8)
        S.ho = S.z
        S.ln_zsq = P.sb("a_zs", [128, DC, 128], F32, ph)
        S.ln_mean = P.sb("ln_mean3", [128, 128], F32, ph)
        S.ln_rstd = P.sb("ln_rstd3", [128, 128], F32, ph)
        S.ln_tmp = P.sb("ln_tmp3", [128, 128], F32, ph)
        S.obank = cx.ps[5 + par]
        S.ln_banks = (S.obank[:, 0:128], S.obank[:, 128:256])
        cx.a_sets.append(S)
    cx.bank_i = 0

    def nextbank():
        b = cx.ps[cx.bank_i % 4]
        cx.bank_i += 1
        return b

    cx.nextbank = nextbank


def rope_tm(P, src, dst, nh, ropeb, tA, tB):
    sv = V(src.t, src.ap.rearrange("p (h two f) -> p h two f", h=nh, two=2))
    dv = V(dst.t, dst.ap.rearrange("p (h two f) -> p h two f", h=nh, two=2))
    c = bcast_mid(ropeb[:, 0:32], nh)
    s = bcast_mid(ropeb[:, 32:64], nh)
    a = v3(tA[:, 0:nh * 32], nh)
    b = v3(tB[:, 0:nh * 32], nh)
    P.tt("dve", a, sv[:, :, 0, :], c, ALU.mult)
    P.tt("pool", b, sv[:, :, 1, :], s, ALU.mult)
    yield
    P.tt("dve", dv[:, :, 0, :], a, b, ALU.subtract)
    P.tt("dve", a, sv[:, :, 1, :], c, ALU.mult)
    P.tt("pool", b, sv[:, :, 0, :], s, ALU.mult)
    yield
    P.tt("dve", dv[:, :, 1, :], a, b, ALU.add)


def attn_block(P, cx, n, hin, hout):
    A = cx.aw
    nb = cx.nextbank
    ident = cx.ident
    colp = cx.colp
    S = cx.a_sets[n % 3]
    t0 = n * 128
    hin_v = hin.ap.rearrange("(c p) t -> p c t", p=128)
    hout_v = hout.ap.rearrange("(c p) t -> p c t", p=128)
    x, xb, ropeb = S.x, S.xb, S.rope
    P.dma("sp", x[:, :, :], hin.v(hin_v[:, :, t0:t0 + 128]))
    P.dma("sp", ropeb[:, :], cx.rope_in[t0:t0 + 128, :])
    P.copy("act", xb[:, :, :], x[:, :, :])
    cur, prev = n % 4, (n + 3) % 4
    yield
    q32, qr, k32, kr = S.q32, S.qr, S.k32, S.kr
    for half in range(2):
        pb = nb()
        for c in range(DC):
            P.mm(pb[:, :], xb[:, c, :], A["wq"][:, c, half * 512:(half + 1) * 512], start=(c == 0), stop=(c == DC - 1))
        P.copy("act", q32[:, half * 512:(half + 1) * 512], pb[:, :])
    yield
    pb = nb()
    for c in range(DC):
        P.mm(pb[:, 0:256], xb[:, c, :], A["wk"][:, c, :], start=(c == 0), stop=(c == DC - 1))
    P.copy("act", k32[:, :], pb[:, 0:256])
    pb = nb()
    for c in range(DC):
        P.mm(pb[:, 0:256], xb[:, c, :], A["wv"][:, c, :], start=(c == 0), stop=(c == DC - 1))
    vb = cx.a_vb
    P.copy("act", vb[:, cur, :], pb[:, 0:256])
    yield
    yield from rope_tm(P, q32[:, :], qr[:, :], 16, ropeb, S.tA, S.tB)
    yield
    yield from rope_tm(P, k32[:, :], kr[:, 0, :], 4, ropeb, S.tA, S.tB)
    krn = V(kr, kr.ap[:, 0, :].rearrange("p (pr e f) -> p pr e f", pr=2, e=2))
    krs = V(kr, kr.ap[:, 1, :].rearrange("p (pr e f) -> p pr e f", pr=2, e=2))
    P.copy("act", krs[:, :, 0, :], krn[:, :, 1, :])
    P.copy("act", krs[:, :, 1, :], krn[:, :, 0, :])
    yield
    qT = S.qT
    for half in range(2):
        pb = nb()
        for c4 in range(4):
            p = half * 4 + c4
            P.tr(pb[:, c4 * 128:(c4 + 1) * 128], qr[:, p * 128:(p + 1) * 128], ident[:, :])
        P.copy("act" if half == 0 else "dve", qT[:, half * 4:(half + 1) * 4, :], v3(pb[:, :], 4))
    kT = cx.a_kT
    pb = nb()
    for j in range(4):
        P.tr(pb[:, j * 128:(j + 1) * 128], kr[:, j // 2, (j % 2) * 128:(j % 2 + 1) * 128], ident[:, :])
    pb4 = v3(pb[:, :], 4)
    kTv = V(kT, kT.ap.rearrange("p (pr g2) e s t -> p pr g2 e s t", pr=2, g2=2))
    P.copy("act", kTv[0:64, :, 0, 0, cur, :], pb4[0:64, 0:2, :])
    P.copy("dve", kTv[64:128, :, 1, 1, cur, :], pb4[64:128, 0:2, :])
    P.copy("act", kTv[0:64, :, 1, 0, cur, :], pb4[0:64, 2:4, :])
    P.copy("dve", kTv[64:128, :, 0, 1, cur, :], pb4[64:128, 2:4, :])
    yield
    mv = 1 if n == 0 else (2 if n == 1 else 0)
    mask = cx.amask[:, mv, :]
    mx, negm, rs, es = S.mx, S.negm, S.rs, S.es
    for hp in range(8):
        pb = nb()
        for i in range(2):
            h = hp * 2 + i
            g, e, p = h // 4, h % 2, h // 2
            P.mm(pb[:, i * 256:i * 256 + 128], qT[:, p, :], kT[:, g, e, prev, :])
            P.mm(pb[:, i * 256 + 128:(i + 1) * 256], qT[:, p, :], kT[:, g, e, cur, :])
        sm = S.sm[hp]
        P.stt(sm[:, :, :], v3(pb[:, :], 2), 0.125, bcast_mid(mask, 2), ALU.mult, ALU.add)
        P.reduce(mx[:, hp * 2:hp * 2 + 2], sm[:, :, :], ALU.max)
        if hp % 2 == 1:
            yield
    P.tt("dve", mx[:, :], mx[:, :], cx.sinkb[:, :], ALU.max)
    P.ts("dve", negm[:, :], mx[:, :], -1.0, ALU.mult)
    P.tt("dve", es[:, :], cx.sinkb[:, :], mx[:, :], ALU.subtract)
    P.act(es[:, :], es[:, :], AF.Exp)
    for hp in range(8):
        sm = S.sm[hp]
        for i in range(2):
            h = hp * 2 + i
            P.act(sm[:, i, :], sm[:, i, :], AF.Exp, bias=negm[:, h:h + 1], scale=1.0, accum_out=rs[:, h:h + 1])
        if hp % 4 == 3:
            yield
    P.tt("dve", rs[:, :], rs[:, :], es[:, :], ALU.add)
    P.recip(rs[:, :], rs[:, :])
    o32 = S.o32
    ob = S.obank

    def ptrans(hp):
        sm = S.sm[hp]
        pb = nb()
        for i in range(2):
            for j in range(2):
                P.tr(pb[:, (i * 2 + j) * 128:(i * 2 + j + 1) * 128], sm[:, i, j * 128:(j + 1) * 128], ident[:, :])
        pT = S.pT[hp % 2]
        P.copy("act" if hp % 2 == 0 else "dve", pT[:, :, :], v3(pb[:, :], 4))

    ptrans(0)
    for hp in range(8):
        if hp + 1 < 8:
            ptrans(hp + 1)
        pT = S.pT[hp % 2]
        for i in range(2):
            h = hp * 2 + i
            g = h // 4
            oc = slice((h % 8) * 64, (h % 8 + 1) * 64)
            P.mm(ob[:, oc], pT[:, i * 2 + 0, :], vb[:, prev, g * 64:(g + 1) * 64], start=True, stop=False)
            P.mm(ob[:, oc], pT[:, i * 2 + 1, :], vb[:, cur, g * 64:(g + 1) * 64], start=False, stop=True)
        if hp % 2 == 1:
            yield
        if hp % 4 == 3:
            half = hp // 4
            P.tt("dve", v3(o32[:, half * 512:(half + 1) * 512], 8), v3(ob[:, :], 8),
                 bcast_last(rs[:, half * 8:(half + 1) * 8], 64), ALU.mult)
            yield
    yield
    ofm = S.ofm
    for half in range(2):
        pb = nb()
        for c4 in range(4):
            c = half * 4 + c4
            P.tr(pb[:, c4 * 128:(c4 + 1) * 128], o32[:, c * 128:(c + 1) * 128], ident[:, :])
        P.copy("act" if half == 0 else "dve", ofm[:, half * 4:(half + 1) * 4, :], v3(pb[:, :], 4))
    yield
    z = S.z
    for half in range(2):
        pb = nb()
        for c4 in range(4):
            oc = half * 4 + c4
            for kc in range(DC):
                P.mm(pb[:, c4 * 128:(c4 + 1) * 128], A["wo"][:, kc, oc * 128:(oc + 1) * 128], ofm[:, kc, :],
                     start=(kc == 0), stop=(kc == DC - 1))
        P.stt(z[:, half * 4:(half + 1) * 4, :], x[:, half * 4:(half + 1) * 4, :], ALPHA,
              v3(pb[:, :], 4), ALU.mult, ALU.add)
        yield
    ho = S.ho
    yield from layer_norm_fm(P, cx, z, ho, 128, colp[:, COL_LNG + 16:COL_LNG + 24],
                             colp[:, COL_LNB + 16:COL_LNB + 24], S)
    P.dma("pool", hout.v(hout_v[:, :, t0:t0 + 128]), ho[:, :, :])
    yield


def run_pipelined(gens, lag, depth=2, bg=None, bg_every=1):
    it = iter(gens)
    active = []

    def start():
        g = next(it, None)
        if g is not None:
            active.append([g, 0])
            return True
        return False

    start()
    rnd = 0
    while active:
        for ent in list(active):
            try:
                next(ent[0])
                ent[1] += 1
            except StopIteration:
                active.remove(ent)
        if len(active) < depth and (not active or active[-1][1] >= lag):
            start()
        rnd += 1
        if bg is not None and rnd % bg_every == 0:
            next(bg, None)
    if bg is not None:
        for _ in bg:
            pass


def rwkv_front_setup(P, cx, ph, wbf, rowp_in):
    R = {}

    def wload(key, name, rows, cols):
        if rows % 128 == 0 and rows > 128:
            t = P.sb("rw_" + key, [128, rows // 128, cols], BF16, ph)
            P.dma("sp", t[:, :, :], wbf[name].v(wbf[name].ap.rearrange("(kc p) o -> p kc o", p=128)))
        else:
            t = P.sb("rw_" + key, [rows, cols], BF16, ph)
            P.dma("sp", t[:, :], wbf[name][:, :])
        R[key] = t

    wload("wr", "a_w_r", 1024, 1024)
    wload("wk", "a_w_k", 1024, 1024)
    wload("wv", "a_w_v", 1024, 1024)
    for key, name in (("w1", "a_w1"), ("a1", "a_a1")):
        t = P.sb("rw_" + key, [128, 8, 128], BF16, ph)
        P.memset("pool", t[:, :, :], 0.0)
        P.dma("sp", t[:, :, 0:64], wbf[name].v(wbf[name].ap.rearrange("(kc p) o -> p kc o", p=128)))
        R[key] = t
    wload("g1", "a_g1", 1024, 128)
    for key, name in (("w2", "a_w2"), ("a2", "a_a2")):
        t = P.sb("rw_" + key, [128, 1024], BF16, ph)
        P.memset("pool", t[:, :], 0.0)
        P.dma("sp", t[0:64, :], wbf[name][:, :])
        R[key] = t
    wload("g2", "a_g2", 128, 1024)
    cx.rw = R
    cx.rowb = P.sb("rowb", [128, 3, D], F32, ph)
    for j, ri in enumerate([ROW_KK, ROW_KA, ROW_RK]):
        src = rowp_in.ap[ri:ri + 1, :]
        P.dma("sp", cx.rowb[:, j, :], rowp_in.v(bass.AP(src.tensor, src.offset, [[0, 128], [1, D]])))
    cx.brow = P.sb("brow", [128, 2, D], F32, ph)
    P.memset("pool", cx.brow[:, :, :], 0.0)
    P.dma("sp", cx.brow[0:1, 0, :], rowp_in[ROW_W0:ROW_W0 + 1, :])
    P.dma("sp", cx.brow[0:1, 1, :], rowp_in[ROW_A0:ROW_A0 + 1, :])
    cx.ones_row = P.sb("ones_row", [128, 128], F32, ph)
    P.memset("dve", cx.ones_row[:, :], 0.0)
    P.memset("dve", cx.ones_row[0:1, :], 1.0)
    cx.ones2 = P.sb("ones2", [128, 2], F32, ph)
    P.memset("dve", cx.ones2[:, :], 1.0)
    P.memset("dve", cx.ones2[64:128, 1:2], 0.0)
    cx.fsets = []
    for par in range(2):
        S = Ctx()
        S.xfm = P.sb("f_xfm", [128, DC, 129], F32, ph)
        S.mixb = [P.sb("f_mixb", [128, DC, 128], BF16, ph) for i in range(2)]
        S.mix_i = 0
        S.hw = P.sb("f_hw", [128, 128], BF16, ph)
        S.ha = P.sb("f_ha", [128, 128], BF16, ph)
        S.hg = P.sb("f_hg", [128, 128], BF16, ph)
        S.cm = P.sb("f_cm", [128, 8, 4, 128], BF16, ph)
        S.tmb = P.sb("f_tmb", [128, 4, D], BF16, ph)
        S.sm = P.sb("f_sm", [128, 32], F32, ph)
        S.ss = P.sb("f_ss", [128, 16], F32, ph)
        S.rn = P.sb("f_rn", [128, 16], F32, ph)
        cx.fsets.append(S)
    cx.tmpool = Pool_([P.sb("tm%d" % i, [128, D], F32, ph) for i in range(cx.front_pool)])
    cx.bank_i = 0

    def nextbank():
        b = cx.ps[cx.bank_i % 8]
        cx.bank_i += 1
        return b

    cx.nextbank = nextbank


def rwkv_front_gen(P, cx, n, h0):
    R = cx.rw
    ident = cx.ident
    tp = cx.tmpool
    colp = cx.colp
    cm_ = cx.cmat
    nb = cx.nextbank
    S = cx.fsets[n % 2]
    t0 = n * 128
    h0v = h0.ap.rearrange("(c p) t -> p c t", p=128)
    xf = S.xfm
    if n == 0:
        P.memset("pool", xf[:, :, 0:1], 0.0)
        P.dma("sp", xf[:, :, 1:129], h0.v(h0v[:, :, 0:128]))
    else:
        P.dma("sp", xf[:, :, 0:129], h0.v(h0v[:, :, t0 - 1:t0 + 128]))
    xxb = tp.get()
    xx = v3(xxb[:, :], 8)
    P.tt("dve", xx[:, :, :], xf[:, :, 0:128], xf[:, :, 1:129], ALU.subtract)
    yield
    rowb = cx.rowb
    blk = slice(n * 128, (n + 1) * 128)
    st = {}

    def mix(i):
        m = S.mixb[S.mix_i % 2]
        S.mix_i += 1
        tmp = tp.get()
        tv = v3(tmp[:, :], 8)
        P.tt("dve", tv, xx[:, :, :], bcast_last(colp[:, COL_MU + i * 8:COL_MU + i * 8 + 8], 128), ALU.mult)
        P.tt("dve", m[:, :, :], tv, xf[:, :, 1:129], ALU.add)
        tp.put(tmp)
        return m

    def proj_tm(m, W):
        banks = [nb(), nb()]
        for half in range(2):
            for c in range(DC):
                P.mm(banks[half][:, :], m[:, c, :], W[:, c, half * 512:(half + 1) * 512],
                     start=(c == 0), stop=(c == DC - 1))
        return banks

    def evac2(banks, dst, engs):
        for half in range(2):
            P.copy(engs[half], dst[:, half * 512:(half + 1) * 512], banks[half][:, :])

    def chain_x():
        r32 = tp.get()
        evac2(proj_tm(mix(0), R["wr"]), r32, ["act", "act"])
        yield
        k32 = tp.get()
        evac2(proj_tm(mix(2), R["wk"]), k32, ["act", "act"])
        yield
        v32 = tp.get()
        evac2(proj_tm(mix(3), R["wv"]), v32, ["act", "act"])
        P.copy("act", S.tmb[:, 3, :], v32[:, :])
        P.dma("pool", cx.tmf_scr[blk, 0:D], v32[:, :])
        tp.put(v32)
        yield
        kk = tp.get()
        P.tt("dve", kk[:, :], k32[:, :], rowb[:, 0, :], ALU.mult)
        sq = tp.get()
        P.act(sq[:, :], kk[:, :], AF.Square)
        yield
        ss, rn = S.ss, S.rn
        P.reduce(ss[:, :], v3(sq[:, :], 16), ALU.add)
        P.ts("dve", ss[:, :], ss[:, :], 1e-24, ALU.max)
        tp.put(sq)
        yield
        P.act(ss[:, :], ss[:, :], AF.Sqrt)
        yield
        P.recip(rn[:, :], ss[:, :])
        P.tt("dve", v3(kk[:, :], 16), v3(kk[:, :], 16), bcast_last(rn[:, :], 64), ALU.mult)
        st["r32"], st["k32"], st["kk"] = r32, k32, kk
        yield

    def lora1(i, W1, func, dst):
        m = mix(i)
        b = nb()
        for c in range(DC):
            P.mm(b[:, 0:128], W1[:, c, :], m[:, c, :], start=(c == 0), stop=(c == DC - 1))
        P.act(dst[:, :], b[:, 0:128], func)

    def lora2(hsrc, W2, biasrow):
        banks = [nb(), nb()]
        for half in range(2):
            sl = slice(half * 512, (half + 1) * 512)
            if biasrow is not None:
                P.mm(banks[half][:, :], cx.ones_row[:, :], cx.brow[:, biasrow, sl], start=True, stop=False)
            P.mm(banks[half][:, :], hsrc[:, :], W2[:, sl], start=(biasrow is None), stop=True)
        return banks

    def chain_y():
        lora1(1, R["w1"], AF.Tanh, S.hw)
        yield
        lora1(4, R["a1"], AF.Identity, S.ha)
        yield
        lora1(5, R["g1"], AF.Sigmoid, S.hg)
        sg = tp.get()
        bw = lora2(S.hw, R["w2"], 0)
        for half in range(2):
            P.act(sg[:, half * 512:(half + 1) * 512], bw[half][:, :], AF.Sigmoid)
        yield
        alr = tp.get()
        ba = lora2(S.ha, R["a2"], 1)
        for half in range(2):
            P.act(alr[:, half * 512:(half + 1) * 512], ba[half][:, :], AF.Sigmoid)
        g32 = tp.get()
        evac2(lora2(S.hg, R["g2"], None), g32, ["act", "act"])
        P.dma("pool", cx.tmf_scr[blk, D:2 * D], g32[:, :])
        tp.put(g32)
        yield
        M1 = cm_[:, CM_M1 * 128:(CM_M1 + 1) * 128]
        M2 = cm_[:, CM_M2 * 128:(CM_M2 + 1) * 128]
        eP, eM, eR, eS = tp.get(), tp.get(), tp.get(), tp.get()
        for half in range(2):
            sl = slice(half * 512, (half + 1) * 512)
            bD = nb()
            P.mm(bD[:, :], M1, sg[:, sl])
            P.act(eP[:, sl], bD[:, :], AF.Exp, scale=-C0)
            P.act(eM[:, sl], bD[:, :], AF.Exp, scale=C0)
            bR = nb()
            P.mm(bR[:, :], M2, sg[:, sl])
            P.act(eR[:, sl], bR[:, :], AF.Exp, scale=-C0)
        P.act(eS[:, :], sg[:, :], AF.Exp, scale=C0)
        bT = nb()
        for c in range(DC):
            P.mm(bT[:, 2 * c:2 * c + 2], sg[:, c * 128:(c + 1) * 128], cx.ones2[:, 0:2])
        bTv = v3(bT[:, 0:16], 8)
        P.act(S.sm[:, 16:24], bTv[:, :, 0], AF.Exp, scale=-C0)
        P.act(S.sm[:, 24:32], bTv[:, :, 1], AF.Exp, scale=-C0)
        tp.put(sg)
        yield
        P.tt("dve", eS[:, :], eS[:, :], eP[:, :], ALU.mult)
        st["alr"], st["eP"], st["eM"], st["eR"], st["eA"] = alr, eP, eM, eR, eS
        yield

    gx, gy = chain_x(), chain_y()
    alive = [gx, gy]
    while alive:
        for g in list(alive):
            try:
                next(g)
            except StopIteration:
                alive.remove(g)
        yield
    tp.put(xxb)
    r32, k32, kk = st["r32"], st["k32"], st["kk"]
    alr, eP, eM, eR, eA = st["alr"], st["eP"], st["eM"], st["eR"], st["eA"]
    kmod = tp.get()
    P.stt(kmod[:, :], alr[:, :], -1.0, rowb[:, 1, :], ALU.add, ALU.mult)
    P.stt(kmod[:, :], kmod[:, :], 1.0, k32[:, :], ALU.add, ALU.mult)
    tp.put(k32)
    bb = tp.get()
    P.tt("dve", bb[:, :], kk[:, :], alr[:, :], ALU.mult)
    tp.put(alr)
    yield
    sq = tp.get()
    P.tt("pool", sq[:, :], r32[:, :], kmod[:, :], ALU.mult)
    P.tt("dve", eP[:, :], eP[:, :], r32[:, :], ALU.mult)
    rt = eP
    tp.put(r32)
    yield
    P.tt("dve", sq[:, :], sq[:, :], rowb[:, 2, :], ALU.mult)
    P.reduce(S.sm[:, 0:16], v3(sq[:, :], 16), ALU.add)
    tp.put(sq)
    bt = tp.get()
    P.tt("dve", bt[:, :], bb[:, :], eM[:, :], ALU.mult)
    P.tt("dve", eM[:, :], eM[:, :], kmod[:, :], ALU.mult)
    kt = eM
    P.stt(eA[:, :], kk[:, :], -1.0, eA[:, :], ALU.mult, ALU.mult)
    at = eA
    tp.put(kk)
    P.tt("pool", S.tmb[:, 1, :], kmod[:, :], eR[:, :], ALU.mult)
    P.tt("pool", S.tmb[:, 2, :], bb[:, :], eR[:, :], ALU.mult)
    tp.put(eR, kmod, bb)
    yield
    P.copy("act", S.tmb[:, 0, :], at[:, :])
    P.dma("pool", cx.sm_scr[blk, :], S.sm[:, :])
    cm = S.cm
    for kind, src in enumerate([bt, kt, at, rt]):
        for half in range(2):
            pb = nb()
            for c4 in range(4):
                p = half * 4 + c4
                P.tr(pb[:, c4 * 128:(c4 + 1) * 128], src[:, p * 128:(p + 1) * 128], ident[:, :])
            P.copy("act" if half == 0 else "dve", cm[:, half * 4:(half + 1) * 4, kind, :], v3(pb[:, :], 4))
        if kind % 2 == 1:
            yield
    tp.put(bt, kt, at, rt)
    P.dma("pool", cx.tmb_scr[blk, :], S.tmb.v(S.tmb.ap.rearrange("p a d -> p (a d)")))
    P.dma("pool", cx.cm_scr[blk, :], cm.v(cm.ap.rearrange("p a k t -> p (a k t)")))
    yield


def rwkv_back_setup(P, cx, ph, rowp_in):
    cx.rowb2 = P.sb("rowb2", [128, 2, D], F32, ph)
    for j, ri in enumerate([ROW_GNW, ROW_GNB]):
        src = rowp_in.ap[ri:ri + 1, :]
        P.dma("sp", cx.rowb2[:, j, :], rowp_in.v(bass.AP(src.tensor, src.offset, [[0, 128], [1, D]])))
    cx.identb = P.sb("identb", [128, 128], BF16, ph)
    P.copy("dve", cx.identb[:, :], cx.ident[:, :])
    cx.ST = P.sb("ST", [128, 8, 64], F32, ph)
    P.memset("dve", cx.ST[:, :, :], 0.0)
    cx.bk_sets = []
    for i in range(3):
        S = Ctx()
        S.cmz = [P.sb("b_cmz", [128, 8, 4, 128], BF16, ph) for e in range(2)]
        S.STz = [P.sb("b_STz", [128, 8, 64], BF16, ph) for e in range(2)]
        for e in range(2):
            P.memset("pool", S.cmz[e][:, :, :, :], 0.0)
            P.memset("pool", S.STz[e][:, :, :], 0.0)
        S.tmb = P.sb("b_tmb", [128, 4, D], BF16, ph)
        S.tmf = P.sb("b_tmf", [128, 2, D], F32, ph)
        S.sm = P.sb("b_sm", [128, 32], F32, ph)
        S.y32 = P.sb("b_y32", [128, D], F32, ph)
        S.sq = P.sb("b_sq", [128, D], F32, ph)
        S.sq2 = P.sb("b_sq2", [128, D], F32, ph)
        S.ysum = P.sb("b_ysum", [128, 16], F32, ph)
        S.yv = P.sb("b_yv", [128, 16], F32, ph)
        S.ofm = P.sb("b_ofm", [128, DC, 128], BF16, ph)
        cx.bk_sets.append(S)
    cx.bsets = []
    for i in range(cx.back_depth):
        BS = Ctx()
        BS.X = P.sb("Xb", [128, 4, 128], BF16, ph)
        BS.XT = P.sb("XTb", [128, 4, 128], BF16, ph)
        BS.Pm = P.sb("Pb", [128, 4, 128], BF16, ph)
        BS.AakT = P.sb("AakT", [128, 4, 128], BF16, ph)
        BS.Arb = P.sb("Arb", [128, 4, 128], BF16, ph)
        BS.Ark = P.sb("Ark", [128, 4, 128], BF16, ph)
        BS.Wt = P.sb("Wt", [128, 2, 128], BF16, ph)
        BS.M2m = P.sb("M2m", [128, 4, 128], BF16, ph)
        BS.Ubf = P.sb("Ubf", [128, 256], BF16, ph)
        cx.bsets.append(BS)
    cx.bank_i = 0

    def nextbank():
        b = cx.ps[cx.bank_i % 8]
        cx.bank_i += 1
        return b

    cx.nextbank = nextbank


def rwkv_back_load(P, cx, n):
    S = cx.bk_sets[n % 3]
    blk = slice(n * 128, (n + 1) * 128)
    lo = slice(n * 128, n * 128 + 64)
    hi = slice(n * 128 + 64, (n + 1) * 128)
    cmv = cx.cm_scr.ap.rearrange("r (a k t) -> r a k t", a=8, k=4)
    P.dma("sp", S.cmz[0][0:64, :, :, :], cx.cm_scr.v(cmv[lo]))
    P.dma("sp", S.cmz[1][64:128, :, :, :], cx.cm_scr.v(cmv[hi]))
    P.dma("sp", S.tmb[:, :, :], cx.tmb_scr.v(cx.tmb_scr.ap.rearrange("r (a d) -> r a d", a=4)[blk]))
    P.dma("sp", S.tmf[:, :, :], cx.tmf_scr.v(cx.tmf_scr.ap.rearrange("r (a d) -> r a d", a=2)[blk]))
    P.dma("sp", S.sm[:, :], cx.sm_scr[blk, :])


def rwkv_back_gen(P, cx, n, bt_i, pos, nblk):
    ident = cx.ident
    cm_ = cx.cmat
    nb = cx.nextbank
    S = cx.bk_sets[n % 3]
    BS = cx.bsets[pos % cx.back_depth]
    t0 = n * 128
    if pos == 0:
        for m in range(min(3, nblk)):
            rwkv_back_load(P, cx, m)
        yield
    cmz = S.cmz
    STz = S.STz
    ST = cx.ST
    atbf, khat, bhat, vbf = S.tmb[:, 0, :], S.tmb[:, 1, :], S.tmb[:, 2, :], S.tmb[:, 3, :]
    PC, PM = S.sm[:, 16:24], S.sm[:, 24:32]
    y32 = S.y32
    MU = cm_[:, CM_MU * 128:(CM_MU + 1) * 128]
    MUE = cm_[:, CM_MUE * 128:(CM_MUE + 1) * 128]
    ML = cm_[:, CM_ML * 128:(CM_ML + 1) * 128]
    identb = cx.identb
    heads = [4 * bt_i + i for i in range(4)]

    def hsl(h):
        return h % 2, h // 2

    def gram(kl, kr, mask, dst, eng):
        pb = nb()
        for i, h in enumerate(heads):
            e, p = hsl(h)
            P.mm(pb[:, i * 128:(i + 1) * 128], cmz[e][:, p, kl, :], cmz[e][:, p, kr, :])
        P.tt(eng, dst[:, :, :], v3(pb[:, :], 4), bcast_mid(mask, 4), ALU.mult)

    X, XT, Pm = BS.X, BS.XT, BS.Pm
    gram(0, 2, MU, X, "dve")
    gram(2, 0, ML, XT, "dve")
    yield
    AakT, Arb, Ark = BS.AakT, BS.Arb, BS.Ark
    gram(2, 1, ML, AakT, "dve")
    gram(0, 3, MUE, Arb, "dve")
    gram(1, 3, MUE, Ark, "dve")
    yield
    P.tt("dve", Pm[:, :, :], X[:, :, :], bcast_mid(identb[:, :], 4), ALU.add)
    for k in range(1, 7):
        pbx = None
        if k <= 5:
            pbx = nb()
            for i in range(4):
                P.mm(pbx[:, i * 128:(i + 1) * 128], XT[:, i, :], X[:, i, :])
        pbt = nb()
        for i in range(4):
            P.mm(pbt[:, i * 128:(i + 1) * 128], X[:, i, :], XT[:, i, :])
        if pbx is not None:
            P.copy("act", X[:, :, :], v3(pbx[:, :], 4))
        P.copy("act", XT[:, :, :], v3(pbt[:, :], 4))
        yield
        pb = nb()
        for i in range(4):
            P.mm(pb[:, i * 128:(i + 1) * 128], XT[:, i, :], Pm[:, i, :])
        P.tt("dve", Pm[:, :, :], v3(pb[:, :], 4), Pm[:, :, :], ALU.add)
        yield
    Tm = Pm
    Wt, M2m, Ubf = BS.Wt, BS.M2m, BS.Ubf
    pb = nb()
    for i, h in enumerate(heads):
        e, p = hsl(h)
        P.mm(pb[:, i * 128:(i + 1) * 128], atbf[:, p * 128:(p + 1) * 128], Tm[:, i, :])
    pb4 = v3(pb[:, :], 4)
    P.copy("act", Wt[0:64, 0, :], pb4[0:64, 0, :])
    P.copy("act", Wt[0:64, 1, :], pb4[0:64, 2, :])
    P.copy("dve", Wt[64:128, 0, :], pb4[64:128, 1, :])
    P.copy("dve", Wt[64:128, 1, :], pb4[64:128, 3, :])
    pb = nb()
    for i in range(4):
        P.mm(pb[:, i * 128:(i + 1) * 128], AakT[:, i, :], Tm[:, i, :])
    P.copy("act", M2m[:, :, :], v3(pb[:, :], 4))
    pr = slice(2 * bt_i, 2 * bt_i + 2)
    P.tt("pool", STz[0][0:64, pr, :], ST[0:64, pr, :], bcast_last(PM[0:64, pr], 64), ALU.mult)
    P.tt("pool", STz[1][64:128, pr, :], ST[64:128, pr, :], bcast_last(PM[64:128, pr], 64), ALU.mult)
    yield
    pb = nb()
    for i, h in enumerate(heads):
        e, p = hsl(h)
        hs = slice(h * 64, (h + 1) * 64)
        P.mm(pb[:, i * 64:(i + 1) * 64], Wt[:, i // 2, :], STz[e][:, p, :], start=True, stop=False)
        P.mm(pb[:, i * 64:(i + 1) * 64], M2m[:, i, :], vbf[:, hs], start=False, stop=True)
    P.copy("act", Ubf[:, :], pb[:, 0:256])
    yield
    pb = nb()
    for i, h in enumerate(heads):
        e, p = hsl(h)
        hs = slice(h * 64, (h + 1) * 64)
        P.mm(pb[:, i * 64:(i + 1) * 64], cmz[e][:, p, 3, :], STz[e][:, p, :], start=True, stop=False)
        P.mm(pb[:, i * 64:(i + 1) * 64], Arb[:, i, :], Ubf[:, i * 64:(i + 1) * 64], start=False, stop=False)
        P.mm(pb[:, i * 64:(i + 1) * 64], Ark[:, i, :], vbf[:, hs], start=False, stop=True)
    P.copy("act", y32[:, bt_i * 256:(bt_i + 1) * 256], pb[:, 0:256])
    yield
    pb = nb()
    for i, h in enumerate(heads):
        e, p = hsl(h)
        hs = slice(h * 64, (h + 1) * 64)
        ps_ = slice(p * 128, (p + 1) * 128)
        P.mm(pb[:, i * 64:(i + 1) * 64], bhat[:, ps_], Ubf[:, i * 64:(i + 1) * 64], start=True, stop=False)
        P.mm(pb[:, i * 64:(i + 1) * 64], khat[:, ps_], vbf[:, hs], start=False, stop=True)
    for i, h in enumerate(heads):
        e, p = hsl(h)
        o = slice(64 * e, 64 * e + 64)
        P.stt(ST[o, p, :], ST[o, p, :], PC[o, p:p + 1], pb[o, i * 64:(i + 1) * 64], ALU.mult, ALU.add)
    yield
    if bt_i != 3:
        return
    rowb2 = cx.rowb2
    v32, g32 = S.tmf[:, 0, :], S.tmf[:, 1, :]
    bon = S.sm[:, 0:16]
    ysum, yv, sq = S.ysum, S.yv, S.sq
    P.reduce(ysum[:, :], v3(y32[:, :], 16), ALU.add)
    P.ts("dve", ysum[:, :], ysum[:, :], -1.0 / 64, ALU.mult)
    P.tt("dve", v3(sq[:, :], 16), v3(v32, 16), bcast_last(bon, 64), ALU.mult)
    yield
    P.tt("pool", v3(y32[:, :], 16), v3(y32[:, :], 16), bcast_last(ysum[:, :], 64), ALU.add)
    yield
    sq2 = S.sq2
    P.act(sq2[:, :], y32[:, :], AF.Square)
    yield
    P.reduce(yv[:, :], v3(sq2[:, :], 16), ALU.add)
    yield
    P.act(yv[:, :], yv[:, :], AF.Sqrt, bias=cx.eps_gn[:, 0:1], scale=1.0 / 64)
    yield
    P.recip(yv[:, :], yv[:, :])
    P.tt("dve", v3(y32[:, :], 16), v3(y32[:, :], 16), bcast_last(yv[:, :], 64), ALU.mult)
    yield
    P.tt("pool", y32[:, :], y32[:, :], rowb2[:, 0, :], ALU.mult)
    P.tt("pool", y32[:, :], y32[:, :], rowb2[:, 1, :], ALU.add)
    P.tt("pool", y32[:, :], y32[:, :], sq[:, :], ALU.add)
    yield
    P.tt("dve", y32[:, :], y32[:, :], g32, ALU.mult)
    yield
    ofm = S.ofm
    for half in range(2):
        pb = nb()
        for c4 in range(4):
            c = half * 4 + c4
            P.tr(pb[:, c4 * 128:(c4 + 1) * 128], y32[:, c * 128:(c + 1) * 128], ident[:, :])
        P.copy("act" if half == 0 else "dve", ofm[:, half * 4:(half + 1) * 4, :], v3(pb[:, :], 4))
    osv = cx.o_scr.ap.rearrange("(c p) t -> p c t", p=128)
    P.dma("pool", cx.o_scr.v(osv[:, :, t0:t0 + 128]), ofm[:, :, :])
    yield
    if n + 3 < nblk:
        rwkv_back_load(P, cx, n + 3)
    yield
```

```python
import contextlib
import numpy as np
import concourse.bass as bass
import concourse.mybir as mybir
from concourse.bass_utils import run_bass_kernel_spmd

F32 = mybir.dt.float32
BF16 = mybir.dt.bfloat16
ALU = mybir.AluOpType
AF = mybir.ActivationFunctionType
AX = mybir.AxisListType

D = 1024
DC = 8
SEQ = 4096
NMETA = 16
NBLK = 33
LP = NBLK * 128
DFF = 4096
ALPHA = (2.0 * 2) ** 0.25
LN_EPS = 1e-5
GN_EPS = 64e-5
C0 = float(np.exp(-0.5))
NDMA_SLOTS = 8


class Op:
    __slots__ = ("eng", "fn", "waits", "needs_inc", "idx", "semval", "dma", "selfwait")

    def __init__(self, eng, fn):
        self.eng = eng
        self.fn = fn
        self.waits = []
        self.needs_inc = False
        self.idx = 0
        self.semval = 0
        self.dma = None
        self.selfwait = None


class T:
    def __init__(self, ap, name=""):
        self.ap = ap
        self.name = name
        self.w = None
        self.r = {}

    def __getitem__(self, k):
        return V(self, self.ap[k])

    def v(self, ap):
        return V(self, ap)


class V:
    def __init__(self, t, ap):
        self.t = t
        self.ap = ap

    def __getitem__(self, k):
        return V(self.t, self.ap[k])


def _ts(vs):
    out = []
    for v in vs:
        if v is None:
            continue
        out.append(v.t if isinstance(v, V) else v)
    return out


class Prog:
    ENGS = ("pe", "act", "dve", "pool", "sp")

    def __init__(self, nc, stack):
        self.nc = nc
        self.stack = stack
        self.streams = {e: [] for e in self.ENGS}
        self.cnt = {}
        self.seen = {e: {} for e in self.ENGS}
        self.sems = {}
        for e in self.ENGS:
            self.sems[("eng", e)] = stack.enter_context(nc.semaphore("s_" + e))
            self.cnt[("eng", e)] = 0
        self.dma_n = {}
        self.dma_last = {}
        for q in ("sp", "pool", "act"):
            self.dma_n[q] = 0
            for s in range(NDMA_SLOTS):
                key = ("dma", q, s)
                self.sems[key] = stack.enter_context(nc.semaphore("d_%s%d" % (q, s)))
                self.dma_last[key] = None
        self.n_ops = 0
        self.all_ts = []
        self.semc = {e: 0 for e in self.ENGS}
        self.last_op = {e: None for e in self.ENGS}

    def _reg(self, t):
        self.all_ts.append(t)
        return t

    def sb(self, name, shape, dt, stack=None):
        st = stack if stack is not None else self.stack
        self.n_names = getattr(self, "n_names", 0) + 1
        name = "%s_%d" % (name, self.n_names)
        return self._reg(T(st.enter_context(self.nc.sbuf_tensor(name, list(shape), dt)), name))

    def ps(self, name, shape, dt):
        return self._reg(T(self.stack.enter_context(self.nc.psum_tensor(name, list(shape), dt)), name))

    def dram(self, name, shape, dt, kind="Internal"):
        return self._reg(T(self.nc.dram_tensor(name, list(shape), dt, kind=kind).ap(), name))

    def barrier(self):
        toks = []
        for e in ("pe", "act", "dve", "pool"):
            lo = self.last_op[e]
            if lo is not None:
                lo.needs_inc = True
                toks.append(lo)
        for key, last in self.dma_last.items():
            if last is not None:
                toks.append(last)
        for e in self.ENGS:
            o = Op(e, None)
            o.waits = [p for p in toks if not (p.dma is None and p.eng == e)]
            self.streams[e].append(o)
            for p in toks:
                key = p.dma if p.dma is not None else ("eng", p.eng)
                if self.seen[e].get(key, 0) < p.idx:
                    self.seen[e][key] = p.idx
        for t in self.all_ts:
            t.w = None
            t.r = {}

    def _collect(self, eng, op, reads, writes):
        deps = {}

        def add(key, idx, pop):
            if key not in deps or deps[key][0] < idx:
                deps[key] = (idx, pop)

        for t in reads:
            if t.w is not None:
                add(*t.w)
        for t in writes:
            if t.w is not None:
                add(*t.w)
            for key, (idx, pop) in t.r.items():
                add(key, idx, pop)
        for key, (idx, pop) in deps.items():
            if key == ("eng", "pe") and eng == "pe":
                continue
            if self.seen[eng].get(key, 0) >= idx:
                continue
            self.seen[eng][key] = idx
            pop.needs_inc = True
            op.waits.append(pop)

    def op(self, eng, fn, reads=(), writes=()):
        reads = _ts(reads)
        writes = _ts(writes)
        o = Op(eng, fn)
        self._collect(eng, o, reads, writes)
        key = ("eng", eng)
        self.cnt[key] += 1
        o.idx = self.cnt[key]
        self.streams[eng].append(o)
        self.last_op[eng] = o
        for t in reads:
            if key not in t.r or t.r[key][0] < o.idx:
                t.r[key] = (o.idx, o)
        for t in writes:
            t.w = (key, o.idx, o)
            t.r = {}
        self.n_ops += 1
        return o

    def dma(self, q, out, in_, **kw):
        reads = _ts([in_])
        writes = _ts([out])
        oap, iap = out.ap, in_.ap
        o = Op(q, lambda e: e.dma_start(out=oap, in_=iap, **kw))
        self._collect(q, o, reads, writes)
        n = self.dma_n[q]
        self.dma_n[q] += 1
        slot = n % NDMA_SLOTS
        rnd = n // NDMA_SLOTS
        key = ("dma", q, slot)
        o.dma = key
        o.idx = rnd + 1
        o.semval = 16 * (rnd + 1)
        o.needs_inc = True
        prev = self.dma_last[key]
        if prev is not None and self.seen[q].get(key, 0) < prev.idx:
            self.seen[q][key] = prev.idx
            o.waits.append(prev)
        self.dma_last[key] = o
        self.streams[q].append(o)
        for t in reads:
            t.r[key] = (o.idx, o)
        for t in writes:
            t.w = (key, o.idx, o)
            t.r = {}
        self.n_ops += 1
        return o

    def finish(self, final_ts):
        o = Op("sp", None)
        for key, last in self.dma_last.items():
            if last is not None:
                o.waits.append(last)
        self.streams["sp"].append(o)

    def emit(self):
        nc = self.nc
        for e in self.ENGS:
            c = self.semc[e]
            for o in self.streams[e]:
                if o.dma is None and o.fn is not None and o.needs_inc:
                    c += 1
                    o.semval = c
            self.semc[e] = c
        engmap = {"pe": "tensor", "act": "scalar", "dve": "vector", "pool": "gpsimd", "sp": "sync"}
        sems = self.sems

        def run_stream(ename, eng):
            waited = {}
            for o in self.streams[ename]:
                for p in o.waits:
                    key = p.dma if p.dma is not None else ("eng", p.eng)
                    val = p.semval
                    if waited.get(key, 0) >= val:
                        continue
                    waited[key] = val
                    eng.wait_ge(sems[key], val)
                if o.fn is None:
                    continue
                ins = o.fn(eng)
                if o.dma is not None:
                    ins.then_inc(sems[o.dma], 16)
                elif o.needs_inc:
                    ins.then_inc(sems[("eng", ename)], 1)

        with nc.Block() as block:
            @block.tensor
            def _(eng):
                run_stream("pe", eng)

            @block.scalar
            def _(eng):
                run_stream("act", eng)

            @block.vector
            def _(eng):
                run_stream("dve", eng)

            @block.gpsimd
            def _(eng):
                run_stream("pool", eng)

            @block.sync
            def _(eng):
                run_stream("sp", eng)
        for e in self.ENGS:
            self.streams[e] = []

    def mm(self, out, lhsT, rhs, start=True, stop=True):
        a, b, c = out.ap, lhsT.ap, rhs.ap
        return self.op("pe", lambda e: e.matmul(a, b, c, start=start, stop=stop),
                       reads=[lhsT, rhs], writes=[out])

    def tr(self, out, in_, ident):
        a, b, c = out.ap, in_.ap, ident.ap
        return self.op("pe", lambda e: e.transpose(a, b, c), reads=[in_, ident], writes=[out])

    def act(self, out, in_, func, bias=None, scale=None, accum_out=None, eng="act"):
        kw = {}
        rd = [in_]
        wr = [out]
        if bias is not None:
            if isinstance(bias, V):
                kw["bias"] = bias.ap
                rd.append(bias)
            else:
                kw["bias"] = bias
        if scale is not None:
            if isinstance(scale, V):
                kw["scale"] = scale.ap
                rd.append(scale)
            else:
                kw["scale"] = scale
        if accum_out is not None:
            kw["accum_out"] = accum_out.ap
            wr.append(accum_out)
        a, b = out.ap, in_.ap
        return self.op("act", lambda e: e.activation(a, b, func, **kw), reads=rd, writes=wr)

    def tt(self, eng, out, in0, in1, op):
        a, b, c = out.ap, in0.ap, in1.ap
        return self.op(eng, lambda e: e.tensor_tensor(a, b, c, op), reads=[in0, in1], writes=[out])

    def ts(self, eng, out, in0, s1, op0, s2=None, op1=None, accum_out=None):
        rd = [in0]
        wr = [out]
        a, b = out.ap, in0.ap
        if isinstance(s1, V):
            rd.append(s1)
            s1 = s1.ap
        if isinstance(s2, V):
            rd.append(s2)
            s2 = s2.ap
        kw = {}
        if op1 is not None:
            kw["op1"] = op1
        if accum_out is not None:
            kw["accum_out"] = accum_out.ap
            wr.append(accum_out)
        return self.op(eng, lambda e: e.tensor_scalar(a, b, s1, s2, op0, **kw), reads=rd, writes=wr)

    def stt(self, out, in0, scalar, in1, op0, op1, eng="dve"):
        rd = [in0, in1]
        a, b, c = out.ap, in0.ap, in1.ap
        if isinstance(scalar, V):
            rd.append(scalar)
            scalar = scalar.ap
        return self.op(eng, lambda e: e.scalar_tensor_tensor(a, b, scalar, c, op0, op1),
                       reads=rd, writes=[out])

    def copy(self, eng, out, in_):
        a, b = out.ap, in_.ap
        if eng == "act":
            return self.op("act", lambda e: e.copy(a, b), reads=[in_], writes=[out])
        return self.op(eng, lambda e: e.tensor_copy(a, b), reads=[in_], writes=[out])

    def reduce(self, out, in_, op, axis=None, eng="dve"):
        a, b = out.ap, in_.ap
        ax = axis if axis is not None else AX.X
        return self.op(eng, lambda e: e.tensor_reduce(a, b, ax, op), reads=[in_], writes=[out])

    def recip(self, out, in_):
        a, b = out.ap, in_.ap
        return self.op("dve", lambda e: e.reciprocal(a, b), reads=[in_], writes=[out])

    def memset(self, eng, out, val):
        a = out.ap
        return self.op(eng, lambda e: e.memset(a, val), reads=[], writes=[out])


def bcast_last(v, n):
    ap = v.ap
    pairs = [list(x) for x in ap.ap]
    new = bass.AP(ap.tensor, ap.offset, pairs + [[0, n]])
    return V(v.t, new)


def bcast_mid(v, n):
    ap = v.ap
    pairs = [list(x) for x in ap.ap]
    new = bass.AP(ap.tensor, ap.offset, [pairs[0], [0, n]] + pairs[1:])
    return V(v.t, new)


class RR:
    def __init__(self, engs):
        self.engs = engs
        self.i = 0

    def __call__(self):
        e = self.engs[self.i % len(self.engs)]
        self.i += 1
        return e


class Ctx:
    pass


def convert_weight_gen(P, cx, dst, src, rows, cols):
    kc_n = rows // 128
    sv = src.ap.rearrange("(kc p) o -> p kc o", p=128)
    dv = dst.ap.rearrange("(kc p) o -> p kc o", p=128)
    UN = 1024
    if cols >= UN:
        steps = [(k, 1, c0, UN) for k in range(kc_n) for c0 in range(0, cols, UN)]
    else:
        g = max(1, UN // cols)
        steps = [(k, min(g, kc_n - k), 0, cols) for k in range(0, kc_n, g)]
    for (k, nk, c0, cw) in steps:
        i = cx.cvg_i
        cx.cvg_i += 1
        st = cx.cvg_f[i % 2]
        sb = cx.cvg_b[i % 2]
        sview = st.v(st.ap[:, 0:nk * cw].rearrange("p (k c) -> p k c", k=nk))
        bview = sb.v(sb.ap[:, 0:nk * cw].rearrange("p (k c) -> p k c", k=nk))
        P.dma("sp", sview, src.v(sv[:, k:k + nk, c0:c0 + cw]))
        yield
        P.copy("act" if i % 2 == 0 else "pool", bview, sview)
        yield
        P.dma("pool", dst.v(dv[:, k:k + nk, c0:c0 + cw]), bview)
        yield


def convert_weight(P, cx, dst, src, rows, cols):
    if rows % 128 == 0:
        kc_n = rows // 128
        sv = src.ap.rearrange("(kc p) o -> p kc o", p=128)
        dv = dst.ap.rearrange("(kc p) o -> p kc o", p=128)
        npart = 128
    else:
        kc_n = 1
        sv = src.ap.rearrange("(kc p) o -> p kc o", p=rows)
        dv = dst.ap.rearrange("(kc p) o -> p kc o", p=rows)
        npart = rows
    UN = 2048
    if cols >= UN:
        steps = [(k, 1, c0, UN) for k in range(kc_n) for c0 in range(0, cols, UN)]
    else:
        g = max(1, UN // cols)
        steps = [(k, min(g, kc_n - k), 0, cols) for k in range(0, kc_n, g)]
    for (k, nk, c0, cw) in steps:
        i = cx.cv_i
        cx.cv_i += 1
        st = cx.cv_f[i % len(cx.cv_f)]
        sb = cx.cv_b[i % len(cx.cv_b)]
        sview = st.v(st.ap[0:npart, 0:nk * cw].rearrange("p (k c) -> p k c", k=nk))
        bview = sb.v(sb.ap[0:npart, 0:nk * cw].rearrange("p (k c) -> p k c", k=nk))
        P.dma("sp", sview, src.v(sv[:, k:k + nk, c0:c0 + cw]))
        eng = cx.cv_rr()
        P.copy(eng, bview, sview)
        P.dma("pool", dst.v(dv[:, k:k + nk, c0:c0 + cw]), bview)


def layer_norm_fm(P, cx, z, hout, Tn, gcol, bcol, ln=None):
    if ln is None:
        ln = cx
    zsq = ln.ln_zsq
    P.act(zsq[:, :, 0:Tn], z, AF.Square)
    pm, pe2 = (cx.ps[6], cx.ps[7]) if not hasattr(ln, "ln_banks") else ln.ln_banks
    for c in range(DC):
        P.mm(pm[:, 0:Tn], cx.onesD[:, :], z[:, c, :], start=(c == 0), stop=(c == DC - 1))
    yield
    for c in range(DC):
        P.mm(pe2[:, 0:Tn], cx.onesD[:, :], zsq[:, c, 0:Tn], start=(c == 0), stop=(c == DC - 1))
    mean = ln.ln_mean
    rstd = ln.ln_rstd
    tmp = ln.ln_tmp
    P.copy("act", mean[:, 0:Tn], pm[:, 0:Tn])
    yield
    P.tt("dve", tmp[:, 0:Tn], mean[:, 0:Tn], mean[:, 0:Tn], ALU.mult)
    P.tt("dve", tmp[:, 0:Tn], pe2[:, 0:Tn], tmp[:, 0:Tn], ALU.subtract)
    yield
    P.act(tmp[:, 0:Tn], tmp[:, 0:Tn], AF.Sqrt, bias=cx.eps_ln[:, 0:1], scale=1.0)
    yield
    P.recip(rstd[:, 0:Tn], tmp[:, 0:Tn])
    P.tt("dve", z, z, bcast_mid(mean[:, 0:Tn], DC), ALU.subtract)
    yield
    P.tt("dve", z, z, bcast_mid(rstd[:, 0:Tn], DC), ALU.mult)
    yield
    for c in range(DC):
        P.act(hout[:, c, :], z[:, c, :], AF.Identity, bias=bcol[:, c:c + 1], scale=gcol[:, c:c + 1])


def mlp_tile(P, cx, ti, t0, Tn, hin, hout, wu, wd, gcol, bcol, final_out, nxt=None):
    hin_v = hin.ap.rearrange("(c p) t -> p c t", p=128)
    hout_v = None if hout is None else hout.ap.rearrange("(c p) t -> p c t", p=128)
    wu_v = wu.ap.rearrange("(kc p) o -> p kc o", p=128)
    wd_v = wd.ap.rearrange("(kc p) o -> p kc o", p=128)
    rr_sq = RR(["dve", "pool"])
    xin = cx.m_xin[ti % 2]
    xb = cx.m_xb[ti % 2]

    def load_in(tj, tt0, tTn):
        xi, xbb = cx.m_xin[tj % 2], cx.m_xb[tj % 2]
        P.dma("sp", xi[:, :, 0:tTn], hin.v(hin_v[:, :, tt0:tt0 + tTn]))
        P.copy("pool", xbb[:, :, 0:tTn], xi[:, :, 0:tTn])

    load_in(ti, t0, Tn)
    hmid = cx.m_hmid
    yield
    for u in range(4):
        wi = cx.w_i
        cx.w_i += 1
        wbuf = cx.wring[wi % len(cx.wring)]
        P.dma("sp", wbuf[:, :, :], wu.v(wu_v[:, :, u * 1024:(u + 1) * 1024]))
        for f8 in range(8):
            f = u * 8 + f8
            pb = cx.ps[f % 4]
            for kc in range(DC):
                P.mm(pb[:, 0:Tn], wbuf[:, kc, f8 * 128:(f8 + 1) * 128], xb[:, kc, 0:Tn],
                     start=(kc == 0), stop=(kc == DC - 1))
            rl = cx.m_relu[f % 2]
            P.act(rl[:, 0:Tn], pb[:, 0:Tn], AF.Relu)
            P.tt(rr_sq(), hmid[:, f, 0:Tn], rl[:, 0:Tn], rl[:, 0:Tn], ALU.mult)
        yield
    yield
    wbufs = []
    for u in range(4):
        wi = cx.w_i
        cx.w_i += 1
        wbuf = cx.wring[wi % len(cx.wring)]
        P.dma("sp", wbuf[:, :, :], wd.v(wd_v[:, u * 8:(u + 1) * 8, :]))
        wbufs.append(wbuf)
    z = cx.m_z
    for oc in range(DC):
        pb = cx.ps[4 + oc % 2]
        for kc in range(32):
            P.mm(pb[:, 0:Tn], wbufs[kc // 8][:, kc % 8, oc * 128:(oc + 1) * 128], hmid[:, kc, 0:Tn],
                 start=(kc == 0), stop=(kc == 31))
        P.stt(z[:, oc, 0:Tn], xin[:, oc, 0:Tn], ALPHA, pb[:, 0:Tn], ALU.mult, ALU.add)
        if oc % 2 == 1:
            yield
    ho = cx.m_ho
    yield from layer_norm_fm(P, cx, z[:, :, 0:Tn], ho[:, :, 0:Tn], Tn, gcol, bcol, None)
    yield
    if final_out is None:
        P.dma("pool", hout.v(hout_v[:, :, t0:t0 + Tn]), ho[:, :, 0:Tn])
    else:
        for sbk in range(Tn // 128):
            tok0 = t0 + sbk * 128
            if tok0 < 128:
                continue
            ot = cx.m_otm[cx.o_i % 2]
            cx.o_i += 1
            for half in range(2):
                pb = cx.ps[4 + half]
                for c4 in range(4):
                    c = half * 4 + c4
                    P.tr(pb[:, c4 * 128:(c4 + 1) * 128], ho[:, c, sbk * 128:(sbk + 1) * 128], cx.ident[:, :])
                P.copy("act" if half == 0 else "dve", ot[:, half * 512:(half + 1) * 512], pb[:, :])
            P.dma("pool", final_out[tok0 - 128:tok0, :], ot[:, :])
            yield


def mlp_phase(P, cx, hin, hout, wu, wd, gcol, bcol, tiles, final_out=None):
    gens = [mlp_tile(P, cx, ti, t0, Tn, hin, hout, wu, wd, gcol, bcol, final_out,
                     nxt=(tiles[ti + 1] if ti + 1 < len(tiles) else None))
            for ti, (t0, Tn) in enumerate(tiles)]
    run_pipelined(gens, lag=cx.mlp_lag, depth=2)


WSPECS = [
    ("a_w_r", 1024, 1024), ("a_w_k", 1024, 1024), ("a_w_v", 1024, 1024), ("a_w_o", 1024, 1024),
    ("a_w1", 1024, 64), ("a_w2", 64, 1024), ("a_a1", 1024, 64), ("a_a2", 64, 1024),
    ("a_g1", 1024, 128), ("a_g2", 128, 1024),
    ("kv_w_k", 1024, 256), ("kv_w_v", 1024, 256), ("b_w_q", 1024, 1024), ("b_w_o", 1024, 1024),
    ("up0", 1024, 4096), ("dn0", 4096, 1024), ("up1", 1024, 4096), ("dn1", 4096, 1024),
]
COL_MU, COL_LNG, COL_LNB, NCOL = 0, 48, 80, 112
ROW_W0, ROW_A0, ROW_KK, ROW_KA, ROW_RK, ROW_GNW, ROW_GNB, NROW = 0, 1, 2, 3, 4, 5, 6, 7
CM_IDENT, CM_ONESD, CM_MU, CM_MUE, CM_ML, CM_M1, CM_M2, NCM = 0, 1, 2, 3, 4, 5, 6, 7


def build_program(cfg):
    nc = bass.Bass("TRN2", target_bir_lowering=False)
    stack = contextlib.ExitStack()
    with stack:
        P = Prog(nc, stack)
        cx = Ctx()
        x_in = P.dram("x", [SEQ, D], F32, kind="ExternalInput")
        meta_in = P.dram("meta", [NMETA, D], F32, kind="ExternalInput")
        win = {}
        wbf = {}
        for (n, r, c) in WSPECS:
            win[n] = P.dram(n, [r, c], F32, kind="ExternalInput")
            wbf[n] = P.dram(n + "_bf", [r, c], BF16)
        colp_in = P.dram("colp", [128, NCOL], F32, kind="ExternalInput")
        rowp_in = P.dram("rowp", [NROW, D], F32, kind="ExternalInput")
        sinks_in = P.dram("sinks", [1, 16], F32, kind="ExternalInput")
        cmat_in = P.dram("cmat", [128, NCM * 128], F32, kind="ExternalInput")
        amask_in = P.dram("amask", [128, 3 * 256], F32, kind="ExternalInput")
        rope_in = P.dram("rope", [LP, 64], F32, kind="ExternalInput")
        out_t = P.dram("out", [SEQ, D], F32, kind="ExternalOutput")
        h_scr = [P.dram("hscr%d" % i, [D, LP], F32,
                        kind=("ExternalOutput" if cfg.get("dbg_h") == i else "Internal")) for i in range(3)]
        cx.o_scr = P.dram("o_scr", [D, LP], BF16)
        if cfg.get("test_hin"):
            h_test = P.dram("h_test", [D, LP], F32, kind="ExternalInput")
        if cfg.get("dbg_out"):
            dbg = P.dram("dbg", [D, LP], F32, kind="ExternalOutput")

        cx.ps = [P.ps("psb%d" % i, [128, 512], F32) for i in range(8)]
        cmat = P.sb("cmat_sb", [128, NCM * 128], F32)
        colp = P.sb("colp_sb", [128, NCOL], F32)
        cx.eps_ln = P.sb("eps_ln", [128, 1], F32)
        cx.eps_gn = P.sb("eps_gn", [128, 1], F32)
        P.dma("sp", cmat[:, :], cmat_in[:, :])
        P.dma("sp", colp[:, :], colp_in[:, :])
        P.memset("dve", cx.eps_ln[:, :], LN_EPS)
        P.memset("dve", cx.eps_gn[:, :], GN_EPS)
        cx.cmat = cmat
        cx.colp = colp
        cx.ident = cmat[:, CM_IDENT * 128:(CM_IDENT + 1) * 128]
        cx.onesD = cmat[:, CM_ONESD * 128:(CM_ONESD + 1) * 128]
        cx.w_i = 0
        cx.o_i = 0
        cx.cv_i = 0
        cx.TM = 384
        cx.c_depth = cfg.get("c_depth", 2)
        cx.mlp_lag = cfg.get("mlp_lag", 11)

        nblk = cfg.get("nblk", NBLK)
        with contextlib.ExitStack() as ph:
            gens0 = []
            if cfg.get("p0", True):
                cx.cv_f = [P.sb("cvf%d" % i, [128, 2048], F32, ph) for i in range(3)]
                cx.cv_b = [P.sb("cvb%d" % i, [128, 2048], BF16, ph) for i in range(3)]
                cx.cv_rr = RR(["dve", "act", "pool"])
                bg_on = cfg.get("bg_conv", True) and cfg.get("p1", True) and cfg.get("p1_split", True) \
                    and "b" in cfg.get("p1_parts", "abc")
                cx.bg_list = []
                for (n, r, c) in WSPECS:
                    if cfg.get("only_w") is not None and n not in cfg["only_w"]:
                        continue
                    if bg_on and not n.startswith("a_"):
                        cx.bg_list.append((n, r, c))
                        continue
                    gens0.append(("w", n, r, c))
            if cfg.get("p0b", True):
                cx.x_tm = [P.sb("x_tm%d" % i, [128, D], F32, ph) for i in range(2)]
                cx.x_fm = [P.sb("x_fm%d" % i, [128, DC, 128], F32, ph) for i in range(2)]

            def g_conv():
                for (_, n_, r_, c_) in gens0:
                    convert_weight(P, cx, wbf[n_], win[n_], r_, c_)
                    yield

            def g_xpose():
                if cfg.get("p0b", True):
                    for nn in range(nblk):
                        xpose_in_phase(P, cx, x_in, meta_in, h_scr[0], nn)
                        yield

            ga, gb = g_conv(), g_xpose()
            alive = [ga, gb]
            while alive:
                for g in list(alive):
                    try:
                        next(g)
                    except StopIteration:
                        alive.remove(g)
            P.barrier()
            P.emit()

        def mlp(layer, hin, hout, final_out, ph):
            TMm = cx.TM
            cx.m_xin = [P.sb("m_xin%d" % i, [128, DC, TMm], F32, ph) for i in range(2)]
            cx.m_xb = [P.sb("m_xb%d" % i, [128, DC, TMm], BF16, ph) for i in range(2)]
            cx.m_hmid = P.sb("m_hmid", [128, 32, TMm], BF16, ph)
            cx.m_relu = [P.sb("m_relu%d" % i, [128, TMm], F32, ph) for i in range(2)]
            cx.m_z = P.sb("m_z", [128, DC, TMm], F32, ph)
            cx.m_ho = P.sb("m_ho", [128, DC, TMm], F32, ph)
            cx.ln_zsq = P.sb("ln_zsq", [128, DC, TMm], F32, ph)
            cx.ln_mean = P.sb("ln_mean", [128, TMm], F32, ph)
            cx.ln_rstd = P.sb("ln_rstd", [128, TMm], F32, ph)
            cx.ln_tmp = P.sb("ln_tmp", [128, TMm], F32, ph)
            cx.m_otm = [P.sb("m_otm%d" % i, [128, D], F32, ph) for i in range(2)]
            cx.wring = [P.sb("wring%d" % i, [128, DC, 1024], BF16, ph) for i in range(5)]
            ntl = cfg.get("mlp_tiles", LP // TMm)
            if "nblk" in cfg and "mlp_tiles" not in cfg:
                tiles = [(i * 128, 128) for i in range(cfg["nblk"])]
            else:
                tiles = [(i * TMm, TMm) for i in range(ntl)]
            gi = COL_LNG + (layer * 2 + 1) * 8
            bi = COL_LNB + (layer * 2 + 1) * 8
            mlp_phase(P, cx, hin, hout, wbf["up%d" % layer], wbf["dn%d" % layer],
                      colp[:, gi:gi + 8], colp[:, bi:bi + 8], tiles,
                      final_out=final_out)

        if cfg.get("p1", True) and cfg.get("p1_split", True):
            cx.cm_scr = P.dram("cm_scr", [NBLK * 128, 4096], BF16)
            cx.tmb_scr = P.dram("tmb_scr", [NBLK * 128, 4096], BF16)
            cx.tmf_scr = P.dram("tmf_scr", [NBLK * 128, 2048], F32)
            cx.sm_scr = P.dram("sm_scr", [NBLK * 128, 32], F32)
            cx.front_pool = cfg.get("front_pool", 17)
            cx.back_depth = cfg.get("back_depth", 4)
            with contextlib.ExitStack() as ph:
                rwkv_front_setup(P, cx, ph, wbf, rowp_in)
                run_pipelined([rwkv_front_gen(P, cx, n, h_scr[0]) for n in range(nblk)],
                              lag=cfg.get("front_lag", 6), depth=2)
                P.barrier()
                P.emit()
            with contextlib.ExitStack() as ph:
                if "b" not in cfg.get("p1_parts", "abc"):
                    nblk_b = 0
                else:
                    nblk_b = nblk
                rwkv_back_setup(P, cx, ph, rowp_in)
                gens = []
                for n in range(nblk_b):
                    for b in range(4):
                        gens.append(rwkv_back_gen(P, cx, n, b, len(gens), nblk))
                bg = None
                if getattr(cx, "bg_list", None):
                    cx.cvg_f = [P.sb("cvgf", [128, 1024], F32, ph) for i in range(2)]
                    cx.cvg_b = [P.sb("cvgb", [128, 1024], BF16, ph) for i in range(2)]
                    cx.cvg_i = 0

                    def bg_all():
                        for (nm_, r_, c_) in cx.bg_list:
                            yield from convert_weight_gen(P, cx, wbf[nm_], win[nm_], r_, c_)
                    bg = bg_all()
                run_pipelined(gens, lag=cfg.get("back_lag", 2), depth=cx.back_depth, bg=bg,
                              bg_every=cfg.get("bg_every", 1))
                P.barrier()
                P.emit()
        if cfg.get("p1", True) and not cfg.get("p1_split", True):
            with contextlib.ExitStack() as ph:
                rwkv_setup(P, cx, ph, wbf, rowp_in)
                for n in range(0 if cfg.get("p1_skip_blocks") else nblk):
                    rwkv_block(P, cx, n, h_scr[0], h_scr[1])
                P.barrier()
                P.emit()
        if cfg.get("p1", True):
            with contextlib.ExitStack() as ph:
                if not cfg.get("p1c", True) or "c" not in cfg.get("p1_parts", "abc"):
                    raise_skip = True
                else:
                    raise_skip = False
                if "nblk" in cfg:
                    ctiles = [(i * 128, 128) for i in range(cfg["nblk"])]
                else:
                    ctiles = [(i * cx.TM, cx.TM) for i in range(LP // cx.TM)]
                if not raise_skip:
                    oproj_phase(P, cx, ph, wbf, h_scr[0], h_scr[1], ctiles)
                P.barrier()
                P.emit()

        cx.rope_in = rope_in
        if cfg.get("p2", True):
            with contextlib.ExitStack() as ph:
                mlp(0, h_scr[1], h_scr[2], None, ph)
                P.barrier()
                P.emit()

        if cfg.get("p3", True):
            with contextlib.ExitStack() as ph:
                attn_setup(P, cx, ph, wbf, sinks_in, amask_in)
                run_pipelined([attn_block(P, cx, n, h_scr[2], h_scr[0]) for n in range(nblk)], lag=cfg.get("attn_lag", 5), depth=3)
                P.barrier()
                P.emit()

        if cfg.get("p4", True):
            with contextlib.ExitStack() as ph:
                hin = h_test if cfg.get("test_hin") else h_scr[0]
                mlp(1, hin, None, out_t, ph)
                P.barrier()
                P.emit()
        P.finish([out_t])
        P.emit()
        cx.n_ops = P.n_ops
    return nc, cx


def host_consts():
    cm = np.zeros((128, NCM, 128), np.float32)
    i = np.arange(128)
    cm[:, CM_IDENT] = np.eye(128, dtype=np.float32)
    cm[:, CM_ONESD] = 1.0 / D
    s, t = i[:, None], i[None, :]
    cm[:, CM_MU] = (s < t)
    cm[:, CM_MUE] = (s <= t)
    cm[:, CM_ML] = (t < s)
    cm[:, CM_M1] = (s <= t).astype(np.float32) - (s <= 63).astype(np.float32)
    cm[:, CM_M2] = (s > t)
    return cm.reshape(128, NCM * 128)


def host_rope():
    inv_freq = (1.0 / (10000.0 ** (np.arange(0, 64, 2, dtype=np.float32) / np.float32(64)))).astype(np.float32)
    pos = np.maximum(np.arange(LP) - 112, 0).astype(np.float32)
    ang = (pos[:, None] * inv_freq[None, :]).astype(np.float32)
    return np.concatenate([np.cos(ang), np.sin(ang)], axis=1).astype(np.float32)


def host_amask():
    NEG = -30000.0
    qi = np.arange(128)[:, None]
    kj = np.arange(256)[None, :]
    rel = 128 + qi - kj
    inwin = (rel >= 0) & (rel < 128)
    m = np.zeros((3, 128, 256), np.float32)
    for v, nb in enumerate([2, 0, 1]):
        key_pos = (nb - 1) * 128 + np.arange(256)[None, :]
        ok = inwin & (key_pos >= 112)
        m[v] = np.where(ok, 0.0, NEG)
    return np.ascontiguousarray(m.transpose(1, 0, 2).reshape(128, 768))


_CACHE = {}


def kernel(x, meta_tokens, a_mu, a_w_r, a_w_k, a_w_v, a_w_o, a_w0, a_w1, a_w2,
           a_a0, a_a1, a_a2, a_g1, a_g2, a_k_k, a_k_a, a_r_k, a_gn_w, a_gn_b,
           kv_w_k, kv_w_v, b_w_q, b_sinks, b_w_o, mlp_w_up, mlp_w_down, ln_g, ln_b):
    f = lambda a: np.ascontiguousarray(np.asarray(a, dtype=np.float32))
    x = f(x)
    if "nc" not in _CACHE:
        _CACHE["nc"] = build_program({})[0]
    nc = _CACHE["nc"]
    shared = {
        "meta": f(meta_tokens),
        "a_w_r": f(a_w_r)[0], "a_w_k": f(a_w_k)[0], "a_w_v": f(a_w_v)[0], "a_w_o": f(a_w_o)[0],
        "a_w1": f(a_w1)[0], "a_w2": f(a_w2)[0], "a_a1": f(a_a1)[0], "a_a2": f(a_a2)[0],
        "a_g1": f(a_g1)[0], "a_g2": f(a_g2)[0],
        "kv_w_k": f(kv_w_k), "kv_w_v": f(kv_w_v), "b_w_q": f(b_w_q)[0], "b_w_o": f(b_w_o)[0],
        "up0": f(mlp_w_up)[0], "dn0": f(mlp_w_down)[0], "up1": f(mlp_w_up)[1], "dn1": f(mlp_w_down)[1],
    }
    colp = np.zeros((128, NCOL), np.float32)
    mu = f(a_mu)[0]
    for i in range(6):
        colp[:, COL_MU + i * 8:COL_MU + (i + 1) * 8] = mu[i].reshape(8, 128).T
    lg, lb = f(ln_g), f(ln_b)
    for l in range(2):
        for j in range(2):
            k = (l * 2 + j) * 8
            colp[:, COL_LNG + k:COL_LNG + k + 8] = lg[l, j].reshape(8, 128).T
            colp[:, COL_LNB + k:COL_LNB + k + 8] = lb[l, j].reshape(8, 128).T
    rowp = np.stack([f(a_w0)[0], f(a_a0)[0], f(a_k_k)[0], f(a_k_a)[0], f(a_r_k)[0].reshape(-1),
                     f(a_gn_w)[0], f(a_gn_b)[0]], axis=0)
    shared.update({"colp": colp, "rowp": np.ascontiguousarray(rowp),
                   "sinks": f(b_sinks)[0].reshape(1, 16),
                   "cmat": host_consts(), "amask": host_amask(), "rope": host_rope()})
    in_maps = []
    for b in range(8):
        m = dict(shared)
        m["x"] = x[b]
        in_maps.append(m)
    res = run_bass_kernel_spmd(nc, in_maps, core_ids=list(range(8)))
    return np.stack([res.results[b]["out"] for b in range(8)], axis=0).astype(np.float32)


def xpose_in_phase(P, cx, x_in, meta_in, h0, n_only):
    h0v = h0.ap.rearrange("(c p) t -> p c t", p=128)
    for n in [n_only]:
        xt = cx.x_tm[n % 2]
        if n == 0:
            P.memset("pool", xt[:, :], 0.0)
            P.dma("sp", xt[112:128, :], meta_in[:, :])
        else:
            P.dma("sp", xt[:, :], x_in[(n - 1) * 128:n * 128, :])
        xo = cx.x_fm[n % 2]
        for half in range(2):
            pb = cx.ps[(2 * n + half) % 8]
            for c4 in range(4):
                c = half * 4 + c4
                P.tr(pb[:, c4 * 128:(c4 + 1) * 128], xt[:, c * 128:(c + 1) * 128], cx.ident[:, :])
            dst = xo[:, half * 4:(half + 1) * 4, :]
            P.copy("act" if half == 0 else "dve", dst, v3(pb[:, :], 4))
        P.dma("pool", h0.v(h0v[:, :, n * 128:(n + 1) * 128]), xo[:, :, :])


class Pool_:
    def __init__(self, bufs):
        self.free = list(bufs)

    def get(self):
        return self.free.pop(0)

    def put(self, *bs):
        for b in bs:
            self.free.append(b)


def v3(v, c):
    return V(v.t, v.ap.rearrange("p (c t) -> p c t", c=c))


def rwkv_block(P, cx, n, h0, h1):
    R = cx.rw
    ident = cx.ident
    tp = cx.tmpool
    colp = cx.colp
    cm_ = cx.cmat
    nb = cx.nextbank
    t0 = n * 128
    h0v = h0.ap.rearrange("(c p) t -> p c t", p=128)
    h1v = h1.ap.rearrange("(c p) t -> p c t", p=128)
    xf = cx.xfm[0]
    if n == 0:
        P.memset("pool", xf[:, :, 0:1], 0.0)
        P.dma("sp", xf[:, :, 1:129], h0.v(h0v[:, :, 0:128]))
    else:
        P.dma("sp", xf[:, :, 0:129], h0.v(h0v[:, :, t0 - 1:t0 + 128]))
    xxb = tp.get()
    xx = v3(xxb[:, :], 8)
    P.tt("pool", xx[:, :, :], xf[:, :, 0:128], xf[:, :, 1:129], ALU.subtract)

    def mix(i):
        m = cx.mixb[cx.mix_i % 2]
        cx.mix_i += 1
        tmp = tp.get()
        tv = v3(tmp[:, :], 8)
        P.tt("dve", tv, xx[:, :, :], bcast_last(colp[:, COL_MU + i * 8:COL_MU + i * 8 + 8], 128), ALU.mult)
        P.tt("dve", m[:, :, :], tv, xf[:, :, 1:129], ALU.add)
        tp.put(tmp)
        return m

    def proj_tm(m, W):
        banks = [nb(), nb()]
        for half in range(2):
            for c in range(DC):
                P.mm(banks[half][:, :], m[:, c, :], W[:, c, half * 512:(half + 1) * 512],
                     start=(c == 0), stop=(c == DC - 1))
        return banks

    def evac2(banks, dst, engs):
        for half in range(2):
            P.copy(engs[half], dst[:, half * 512:(half + 1) * 512], banks[half][:, :])

    r32, k32, v32 = tp.get(), tp.get(), tp.get()
    m_r = mix(0)
    m_k = mix(2)
    b_r = proj_tm(m_r, R["wr"])
    m_v = mix(3)
    b_k = proj_tm(m_k, R["wk"])
    evac2(b_r, r32, ["act", "act"])
    b_v = proj_tm(m_v, R["wv"])
    evac2(b_k, k32, ["act", "act"])
    evac2(b_v, v32, ["act", "act"])
    vbf = cx.vbf
    P.copy("act", vbf[:, :], v32[:, :])

    def lora1(i, W1, width, func, dst):
        m = mix(i)
        b = nb()
        for c in range(DC):
            P.mm(b[:, 0:128], W1[:, c, :], m[:, c, :], start=(c == 0), stop=(c == DC - 1))
        P.act(dst[:, :], b[:, 0:128], func)

    lora1(1, R["w1"], 64, AF.Tanh, cx.hw)
    lora1(4, R["a1"], 64, AF.Identity, cx.ha)
    lora1(5, R["g1"], 128, AF.Sigmoid, cx.hg)
    tp.put(xxb)

    def lora2(hsrc, width, W2, biasrow):
        banks = [nb(), nb()]
        for half in range(2):
            sl = slice(half * 512, (half + 1) * 512)
            if biasrow is not None:
                P.mm(banks[half][:, :], cx.ones_row[:, :], cx.brow[:, biasrow, sl], start=True, stop=False)
            P.mm(banks[half][:, :], hsrc[:, :], W2[:, sl], start=(biasrow is None), stop=True)
        return banks

    sg, alr, g32 = tp.get(), tp.get(), tp.get()
    bw = lora2(cx.hw, 64, R["w2"], 0)
    for half in range(2):
        P.act(sg[:, half * 512:(half + 1) * 512], bw[half][:, :], AF.Sigmoid)
    ba = lora2(cx.ha, 64, R["a2"], 1)
    for half in range(2):
        P.act(alr[:, half * 512:(half + 1) * 512], ba[half][:, :], AF.Sigmoid)
    evac2(lora2(cx.hg, 128, R["g2"], None), g32, ["act", "dve"])

    M1 = cm_[:, CM_M1 * 128:(CM_M1 + 1) * 128]
    M2 = cm_[:, CM_M2 * 128:(CM_M2 + 1) * 128]
    eP, eM, eR, eS = tp.get(), tp.get(), tp.get(), tp.get()
    for half in range(2):
        sl = slice(half * 512, (half + 1) * 512)
        bD = nb()
        P.mm(bD[:, :], M1, sg[:, sl])
        P.act(eP[:, sl], bD[:, :], AF.Exp, scale=-C0)
        P.act(eM[:, sl], bD[:, :], AF.Exp, scale=C0)
        bR = nb()
        P.mm(bR[:, :], M2, sg[:, sl])
        P.act(eR[:, sl], bR[:, :], AF.Exp, scale=-C0)
    P.act(eS[:, :], sg[:, :], AF.Exp, scale=C0)
    bT = nb()
    for c in range(DC):
        P.mm(bT[:, 2 * c:2 * c + 2], sg[:, c * 128:(c + 1) * 128], cx.ones2[:, 0:2])
    PC, PM = cx.PC, cx.PM
    bTv = v3(bT[:, 0:16], 8)
    P.act(PC[:, :], bTv[:, :, 0], AF.Exp, scale=-C0)
    P.act(PM[:, :], bTv[:, :, 1], AF.Exp, scale=-C0)
    tp.put(sg)
    P.tt("pool", eS[:, :], eS[:, :], eP[:, :], ALU.mult)
    eA = eS

    rowb = cx.rowb
    kk = tp.get()
    P.tt("pool", kk[:, :], k32[:, :], rowb[:, 0, :], ALU.mult)
    kmod = tp.get()
    P.stt(kmod[:, :], alr[:, :], -1.0, rowb[:, 1, :], ALU.add, ALU.mult)
    P.stt(kmod[:, :], kmod[:, :], 1.0, k32[:, :], ALU.add, ALU.mult)
    tp.put(k32)
    sq = tp.get()
    P.act(sq[:, :], kk[:, :], AF.Square)
    ss, rn = cx.ss, cx.rn
    P.reduce(ss[:, :], v3(sq[:, :], 16), ALU.add)
    P.ts("dve", ss[:, :], ss[:, :], 1e-24, ALU.max)
    P.act(ss[:, :], ss[:, :], AF.Sqrt)
    P.recip(rn[:, :], ss[:, :])
    P.tt("dve", v3(kk[:, :], 16), v3(kk[:, :], 16), bcast_last(rn[:, :], 64), ALU.mult)
    tp.put(sq)
    bb = tp.get()
    P.tt("pool", bb[:, :], kk[:, :], alr[:, :], ALU.mult)
    tp.put(alr)
    sq = tp.get()
    P.tt("pool", sq[:, :], r32[:, :], kmod[:, :], ALU.mult)
    P.tt("pool", sq[:, :], sq[:, :], rowb[:, 2, :], ALU.mult)
    bon = cx.bon
    P.reduce(bon[:, :], v3(sq[:, :], 16), ALU.add)
    tp.put(sq)
    P.tt("dve", eP[:, :], eP[:, :], r32[:, :], ALU.mult)
    rt = eP
    tp.put(r32)
    bt = tp.get()
    P.tt("pool", bt[:, :], bb[:, :], eM[:, :], ALU.mult)
    P.tt("dve", eM[:, :], eM[:, :], kmod[:, :], ALU.mult)
    kt = eM
    P.stt(eA[:, :], kk[:, :], -1.0, eA[:, :], ALU.mult, ALU.mult)
    at = eA
    tp.put(kk)
    khat, bhat, atbf = cx.khat, cx.bhat, cx.atbf
    P.tt("pool", khat[:, :], kmod[:, :], eR[:, :], ALU.mult)
    P.tt("pool", bhat[:, :], bb[:, :], eR[:, :], ALU.mult)
    P.copy("act", atbf[:, :], at[:, :])
    tp.put(eR, kmod, bb)

    cmz = cx.cmz
    for kind, src in enumerate([bt, kt, at, rt]):
        for half in range(2):
            pb = nb()
            for c4 in range(4):
                p = half * 4 + c4
                P.tr(pb[:, c4 * 128:(c4 + 1) * 128], src[:, p * 128:(p + 1) * 128], ident[:, :])
            P.copy("act", cmz[0][0:64, half * 4:(half + 1) * 4, kind, :], v3(pb[0:64, :], 4))
            P.copy("dve", cmz[1][64:128, half * 4:(half + 1) * 4, kind, :], v3(pb[64:128, :], 4))
    tp.put(bt, kt, at, rt)

    ST, STz = cx.ST, cx.STz
    P.tt("pool", STz[0][0:64, :, :], ST[0:64, :, :], bcast_last(PM[0:64, :], 64), ALU.mult)
    P.tt("pool", STz[1][64:128, :, :], ST[64:128, :, :], bcast_last(PM[64:128, :], 64), ALU.mult)

    y32 = tp.get()
    MU = cm_[:, CM_MU * 128:(CM_MU + 1) * 128]
    MUE = cm_[:, CM_MUE * 128:(CM_MUE + 1) * 128]
    ML = cm_[:, CM_ML * 128:(CM_ML + 1) * 128]
    identb = cx.identb
    def batch_gen(bt_i):
        heads = [4 * bt_i + i for i in range(4)]
        BS = cx.bsets[bt_i % 3]

        def hsl(h):
            return h % 2, h // 2

        def gram(kl, kr, mask, dst, eng):
            pb = nb()
            for i, h in enumerate(heads):
                e, p = hsl(h)
                P.mm(pb[:, i * 128:(i + 1) * 128], cmz[e][:, p, kl, :], cmz[e][:, p, kr, :])
            P.tt(eng, dst[:, :, :], v3(pb[:, :], 4), bcast_mid(mask, 4), ALU.mult)

        X, XT = BS.Xb[0], BS.XTb[0]
        gram(0, 2, MU, X, "dve")
        gram(2, 0, ML, XT, "dve")
        yield
        AakT, Arb, Ark = BS.AakT, BS.Arb, BS.Ark
        gram(2, 1, ML, AakT, "dve")
        gram(0, 3, MUE, Arb, "dve")
        gram(1, 3, MUE, Ark, "dve")
        yield
        Pm = BS.Pb[0]
        P.tt("pool", Pm[:, :, :], X[:, :, :], bcast_mid(identb[:, :], 4), ALU.add)
        cur = 0
        for k in range(1, 7):
            pbx = None
            if k <= 5:
                pbx = nb()
                for i in range(4):
                    P.mm(pbx[:, i * 128:(i + 1) * 128], XT[:, i, :], X[:, i, :])
            pbt = nb()
            for i in range(4):
                P.mm(pbt[:, i * 128:(i + 1) * 128], X[:, i, :], XT[:, i, :])
            if pbx is not None:
                P.copy("act", X[:, :, :], v3(pbx[:, :], 4))
            P.copy("act", XT[:, :, :], v3(pbt[:, :], 4))
            pb = nb()
            for i in range(4):
                P.mm(pb[:, i * 128:(i + 1) * 128], XT[:, i, :], Pm[:, i, :])
            P.tt("dve", Pm[:, :, :], v3(pb[:, :], 4), Pm[:, :, :], ALU.add)
            yield
        Tm = Pm
        Wt, M2m = BS.Wt, BS.M2m
        Ubf = BS.Ubf
        pb = nb()
        for i, h in enumerate(heads):
            e, p = hsl(h)
            P.mm(pb[:, i * 128:(i + 1) * 128], atbf[:, p * 128:(p + 1) * 128], Tm[:, i, :])
        pb4 = v3(pb[:, :], 4)
        P.copy("act", Wt[0:64, 0, :], pb4[0:64, 0, :])
        P.copy("act", Wt[0:64, 1, :], pb4[0:64, 2, :])
        P.copy("dve", Wt[64:128, 0, :], pb4[64:128, 1, :])
        P.copy("dve", Wt[64:128, 1, :], pb4[64:128, 3, :])
        pb = nb()
        for i in range(4):
            P.mm(pb[:, i * 128:(i + 1) * 128], AakT[:, i, :], Tm[:, i, :])
        P.copy("act", M2m[:, :, :], v3(pb[:, :], 4))
        yield
        pb = nb()
        for i, h in enumerate(heads):
            e, p = hsl(h)
            hs = slice(h * 64, (h + 1) * 64)
            P.mm(pb[:, i * 64:(i + 1) * 64], Wt[:, i // 2, :], STz[e][:, p, :], start=True, stop=False)
            P.mm(pb[:, i * 64:(i + 1) * 64], M2m[:, i, :], vbf[:, hs], start=False, stop=True)
        P.copy("act", Ubf[:, :], pb[:, 0:256])
        yield
        pb = nb()
        for i, h in enumerate(heads):
            e, p = hsl(h)
            hs = slice(h * 64, (h + 1) * 64)
            P.mm(pb[:, i * 64:(i + 1) * 64], cmz[e][:, p, 3, :], STz[e][:, p, :], start=True, stop=False)
            P.mm(pb[:, i * 64:(i + 1) * 64], Arb[:, i, :], Ubf[:, i * 64:(i + 1) * 64], start=False, stop=False)
            P.mm(pb[:, i * 64:(i + 1) * 64], Ark[:, i, :], vbf[:, hs], start=False, stop=True)
        P.copy("act", y32[:, bt_i * 256:(bt_i + 1) * 256], pb[:, 0:256])
        yield
        pb = nb()
        for i, h in enumerate(heads):
            e, p = hsl(h)
            hs = slice(h * 64, (h + 1) * 64)
            ps_ = slice(p * 128, (p + 1) * 128)
            P.mm(pb[:, i * 64:(i + 1) * 64], bhat[:, ps_], Ubf[:, i * 64:(i + 1) * 64], start=True, stop=False)
            P.mm(pb[:, i * 64:(i + 1) * 64], khat[:, ps_], vbf[:, hs], start=False, stop=True)
        for i, h in enumerate(heads):
            e, p = hsl(h)
            o = slice(64 * e, 64 * e + 64)
            P.stt(ST[o, p, :], ST[o, p, :], PC[o, p:p + 1], pb[o, i * 64:(i + 1) * 64], ALU.mult, ALU.add)

        yield

    run_pipelined([batch_gen(b) for b in range(4)], lag=4, depth=3)

    ysum, yv = cx.ysum, cx.yv
    P.reduce(ysum[:, :], v3(y32[:, :], 16), ALU.add)
    P.ts("dve", ysum[:, :], ysum[:, :], -1.0 / 64, ALU.mult)
    P.tt("pool", v3(y32[:, :], 16), v3(y32[:, :], 16), bcast_last(ysum[:, :], 64), ALU.add)
    sq = tp.get()
    P.act(sq[:, :], y32[:, :], AF.Square)
    P.reduce(yv[:, :], v3(sq[:, :], 16), ALU.add)
    P.act(yv[:, :], yv[:, :], AF.Sqrt, bias=cx.eps_gn[:, 0:1], scale=1.0 / 64)
    P.recip(yv[:, :], yv[:, :])
    P.tt("dve", v3(y32[:, :], 16), v3(y32[:, :], 16), bcast_last(yv[:, :], 64), ALU.mult)
    P.tt("pool", y32[:, :], y32[:, :], rowb[:, 3, :], ALU.mult)
    P.tt("pool", y32[:, :], y32[:, :], rowb[:, 4, :], ALU.add)
    P.tt("dve", v3(sq[:, :], 16), v3(v32[:, :], 16), bcast_last(bon[:, :], 64), ALU.mult)
    P.tt("pool", y32[:, :], y32[:, :], sq[:, :], ALU.add)
    P.tt("dve", y32[:, :], y32[:, :], g32[:, :], ALU.mult)
    tp.put(sq, v32, g32)
    ofm = cx.ofm[0]
    for half in range(2):
        pb = nb()
        for c4 in range(4):
            c = half * 4 + c4
            P.tr(pb[:, c4 * 128:(c4 + 1) * 128], y32[:, c * 128:(c + 1) * 128], ident[:, :])
        P.copy("act" if half == 0 else "dve", ofm[:, half * 4:(half + 1) * 4, :], v3(pb[:, :], 4))
    tp.put(y32)
    osv = cx.o_scr.ap.rearrange("(c p) t -> p c t", p=128)
    P.dma("pool", cx.o_scr.v(osv[:, :, t0:t0 + 128]), ofm[:, :, :])


def oproj_tile(P, cx, ti, t0, Tn, h0, h1):
    S = cx.c_sets[ti % 3]
    h0v = h0.ap.rearrange("(c p) t -> p c t", p=128)
    h1v = h1.ap.rearrange("(c p) t -> p c t", p=128)
    osv = cx.o_scr.ap.rearrange("(c p) t -> p c t", p=128)
    P.dma("sp", S.o[:, :, 0:Tn], cx.o_scr.v(osv[:, :, t0:t0 + Tn]))
    P.dma("sp", S.x[:, :, 0:Tn], h0.v(h0v[:, :, t0:t0 + Tn]))
    yield
    for oc in range(DC):
        pb = cx.ps[oc % 2]
        for kc in range(DC):
            P.mm(pb[:, 0:Tn], cx.c_wo[:, kc, oc * 128:(oc + 1) * 128], S.o[:, kc, 0:Tn],
                 start=(kc == 0), stop=(kc == DC - 1))
        P.stt(S.z[:, oc, 0:Tn], S.x[:, oc, 0:Tn], ALPHA, pb[:, 0:Tn], ALU.mult, ALU.add)
        if oc % 2 == 1:
            yield
    yield from layer_norm_fm(P, cx, S.z[:, :, 0:Tn], S.ho[:, :, 0:Tn], Tn, cx.colp[:, COL_LNG:COL_LNG + 8],
                             cx.colp[:, COL_LNB:COL_LNB + 8], S)
    yield
    P.dma("pool", h1.v(h1v[:, :, t0:t0 + Tn]), S.ho[:, :, 0:Tn])
    yield


def oproj_phase(P, cx, ph, wbf, h0, h1, tiles):
    TMm = cx.TM
    cx.c_wo = P.sb("c_wo", [128, 8, 1024], BF16, ph)
    P.dma("sp", cx.c_wo[:, :, :], wbf["a_w_o"].v(wbf["a_w_o"].ap.rearrange("(kc p) o -> p kc o", p=128)))
    cx.c_sets = []
    for par in range(3):
        S = Ctx()
        S.o = P.sb("c_o", [128, DC, TMm], BF16, ph)
        S.x = P.sb("c_x", [128, DC, TMm], F32, ph)
        S.z = P.sb("c_z", [128, DC, TMm], F32, ph)
        S.ho = P.sb("c_ho", [128, DC, TMm], F32, ph)
        S.ln_zsq = P.sb("c_zsq", [128, DC, TMm], F32, ph)
        S.ln_mean = P.sb("c_mean", [128, TMm], F32, ph)
        S.ln_rstd = P.sb("c_rstd", [128, TMm], F32, ph)
        S.ln_tmp = P.sb("c_tmp", [128, TMm], F32, ph)
        S.ln_banks = (cx.ps[2 + 2 * par], cx.ps[3 + 2 * par])
        cx.c_sets.append(S)
    gens = [oproj_tile(P, cx, ti, t0, Tn, h0, h1) for ti, (t0, Tn) in enumerate(tiles)]
    run_pipelined(gens, lag=4, depth=3)


def rwkv_setup(P, cx, ph, wbf, rowp_in):
    R = {}

    def wload(key, name, rows, cols):
        if rows % 128 == 0 and rows > 128:
            t = P.sb("rw_" + key, [128, rows // 128, cols], BF16, ph)
            P.dma("sp", t[:, :, :], wbf[name].v(wbf[name].ap.rearrange("(kc p) o -> p kc o", p=128)))
        else:
            t = P.sb("rw_" + key, [rows, cols], BF16, ph)
            P.dma("sp", t[:, :], wbf[name][:, :])
        R[key] = t

    wload("wr", "a_w_r", 1024, 1024)
    wload("wk", "a_w_k", 1024, 1024)
    wload("wv", "a_w_v", 1024, 1024)
    for key, name in (("w1", "a_w1"), ("a1", "a_a1")):
        t = P.sb("rw_" + key, [128, 8, 128], BF16, ph)
        P.memset("pool", t[:, :, :], 0.0)
        P.dma("sp", t[:, :, 0:64], wbf[name].v(wbf[name].ap.rearrange("(kc p) o -> p kc o", p=128)))
        R[key] = t
    wload("g1", "a_g1", 1024, 128)
    for key, name in (("w2", "a_w2"), ("a2", "a_a2")):
        t = P.sb("rw_" + key, [128, 1024], BF16, ph)
        P.memset("pool", t[:, :], 0.0)
        P.dma("sp", t[0:64, :], wbf[name][:, :])
        R[key] = t
    wload("g2", "a_g2", 128, 1024)
    cx.rw = R
    cx.rowb = P.sb("rowb", [128, 5, D], F32, ph)
    for j, ri in enumerate([ROW_KK, ROW_KA, ROW_RK, ROW_GNW, ROW_GNB]):
        src = rowp_in.ap[ri:ri + 1, :]
        bsrc = bass.AP(src.tensor, src.offset, [[0, 128], [1, D]])
        P.dma("sp", cx.rowb[:, j, :], rowp_in.v(bsrc))
    cx.brow = P.sb("brow", [128, 2, D], F32, ph)
    P.memset("pool", cx.brow[:, :, :], 0.0)
    P.dma("sp", cx.brow[0:1, 0, :], rowp_in[ROW_W0:ROW_W0 + 1, :])
    P.dma("sp", cx.brow[0:1, 1, :], rowp_in[ROW_A0:ROW_A0 + 1, :])
    cx.ones_row = P.sb("ones_row", [128, 128], F32, ph)
    P.memset("dve", cx.ones_row[:, :], 0.0)
    P.memset("dve", cx.ones_row[0:1, :], 1.0)
    cx.ones2 = P.sb("ones2", [128, 2], F32, ph)
    P.memset("dve", cx.ones2[:, :], 1.0)
    P.memset("dve", cx.ones2[64:128, 1:2], 0.0)
    cx.identb = P.sb("identb", [128, 128], BF16, ph)
    P.copy("dve", cx.identb[:, :], cx.ident[:, :])
    cx.xfm = [P.sb("xfm%d" % i, [128, DC, 129], F32, ph) for i in range(1)]
    cx.mixb = [P.sb("mixb%d" % i, [128, DC, 128], BF16, ph) for i in range(2)]
    cx.mix_i = 0
    cx.tmpool = Pool_([P.sb("tm%d" % i, [128, D], F32, ph) for i in range(11)])
    cx.vbf = P.sb("vbf", [128, D], BF16, ph)
    cx.khat = P.sb("khat", [128, D], BF16, ph)
    cx.bhat = P.sb("bhat", [128, D], BF16, ph)
    cx.atbf = P.sb("atbf", [128, D], BF16, ph)
    cx.hw = P.sb("hw", [128, 128], BF16, ph)
    cx.ha = P.sb("ha", [128, 128], BF16, ph)
    cx.hg = P.sb("hg", [128, 128], BF16, ph)
    cx.cmz = [P.sb("cmz%d" % i, [128, 8, 4, 128], BF16, ph) for i in range(2)]
    for i in range(2):
        P.memset("pool", cx.cmz[i][:, :, :, :], 0.0)
    cx.bsets = []
    for i in range(3):
        BS = Ctx()
        BS.Xb = [P.sb("Xb", [128, 4, 128], BF16, ph) for j in range(1)]
        BS.XTb = [P.sb("XTb", [128, 4, 128], BF16, ph) for j in range(1)]
        BS.Pb = [P.sb("Pb", [128, 4, 128], BF16, ph) for j in range(1)]
        BS.AakT = P.sb("AakT", [128, 4, 128], BF16, ph)
        BS.Arb = P.sb("Arb", [128, 4, 128], BF16, ph)
        BS.Ark = P.sb("Ark", [128, 4, 128], BF16, ph)
        BS.Wt = P.sb("Wt", [128, 2, 128], BF16, ph)
        BS.M2m = P.sb("M2m", [128, 4, 128], BF16, ph)
        BS.Ubf = P.sb("Ubf", [128, 256], BF16, ph)
        cx.bsets.append(BS)
    cx.ST = P.sb("ST", [128, 8, 64], F32, ph)
    cx.STz = [P.sb("STz%d" % i, [128, 8, 64], BF16, ph) for i in range(2)]
    P.memset("dve", cx.ST[:, :, :], 0.0)
    for i in range(2):
        P.memset("pool", cx.STz[i][:, :, :], 0.0)
    cx.PC = P.sb("PC", [128, 8], F32, ph)
    cx.PM = P.sb("PM", [128, 8], F32, ph)
    cx.ss = P.sb("ss", [128, 16], F32, ph)
    cx.rn = P.sb("rn", [128, 16], F32, ph)
    cx.bon = P.sb("bon", [128, 16], F32, ph)
    cx.ysum = P.sb("ysum", [128, 16], F32, ph)
    cx.yv = P.sb("yv", [128, 16], F32, ph)
    cx.ofm = [P.sb("ofm", [128, DC, 128], BF16, ph) for i in range(1)]
    cx.bank_i = 0

    def nextbank():
        b = cx.ps[cx.bank_i % 8]
        cx.bank_i += 1
        return b

    cx.nextbank = nextbank


def attn_setup(P, cx, ph, wbf, sinks_in, amask_in):
    A = {}
    for key, name, cols in (("wq", "b_w_q", 1024), ("wo", "b_w_o", 1024), ("wk", "kv_w_k", 256), ("wv", "kv_w_v", 256)):
        t = P.sb("aw_" + key, [128, 8, cols], BF16, ph)
        P.dma("sp", t[:, :, :], wbf[name].v(wbf[name].ap.rearrange("(kc p) o -> p kc o", p=128)))
        A[key] = t
    cx.aw = A
    cx.amask = P.sb("amask_sb", [128, 3, 256], F32, ph)
    P.dma("sp", cx.amask[:, :, :], amask_in.v(amask_in.ap.rearrange("p (v k) -> p v k", v=3)))
    cx.sinkb = P.sb("sinkb", [128, 16], F32, ph)
    src = sinks_in.ap[0:1, :]
    P.dma("sp", cx.sinkb[:, :], sinks_in.v(bass.AP(src.tensor, src.offset, [[0, 128], [1, 16]])))
    cx.a_kT = P.sb("a_kT", [128, 4, 2, 4, 128], BF16, ph)
    P.memset("pool", cx.a_kT[:, :, :, :, :], 0.0)
    cx.a_vb = P.sb("a_vb", [128, 4, 256], BF16, ph)
    P.memset("pool", cx.a_vb[:, :, :], 0.0)
    cx.a_sets = []
    for par in range(3):
        S = Ctx()
        S.x = P.sb("a_x", [128, DC, 128], F32, ph)
        S.xb = P.sb("a_xb", [128, DC, 128], BF16, ph)
        S.rope = P.sb("a_rope", [128, 64], F32, ph)
        S.q32 = P.sb("a_q32", [128, D], F32, ph)
        S.qr = P.sb("a_qr", [128, D], F32, ph)
        S.k32 = P.sb("a_k32", [128, 256], F32, ph)
        S.kr = P.sb("a_kr", [128, 2, 256], F32, ph)
        S.tA = P.sb("a_tA", [128, 512], F32, ph)
        S.tB = P.sb("a_tB", [128, 512], F32, ph)
        S.qT = P.sb("a_qT", [128, 8, 128], BF16, ph)
        S.sm = [P.sb("a_sm", [128, 2, 256], F32, ph) for i in range(8)]
        S.pT = [P.sb("a_pT", [128, 4, 128], BF16, ph) for i in range(2)]
        S.mx = P.sb("a_mx", [128, 16], F32, ph)
        S.negm = P.sb("a_negm", [128, 16], F32, ph)
        S.rs = P.sb("a_rs", [128, 16], F32, ph)
        S.es = P.sb("a_es", [128, 16], F32, ph)
        S.o32 = S.q32
        S.ofm = P.sb("a_ofm", [128, DC, 128], BF16, ph)
        S.z = v3(S.qr[:, :], 8)
        S.ho = S.z
        S.ln_zsq = P.sb("a_zs", [128, DC, 128], F32, ph)
        S.ln_mean = P.sb("ln_mean3", [128, 128], F32, ph)
        S.ln_rstd = P.sb("ln_rstd3", [128, 128], F32, ph)
        S.ln_tmp = P.sb("ln_tmp3", [128, 128], F32, ph)
        S.obank = cx.ps[5 + par]
        S.ln_banks = (S.obank[:, 0:128], S.obank[:, 128:256])
        cx.a_sets.append(S)
    cx.bank_i = 0

    def nextbank():
        b = cx.ps[cx.bank_i % 4]
        cx.bank_i += 1
        return b

    cx.nextbank = nextbank


def rope_tm(P, src, dst, nh, ropeb, tA, tB):
    sv = V(src.t, src.ap.rearrange("p (h two f) -> p h two f", h=nh, two=2))
    dv = V(dst.t, dst.ap.rearrange("p (h two f) -> p h two f", h=nh, two=2))
    c = bcast_mid(ropeb[:, 0:32], nh)
    s = bcast_mid(ropeb[:, 32:64], nh)
    a = v3(tA[:, 0:nh * 32], nh)
    b = v3(tB[:, 0:nh * 32], nh)
    P.tt("dve", a, sv[:, :, 0, :], c, ALU.mult)
    P.tt("pool", b, sv[:, :, 1, :], s, ALU.mult)
    yield
    P.tt("dve", dv[:, :, 0, :], a, b, ALU.subtract)
    P.tt("dve", a, sv[:, :, 1, :], c, ALU.mult)
    P.tt("pool", b, sv[:, :, 0, :], s, ALU.mult)
    yield
    P.tt("dve", dv[:, :, 1, :], a, b, ALU.add)


def attn_block(P, cx, n, hin, hout):
    A = cx.aw
    nb = cx.nextbank
    ident = cx.ident
    colp = cx.colp
    S = cx.a_sets[n % 3]
    t0 = n * 128
    hin_v = hin.ap.rearrange("(c p) t -> p c t", p=128)
    hout_v = hout.ap.rearrange("(c p) t -> p c t", p=128)
    x, xb, ropeb = S.x, S.xb, S.rope
    P.dma("sp", x[:, :, :], hin.v(hin_v[:, :, t0:t0 + 128]))
    P.dma("sp", ropeb[:, :], cx.rope_in[t0:t0 + 128, :])
    P.copy("act", xb[:, :, :], x[:, :, :])
    cur, prev = n % 4, (n + 3) % 4
    yield
    q32, qr, k32, kr = S.q32, S.qr, S.k32, S.kr
    for half in range(2):
        pb = nb()
        for c in range(DC):
            P.mm(pb[:, :], xb[:, c, :], A["wq"][:, c, half * 512:(half + 1) * 512], start=(c == 0), stop=(c == DC - 1))
        P.copy("act", q32[:, half * 512:(half + 1) * 512], pb[:, :])
    yield
    pb = nb()
    for c in range(DC):
        P.mm(pb[:, 0:256], xb[:, c, :], A["wk"][:, c, :], start=(c == 0), stop=(c == DC - 1))
    P.copy("act", k32[:, :], pb[:, 0:256])
    pb = nb()
    for c in range(DC):
        P.mm(pb[:, 0:256], xb[:, c, :], A["wv"][:, c, :], start=(c == 0), stop=(c == DC - 1))
    vb = cx.a_vb
    P.copy("act", vb[:, cur, :], pb[:, 0:256])
    yield
    yield from rope_tm(P, q32[:, :], qr[:, :], 16, ropeb, S.tA, S.tB)
    yield
    yield from rope_tm(P, k32[:, :], kr[:, 0, :], 4, ropeb, S.tA, S.tB)
    krn = V(kr, kr.ap[:, 0, :].rearrange("p (pr e f) -> p pr e f", pr=2, e=2))
    krs = V(kr, kr.ap[:, 1, :].rearrange("p (pr e f) -> p pr e f", pr=2, e=2))
    P.copy("act", krs[:, :, 0, :], krn[:, :, 1, :])
    P.copy("act", krs[:, :, 1, :], krn[:, :, 0, :])
    yield
    qT = S.qT
    for half in range(2):
        pb = nb()
        for c4 in range(4):
            p = half * 4 + c4
            P.tr(pb[:, c4 * 128:(c4 + 1) * 128], qr[:, p * 128:(p + 1) * 128], ident[:, :])
        P.copy("act" if half == 0 else "dve", qT[:, half * 4:(half + 1) * 4, :], v3(pb[:, :], 4))
    kT = cx.a_kT
    pb = nb()
    for j in range(4):
        P.tr(pb[:, j * 128:(j + 1) * 128], kr[:, j // 2, (j % 2) * 128:(j % 2 + 1) * 128], ident[:, :])
    pb4 = v3(pb[:, :], 4)
    kTv = V(kT, kT.ap.rearrange("p (pr g2) e s t -> p pr g2 e s t", pr=2, g2=2))
    P.copy("act", kTv[0:64, :, 0, 0, cur, :], pb4[0:64, 0:2, :])
    P.copy("dve", kTv[64:128, :, 1, 1, cur, :], pb4[64:128, 0:2, :])
    P.copy("act", kTv[0:64, :, 1, 0, cur, :], pb4[0:64, 2:4, :])
    P.copy("dve", kTv[64:128, :, 0, 1, cur, :], pb4[64:128, 2:4, :])
    yield
    mv = 1 if n == 0 else (2 if n == 1 else 0)
    mask = cx.amask[:, mv, :]
    mx, negm, rs, es = S.mx, S.negm, S.rs, S.es
    for hp in range(8):
        pb = nb()
        for i in range(2):
            h = hp * 2 + i
            g, e, p = h // 4, h % 2, h // 2
            P.mm(pb[:, i * 256:i * 256 + 128], qT[:, p, :], kT[:, g, e, prev, :])
            P.mm(pb[:, i * 256 + 128:(i + 1) * 256], qT[:, p, :], kT[:, g, e, cur, :])
        sm = S.sm[hp]
        P.stt(sm[:, :, :], v3(pb[:, :], 2), 0.125, bcast_mid(mask, 2), ALU.mult, ALU.add)
        P.reduce(mx[:, hp * 2:hp * 2 + 2], sm[:, :, :], ALU.max)
        if hp % 2 == 1:
            yield
    P.tt("dve", mx[:, :], mx[:, :], cx.sinkb[:, :], ALU.max)
    P.ts("dve", negm[:, :], mx[:, :], -1.0, ALU.mult)
    P.tt("dve", es[:, :], cx.sinkb[:, :], mx[:, :], ALU.subtract)
    P.act(es[:, :], es[:, :], AF.Exp)
    for hp in range(8):
        sm = S.sm[hp]
        for i in range(2):
            h = hp * 2 + i
            P.act(sm[:, i, :], sm[:, i, :], AF.Exp, bias=negm[:, h:h + 1], scale=1.0, accum_out=rs[:, h:h + 1])
        if hp % 4 == 3:
            yield
    P.tt("dve", rs[:, :], rs[:, :], es[:, :], ALU.add)
    P.recip(rs[:, :], rs[:, :])
    o32 = S.o32
    ob = S.obank

    def ptrans(hp):
        sm = S.sm[hp]
        pb = nb()
        for i in range(2):
            for j in range(2):
                P.tr(pb[:, (i * 2 + j) * 128:(i * 2 + j + 1) * 128], sm[:, i, j * 128:(j + 1) * 128], ident[:, :])
        pT = S.pT[hp % 2]
        P.copy("act" if hp % 2 == 0 else "dve", pT[:, :, :], v3(pb[:, :], 4))

    ptrans(0)
    for hp in range(8):
        if hp + 1 < 8:
            ptrans(hp + 1)
        pT = S.pT[hp % 2]
        for i in range(2):
            h = hp * 2 + i
            g = h // 4
            oc = slice((h % 8) * 64, (h % 8 + 1) * 64)
            P.mm(ob[:, oc], pT[:, i * 2 + 0, :], vb[:, prev, g * 64:(g + 1) * 64], start=True, stop=False)
            P.mm(ob[:, oc], pT[:, i * 2 + 1, :], vb[:, cur, g * 64:(g + 1) * 64], start=False, stop=True)
        if hp % 2 == 1:
            yield
        if hp % 4 == 3:
            half = hp // 4
            P.tt("dve", v3(o32[:, half * 512:(half + 1) * 512], 8), v3(ob[:, :], 8),
                 bcast_last(rs[:, half * 8:(half + 1) * 8], 64), ALU.mult)
            yield
    yield
    ofm = S.ofm
    for half in range(2):
        pb = nb()
        for c4 in range(4):
            c = half * 4 + c4
            P.tr(pb[:, c4 * 128:(c4 + 1) * 128], o32[:, c * 128:(c + 1) * 128], ident[:, :])
        P.copy("act" if half == 0 else "dve", ofm[:, half * 4:(half + 1) * 4, :], v3(pb[:, :], 4))
    yield
    z = S.z
    for half in range(2):
        pb = nb()
        for c4 in range(4):
            oc = half * 4 + c4
            for kc in range(DC):
                P.mm(pb[:, c4 * 128:(c4 + 1) * 128], A["wo"][:, kc, oc * 128:(oc + 1) * 128], ofm[:, kc, :],
                     start=(kc == 0), stop=(kc == DC - 1))
        P.stt(z[:, half * 4:(half + 1) * 4, :], x[:, half * 4:(half + 1) * 4, :], ALPHA,
              v3(pb[:, :], 4), ALU.mult, ALU.add)
        yield
    ho = S.ho
    yield from layer_norm_fm(P, cx, z, ho, 128, colp[:, COL_LNG + 16:COL_LNG + 24],
                             colp[:, COL_LNB + 16:COL_LNB + 24], S)
    P.dma("pool", hout.v(hout_v[:, :, t0:t0 + 128]), ho[:, :, :])
    yield


def run_pipelined(gens, lag, depth=2, bg=None, bg_every=1):
    it = iter(gens)
    active = []

    def start():
        g = next(it, None)
        if g is not None:
            active.append([g, 0])
            return True
        return False

    start()
    rnd = 0
    while active:
        for ent in list(active):
            try:
                next(ent[0])
                ent[1] += 1
            except StopIteration:
                active.remove(ent)
        if len(active) < depth and (not active or active[-1][1] >= lag):
            start()
        rnd += 1
        if bg is not None and rnd % bg_every == 0:
            next(bg, None)
    if bg is not None:
        for _ in bg:
            pass


def rwkv_front_setup(P, cx, ph, wbf, rowp_in):
    R = {}

    def wload(key, name, rows, cols):
        if rows % 128 == 0 and rows > 128:
            t = P.sb("rw_" + key, [128, rows // 128, cols], BF16, ph)
            P.dma("sp", t[:, :, :], wbf[name].v(wbf[name].ap.rearrange("(kc p) o -> p kc o", p=128)))
        else:
            t = P.sb("rw_" + key, [rows, cols], BF16, ph)
            P.dma("sp", t[:, :], wbf[name][:, :])
        R[key] = t

    wload("wr", "a_w_r", 1024, 1024)
    wload("wk", "a_w_k", 1024, 1024)
    wload("wv", "a_w_v", 1024, 1024)
    for key, name in (("w1", "a_w1"), ("a1", "a_a1")):
        t = P.sb("rw_" + key, [128, 8, 128], BF16, ph)
        P.memset("pool", t[:, :, :], 0.0)
        P.dma("sp", t[:, :, 0:64], wbf[name].v(wbf[name].ap.rearrange("(kc p) o -> p kc o", p=128)))
        R[key] = t
    wload("g1", "a_g1", 1024, 128)
    for key, name in (("w2", "a_w2"), ("a2", "a_a2")):
        t = P.sb("rw_" + key, [128, 1024], BF16, ph)
        P.memset("pool", t[:, :], 0.0)
        P.dma("sp", t[0:64, :], wbf[name][:, :])
        R[key] = t
    wload("g2", "a_g2", 128, 1024)
    cx.rw = R
    cx.rowb = P.sb("rowb", [128, 3, D], F32, ph)
    for j, ri in enumerate([ROW_KK, ROW_KA, ROW_RK]):
        src = rowp_in.ap[ri:ri + 1, :]
        P.dma("sp", cx.rowb[:, j, :], rowp_in.v(bass.AP(src.tensor, src.offset, [[0, 128], [1, D]])))
    cx.brow = P.sb("brow", [128, 2, D], F32, ph)
    P.memset("pool", cx.brow[:, :, :], 0.0)
    P.dma("sp", cx.brow[0:1, 0, :], rowp_in[ROW_W0:ROW_W0 + 1, :])
    P.dma("sp", cx.brow[0:1, 1, :], rowp_in[ROW_A0:ROW_A0 + 1, :])
    cx.ones_row = P.sb("ones_row", [128, 128], F32, ph)
    P.memset("dve", cx.ones_row[:, :], 0.0)
    P.memset("dve", cx.ones_row[0:1, :], 1.0)
    cx.ones2 = P.sb("ones2", [128, 2], F32, ph)
    P.memset("dve", cx.ones2[:, :], 1.0)
    P.memset("dve", cx.ones2[64:128, 1:2], 0.0)
    cx.fsets = []
    for par in range(2):
        S = Ctx()
        S.xfm = P.sb("f_xfm", [128, DC, 129], F32, ph)
        S.mixb = [P.sb("f_mixb", [128, DC, 128], BF16, ph) for i in range(2)]
        S.mix_i = 0
        S.hw = P.sb("f_hw", [128, 128], BF16, ph)
        S.ha = P.sb("f_ha", [128, 128], BF16, ph)
        S.hg = P.sb("f_hg", [128, 128], BF16, ph)
        S.cm = P.sb("f_cm", [128, 8, 4, 128], BF16, ph)
        S.tmb = P.sb("f_tmb", [128, 4, D], BF16, ph)
        S.sm = P.sb("f_sm", [128, 32], F32, ph)
        S.ss = P.sb("f_ss", [128, 16], F32, ph)
        S.rn = P.sb("f_rn", [128, 16], F32, ph)
        cx.fsets.append(S)
    cx.tmpool = Pool_([P.sb("tm%d" % i, [128, D], F32, ph) for i in range(cx.front_pool)])
    cx.bank_i = 0

    def nextbank():
        b = cx.ps[cx.bank_i % 8]
        cx.bank_i += 1
        return b

    cx.nextbank = nextbank


def rwkv_front_gen(P, cx, n, h0):
    R = cx.rw
    ident = cx.ident
    tp = cx.tmpool
    colp = cx.colp
    cm_ = cx.cmat
    nb = cx.nextbank
    S = cx.fsets[n % 2]
    t0 = n * 128
    h0v = h0.ap.rearrange("(c p) t -> p c t", p=128)
    xf = S.xfm
    if n == 0:
        P.memset("pool", xf[:, :, 0:1], 0.0)
        P.dma("sp", xf[:, :, 1:129], h0.v(h0v[:, :, 0:128]))
    else:
        P.dma("sp", xf[:, :, 0:129], h0.v(h0v[:, :, t0 - 1:t0 + 128]))
    xxb = tp.get()
    xx = v3(xxb[:, :], 8)
    P.tt("dve", xx[:, :, :], xf[:, :, 0:128], xf[:, :, 1:129], ALU.subtract)
    yield
    rowb = cx.rowb
    blk = slice(n * 128, (n + 1) * 128)
    st = {}

    def mix(i):
        m = S.mixb[S.mix_i % 2]
        S.mix_i += 1
        tmp = tp.get()
        tv = v3(tmp[:, :], 8)
        P.tt("dve", tv, xx[:, :, :], bcast_last(colp[:, COL_MU + i * 8:COL_MU + i * 8 + 8], 128), ALU.mult)
        P.tt("dve", m[:, :, :], tv, xf[:, :, 1:129], ALU.add)
        tp.put(tmp)
        return m

    def proj_tm(m, W):
        banks = [nb(), nb()]
        for half in range(2):
            for c in range(DC):
                P.mm(banks[half][:, :], m[:, c, :], W[:, c, half * 512:(half + 1) * 512],
                     start=(c == 0), stop=(c == DC - 1))
        return banks

    def evac2(banks, dst, engs):
        for half in range(2):
            P.copy(engs[half], dst[:, half * 512:(half + 1) * 512], banks[half][:, :])

    def chain_x():
        r32 = tp.get()
        evac2(proj_tm(mix(0), R["wr"]), r32, ["act", "act"])
        yield
        k32 = tp.get()
        evac2(proj_tm(mix(2), R["wk"]), k32, ["act", "act"])
        yield
        v32 = tp.get()
        evac2(proj_tm(mix(3), R["wv"]), v32, ["act", "act"])
        P.copy("act", S.tmb[:, 3, :], v32[:, :])
        P.dma("pool", cx.tmf_scr[blk, 0:D], v32[:, :])
        tp.put(v32)
        yield
        kk = tp.get()
        P.tt("dve", kk[:, :], k32[:, :], rowb[:, 0, :], ALU.mult)
        sq = tp.get()
        P.act(sq[:, :], kk[:, :], AF.Square)
        yield
        ss, rn = S.ss, S.rn
        P.reduce(ss[:, :], v3(sq[:, :], 16), ALU.add)
        P.ts("dve", ss[:, :], ss[:, :], 1e-24, ALU.max)
        tp.put(sq)
        yield
        P.act(ss[:, :], ss[:, :], AF.Sqrt)
        yield
        P.recip(rn[:, :], ss[:, :])
        P.tt("dve", v3(kk[:, :], 16), v3(kk[:, :], 16), bcast_last(rn[:, :], 64), ALU.mult)
        st["r32"], st["k32"], st["kk"] = r32, k32, kk
        yield

    def lora1(i, W1, func, dst):
        m = mix(i)
        b = nb()
        for c in range(DC):
            P.mm(b[:, 0:128], W1[:, c, :], m[:, c, :], start=(c == 0), stop=(c == DC - 1))
        P.act(dst[:, :], b[:, 0:128], func)

    def lora2(hsrc, W2, biasrow):
        banks = [nb(), nb()]
        for half in range(2):
            sl = slice(half * 512, (half + 1) * 512)
            if biasrow is not None:
                P.mm(banks[half][:, :], cx.ones_row[:, :], cx.brow[:, biasrow, sl], start=True, stop=False)
            P.mm(banks[half][:, :], hsrc[:, :], W2[:, sl], start=(biasrow is None), stop=True)
        return banks

    def chain_y():
        lora1(1, R["w1"], AF.Tanh, S.hw)
        yield
        lora1(4, R["a1"], AF.Identity, S.ha)
        yield
        lora1(5, R["g1"], AF.Sigmoid, S.hg)
        sg = tp.get()
        bw = lora2(S.hw, R["w2"], 0)
        for half in range(2):
            P.act(sg[:, half * 512:(half + 1) * 512], bw[half][:, :], AF.Sigmoid)
        yield
        alr = tp.get()
        ba = lora2(S.ha, R["a2"], 1)
        for half in range(2):
            P.act(alr[:, half * 512:(half + 1) * 512], ba[half][:, :], AF.Sigmoid)
        g32 = tp.get()
        evac2(lora2(S.hg, R["g2"], None), g32, ["act", "act"])
        P.dma("pool", cx.tmf_scr[blk, D:2 * D], g32[:, :])
        tp.put(g32)
        yield
        M1 = cm_[:, CM_M1 * 128:(CM_M1 + 1) * 128]
        M2 = cm_[:, CM_M2 * 128:(CM_M2 + 1) * 128]
        eP, eM, eR, eS = tp.get(), tp.get(), tp.get(), tp.get()
        for half in range(2):
            sl = slice(half * 512, (half + 1) * 512)
            bD = nb()
            P.mm(bD[:, :], M1, sg[:, sl])
            P.act(eP[:, sl], bD[:, :], AF.Exp, scale=-C0)
            P.act(eM[:, sl], bD[:, :], AF.Exp, scale=C0)
            bR = nb()
            P.mm(bR[:, :], M2, sg[:, sl])
            P.act(eR[:, sl], bR[:, :], AF.Exp, scale=-C0)
        P.act(eS[:, :], sg[:, :], AF.Exp, scale=C0)
        bT = nb()
        for c in range(DC):
            P.mm(bT[:, 2 * c:2 * c + 2], sg[:, c * 128:(c + 1) * 128], cx.ones2[:, 0:2])
        bTv = v3(bT[:, 0:16], 8)
        P.act(S.sm[:, 16:24], bTv[:, :, 0], AF.Exp, scale=-C0)
        P.act(S.sm[:, 24:32], bTv[:, :, 1], AF.Exp, scale=-C0)
        tp.put(sg)
        yield
        P.tt("dve", eS[:, :], eS[:, :], eP[:, :], ALU.mult)
        st["alr"], st["eP"], st["eM"], st["eR"], st["eA"] = alr, eP, eM, eR, eS
        yield

    gx, gy = chain_x(), chain_y()
    alive = [gx, gy]
    while alive:
        for g in list(alive):
            try:
                next(g)
            except StopIteration:
                alive.remove(g)
        yield
    tp.put(xxb)
    r32, k32, kk = st["r32"], st["k32"], st["kk"]
    alr, eP, eM, eR, eA = st["alr"], st["eP"], st["eM"], st["eR"], st["eA"]
    kmod = tp.get()
    P.stt(kmod[:, :], alr[:, :], -1.0, rowb[:, 1, :], ALU.add, ALU.mult)
    P.stt(kmod[:, :], kmod[:, :], 1.0, k32[:, :], ALU.add, ALU.mult)
    tp.put(k32)
    bb = tp.get()
    P.tt("dve", bb[:, :], kk[:, :], alr[:, :], ALU.mult)
    tp.put(alr)
    yield
    sq = tp.get()
    P.tt("pool", sq[:, :], r32[:, :], kmod[:, :], ALU.mult)
    P.tt("dve", eP[:, :], eP[:, :], r32[:, :], ALU.mult)
    rt = eP
    tp.put(r32)
    yield
    P.tt("dve", sq[:, :], sq[:, :], rowb[:, 2, :], ALU.mult)
    P.reduce(S.sm[:, 0:16], v3(sq[:, :], 16), ALU.add)
    tp.put(sq)
    bt = tp.get()
    P.tt("dve", bt[:, :], bb[:, :], eM[:, :], ALU.mult)
    P.tt("dve", eM[:, :], eM[:, :], kmod[:, :], ALU.mult)
    kt = eM
    P.stt(eA[:, :], kk[:, :], -1.0, eA[:, :], ALU.mult, ALU.mult)
    at = eA
    tp.put(kk)
    P.tt("pool", S.tmb[:, 1, :], kmod[:, :], eR[:, :], ALU.mult)
    P.tt("pool", S.tmb[:, 2, :], bb[:, :], eR[:, :], ALU.mult)
    tp.put(eR, kmod, bb)
    yield
    P.copy("act", S.tmb[:, 0, :], at[:, :])
    P.dma("pool", cx.sm_scr[blk, :], S.sm[:, :])
    cm = S.cm
    for kind, src in enumerate([bt, kt, at, rt]):
        for half in range(2):
            pb = nb()
            for c4 in range(4):
                p = half * 4 + c4
                P.tr(pb[:, c4 * 128:(c4 + 1) * 128], src[:, p * 128:(p + 1) * 128], ident[:, :])
            P.copy("act" if half == 0 else "dve", cm[:, half * 4:(half + 1) * 4, kind, :], v3(pb[:, :], 4))
        if kind % 2 == 1:
            yield
    tp.put(bt, kt, at, rt)
    P.dma("pool", cx.tmb_scr[blk, :], S.tmb.v(S.tmb.ap.rearrange("p a d -> p (a d)")))
    P.dma("pool", cx.cm_scr[blk, :], cm.v(cm.ap.rearrange("p a k t -> p (a k t)")))
    yield


def rwkv_back_setup(P, cx, ph, rowp_in):
    cx.rowb2 = P.sb("rowb2", [128, 2, D], F32, ph)
    for j, ri in enumerate([ROW_GNW, ROW_GNB]):
        src = rowp_in.ap[ri:ri + 1, :]
        P.dma("sp", cx.rowb2[:, j, :], rowp_in.v(bass.AP(src.tensor, src.offset, [[0, 128], [1, D]])))
    cx.identb = P.sb("identb", [128, 128], BF16, ph)
    P.copy("dve", cx.identb[:, :], cx.ident[:, :])
    cx.ST = P.sb("ST", [128, 8, 64], F32, ph)
    P.memset("dve", cx.ST[:, :, :], 0.0)
    cx.bk_sets = []
    for i in range(3):
        S = Ctx()
        S.cmz = [P.sb("b_cmz", [128, 8, 4, 128], BF16, ph) for e in range(2)]
        S.STz = [P.sb("b_STz", [128, 8, 64], BF16, ph) for e in range(2)]
        for e in range(2):
            P.memset("pool", S.cmz[e][:, :, :, :], 0.0)
            P.memset("pool", S.STz[e][:, :, :], 0.0)
        S.tmb = P.sb("b_tmb", [128, 4, D], BF16, ph)
        S.tmf = P.sb("b_tmf", [128, 2, D], F32, ph)
        S.sm = P.sb("b_sm", [128, 32], F32, ph)
        S.y32 = P.sb("b_y32", [128, D], F32, ph)
        S.sq = P.sb("b_sq", [128, D], F32, ph)
        S.sq2 = P.sb("b_sq2", [128, D], F32, ph)
        S.ysum = P.sb("b_ysum", [128, 16], F32, ph)
        S.yv = P.sb("b_yv", [128, 16], F32, ph)
        S.ofm = P.sb("b_ofm", [128, DC, 128], BF16, ph)
        cx.bk_sets.append(S)
    cx.bsets = []
    for i in range(cx.back_depth):
        BS = Ctx()
        BS.X = P.sb("Xb", [128, 4, 128], BF16, ph)
        BS.XT = P.sb("XTb", [128, 4, 128], BF16, ph)
        BS.Pm = P.sb("Pb", [128, 4, 128], BF16, ph)
        BS.AakT = P.sb("AakT", [128, 4, 128], BF16, ph)
        BS.Arb = P.sb("Arb", [128, 4, 128], BF16, ph)
        BS.Ark = P.sb("Ark", [128, 4, 128], BF16, ph)
        BS.Wt = P.sb("Wt", [128, 2, 128], BF16, ph)
        BS.M2m = P.sb("M2m", [128, 4, 128], BF16, ph)
        BS.Ubf = P.sb("Ubf", [128, 256], BF16, ph)
        cx.bsets.append(BS)
    cx.bank_i = 0

    def nextbank():
        b = cx.ps[cx.bank_i % 8]
        cx.bank_i += 1
        return b

    cx.nextbank = nextbank


def rwkv_back_load(P, cx, n):
    S = cx.bk_sets[n % 3]
    blk = slice(n * 128, (n + 1) * 128)
    lo = slice(n * 128, n * 128 + 64)
    hi = slice(n * 128 + 64, (n + 1) * 128)
    cmv = cx.cm_scr.ap.rearrange("r (a k t) -> r a k t", a=8, k=4)
    P.dma("sp", S.cmz[0][0:64, :, :, :], cx.cm_scr.v(cmv[lo]))
    P.dma("sp", S.cmz[1][64:128, :, :, :], cx.cm_scr.v(cmv[hi]))
    P.dma("sp", S.tmb[:, :, :], cx.tmb_scr.v(cx.tmb_scr.ap.rearrange("r (a d) -> r a d", a=4)[blk]))
    P.dma("sp", S.tmf[:, :, :], cx.tmf_scr.v(cx.tmf_scr.ap.rearrange("r (a d) -> r a d", a=2)[blk]))
    P.dma("sp", S.sm[:, :], cx.sm_scr[blk, :])


def rwkv_back_gen(P, cx, n, bt_i, pos, nblk):
    ident = cx.ident
    cm_ = cx.cmat
    nb = cx.nextbank
    S = cx.bk_sets[n % 3]
    BS = cx.bsets[pos % cx.back_depth]
    t0 = n * 128
    if pos == 0:
        for m in range(min(3, nblk)):
            rwkv_back_load(P, cx, m)
        yield
    cmz = S.cmz
    STz = S.STz
    ST = cx.ST
    atbf, khat, bhat, vbf = S.tmb[:, 0, :], S.tmb[:, 1, :], S.tmb[:, 2, :], S.tmb[:, 3, :]
    PC, PM = S.sm[:, 16:24], S.sm[:, 24:32]
    y32 = S.y32
    MU = cm_[:, CM_MU * 128:(CM_MU + 1) * 128]
    MUE = cm_[:, CM_MUE * 128:(CM_MUE + 1) * 128]
    ML = cm_[:, CM_ML * 128:(CM_ML + 1) * 128]
    identb = cx.identb
    heads = [4 * bt_i + i for i in range(4)]

    def hsl(h):
        return h % 2, h // 2

    def gram(kl, kr, mask, dst, eng):
        pb = nb()
        for i, h in enumerate(heads):
            e, p = hsl(h)
            P.mm(pb[:, i * 128:(i + 1) * 128], cmz[e][:, p, kl, :], cmz[e][:, p, kr, :])
        P.tt(eng, dst[:, :, :], v3(pb[:, :], 4), bcast_mid(mask, 4), ALU.mult)

    X, XT, Pm = BS.X, BS.XT, BS.Pm
    gram(0, 2, MU, X, "dve")
    gram(2, 0, ML, XT, "dve")
    yield
    AakT, Arb, Ark = BS.AakT, BS.Arb, BS.Ark
    gram(2, 1, ML, AakT, "dve")
    gram(0, 3, MUE, Arb, "dve")
    gram(1, 3, MUE, Ark, "dve")
    yield
    P.tt("dve", Pm[:, :, :], X[:, :, :], bcast_mid(identb[:, :], 4), ALU.add)
    for k in range(1, 7):
        pbx = None
        if k <= 5:
            pbx = nb()
            for i in range(4):
                P.mm(pbx[:, i * 128:(i + 1) * 128], XT[:, i, :], X[:, i, :])
        pbt = nb()
        for i in range(4):
            P.mm(pbt[:, i * 128:(i + 1) * 128], X[:, i, :], XT[:, i, :])
        if pbx is not None:
            P.copy("act", X[:, :, :], v3(pbx[:, :], 4))
        P.copy("act", XT[:, :, :], v3(pbt[:, :], 4))
        yield
        pb = nb()
        for i in range(4):
            P.mm(pb[:, i * 128:(i + 1) * 128], XT[:, i, :], Pm[:, i, :])
        P.tt("dve", Pm[:, :, :], v3(pb[:, :], 4), Pm[:, :, :], ALU.add)
        yield
    Tm = Pm
    Wt, M2m, Ubf = BS.Wt, BS.M2m, BS.Ubf
    pb = nb()
    for i, h in enumerate(heads):
        e, p = hsl(h)
        P.mm(pb[:, i * 128:(i + 1) * 128], atbf[:, p * 128:(p + 1) * 128], Tm[:, i, :])
    pb4 = v3(pb[:, :], 4)
    P.copy("act", Wt[0:64, 0, :], pb4[0:64, 0, :])
    P.copy("act", Wt[0:64, 1, :], pb4[0:64, 2, :])
    P.copy("dve", Wt[64:128, 0, :], pb4[64:128, 1, :])
    P.copy("dve", Wt[64:128, 1, :], pb4[64:128, 3, :])
    pb = nb()
    for i in range(4):
        P.mm(pb[:, i * 128:(i + 1) * 128], AakT[:, i, :], Tm[:, i, :])
    P.copy("act", M2m[:, :, :], v3(pb[:, :], 4))
    pr = slice(2 * bt_i, 2 * bt_i + 2)
    P.tt("pool", STz[0][0:64, pr, :], ST[0:64, pr, :], bcast_last(PM[0:64, pr], 64), ALU.mult)
    P.tt("pool", STz[1][64:128, pr, :], ST[64:128, pr, :], bcast_last(PM[64:128, pr], 64), ALU.mult)
    yield
    pb = nb()
    for i, h in enumerate(heads):
        e, p = hsl(h)
        hs = slice(h * 64, (h + 1) * 64)
        P.mm(pb[:, i * 64:(i + 1) * 64], Wt[:, i // 2, :], STz[e][:, p, :], start=True, stop=False)
        P.mm(pb[:, i * 64:(i + 1) * 64], M2m[:, i, :], vbf[:, hs], start=False, stop=True)
    P.copy("act", Ubf[:, :], pb[:, 0:256])
    yield
    pb = nb()
    for i, h in enumerate(heads):
        e, p = hsl(h)
        hs = slice(h * 64, (h + 1) * 64)
        P.mm(pb[:, i * 64:(i + 1) * 64], cmz[e][:, p, 3, :], STz[e][:, p, :], start=True, stop=False)
        P.mm(pb[:, i * 64:(i + 1) * 64], Arb[:, i, :], Ubf[:, i * 64:(i + 1) * 64], start=False, stop=False)
        P.mm(pb[:, i * 64:(i + 1) * 64], Ark[:, i, :], vbf[:, hs], start=False, stop=True)
    P.copy("act", y32[:, bt_i * 256:(bt_i + 1) * 256], pb[:, 0:256])
    yield
    pb = nb()
    for i, h in enumerate(heads):
        e, p = hsl(h)
        hs = slice(h * 64, (h + 1) * 64)
        ps_ = slice(p * 128, (p + 1) * 128)
        P.mm(pb[:, i * 64:(i + 1) * 64], bhat[:, ps_], Ubf[:, i * 64:(i + 1) * 64], start=True, stop=False)
        P.mm(pb[:, i * 64:(i + 1) * 64], khat[:, ps_], vbf[:, hs], start=False, stop=True)
    for i, h in enumerate(heads):
        e, p = hsl(h)
        o = slice(64 * e, 64 * e + 64)
        P.stt(ST[o, p, :], ST[o, p, :], PC[o, p:p + 1], pb[o, i * 64:(i + 1) * 64], ALU.mult, ALU.add)
    yield
    if bt_i != 3:
        return
    rowb2 = cx.rowb2
    v32, g32 = S.tmf[:, 0, :], S.tmf[:, 1, :]
    bon = S.sm[:, 0:16]
    ysum, yv, sq = S.ysum, S.yv, S.sq
    P.reduce(ysum[:, :], v3(y32[:, :], 16), ALU.add)
    P.ts("dve", ysum[:, :], ysum[:, :], -1.0 / 64, ALU.mult)
    P.tt("dve", v3(sq[:, :], 16), v3(v32, 16), bcast_last(bon, 64), ALU.mult)
    yield
    P.tt("pool", v3(y32[:, :], 16), v3(y32[:, :], 16), bcast_last(ysum[:, :], 64), ALU.add)
    yield
    sq2 = S.sq2
    P.act(sq2[:, :], y32[:, :], AF.Square)
    yield
    P.reduce(yv[:, :], v3(sq2[:, :], 16), ALU.add)
    yield
    P.act(yv[:, :], yv[:, :], AF.Sqrt, bias=cx.eps_gn[:, 0:1], scale=1.0 / 64)
    yield
    P.recip(yv[:, :], yv[:, :])
    P.tt("dve", v3(y32[:, :], 16), v3(y32[:, :], 16), bcast_last(yv[:, :], 64), ALU.mult)
    yield
    P.tt("pool", y32[:, :], y32[:, :], rowb2[:, 0, :], ALU.mult)
    P.tt("pool", y32[:, :], y32[:, :], rowb2[:, 1, :], ALU.add)
    P.tt("pool", y32[:, :], y32[:, :], sq[:, :], ALU.add)
    yield
    P.tt("dve", y32[:, :], y32[:, :], g32, ALU.mult)
    yield
    ofm = S.ofm
    for half in range(2):
        pb = nb()
        for c4 in range(4):
            c = half * 4 + c4
            P.tr(pb[:, c4 * 128:(c4 + 1) * 128], y32[:, c * 128:(c + 1) * 128], ident[:, :])
        P.copy("act" if half == 0 else "dve", ofm[:, half * 4:(half + 1) * 4, :], v3(pb[:, :], 4))
    osv = cx.o_scr.ap.rearrange("(c p) t -> p c t", p=128)
    P.dma("pool", cx.o_scr.v(osv[:, :, t0:t0 + 128]), ofm[:, :, :])
    yield
    if n + 3 < nblk:
        rwkv_back_load(P, cx, n + 3)
    yield
```
